# Optimizing a Trainium2 kernel written in Bass

```python
import math
import jax, jax.numpy as jnp
from jax import lax
import numpy as np

D_MODEL = 1024
BATCH = 2
SEQ = 8192
DEPTH = 1

MEM_LEN = 256
HG_HEADS = 8
HG_DK = 128
HG_DV = D_MODEL // HG_HEADS
HG_WIDTH = HG_HEADS * HG_DK
HG_VWIDTH = HG_HEADS * HG_DV
HG_CHUNK = 64
SWA_HEADS = 16
SWA_KV_HEADS = 2
SWA_HEAD_DIM = 64
SWA_GROUP = SWA_HEADS // SWA_KV_HEADS
SWA_WIDTH = SWA_HEADS * SWA_HEAD_DIM
SWA_KV_WIDTH = SWA_KV_HEADS * SWA_HEAD_DIM
SWA_WINDOW = 128
SWA_BLOCK = 128
MEM_HEADS = 4
MEM_HEAD_DIM = 256
MEM_WIDTH = MEM_HEADS * MEM_HEAD_DIM
N_BRANCHES = 3
NUM_BUCKETS = 32
MAX_DISTANCE = 128
D_FF = 4 * D_MODEL
LN_EPS = 1e-5
RMS_EPS = 1e-6
IN_SPLITS = (HG_WIDTH, HG_WIDTH, HG_VWIDTH, HG_VWIDTH, SWA_WIDTH, SWA_KV_WIDTH, SWA_KV_WIDTH, MEM_WIDTH, N_BRANCHES * D_MODEL)
IN_COLS = sum(IN_SPLITS)

kernel_name = "hybrid_hgrn2_swa_sink_memory_deepnorm"


def split_cols(z, sizes):
    out = []
    start = 0
    for s in sizes:
        out.append(z[..., start:start + s])
        start += s
    return out


def layer_norm(x, g, b):
    x = x.astype(jnp.float32)
    mu = jnp.mean(x, axis=-1, keepdims=True)
    xc = x - mu
    var = jnp.mean(xc * xc, axis=-1, keepdims=True)
    return xc * lax.rsqrt(var + LN_EPS) * g.astype(jnp.float32) + b.astype(jnp.float32)


def t5_bucket(n):
    max_exact = NUM_BUCKETS // 2
    nf = jnp.maximum(n, 1).astype(jnp.float32)
    large = max_exact + (jnp.log(nf / max_exact) / math.log(MAX_DISTANCE / max_exact)
                         * (NUM_BUCKETS - max_exact)).astype(jnp.int32)
    large = jnp.minimum(large, NUM_BUCKETS - 1)
    return jnp.where(n < max_exact, n, large)


def hgrn2(q, f_logit, v, lb):
    B, S = q.shape[0], q.shape[1]
    n_chunks = S // HG_CHUNK
    f = lb + (1.0 - lb) * jax.nn.sigmoid(f_logit.astype(jnp.float32))
    log_f = jnp.log(f)
    k = 1.0 - f

    def to_chunks(t):
        return t.astype(jnp.float32).reshape(B, n_chunks, HG_CHUNK, HG_HEADS, -1).transpose(1, 0, 3, 2, 4)

    qc, kc, vc, gc = to_chunks(q), to_chunks(k), to_chunks(v), to_chunks(log_f)
    b = jnp.cumsum(gc, axis=3)
    b_last = b[:, :, :, -1:, :]
    q_in = qc * jnp.exp(b)
    k_in = kc * jnp.exp(-b)
    k_out = kc * jnp.exp(b_last - b)
    causal = jnp.tril(jnp.ones((HG_CHUNK, HG_CHUNK), dtype=bool))
    attn = jnp.einsum('nbhcd,nbhsd->nbhcs', q_in, k_in)
    attn = jnp.where(causal, attn, 0.0)
    o_intra = jnp.einsum('nbhcs,nbhsv->nbhcv', attn, vc)

    def step(state, inp):
        k_o, v_n, decay = inp
        new = decay[..., :, None] * state + jnp.einsum('bhsd,bhsv->bhdv', k_o, v_n)
        return new, state

    init = jnp.zeros((B, HG_HEADS, HG_DK, HG_DV), jnp.float32)
    _, states = lax.scan(step, init, (k_out, vc, jnp.exp(b_last[:, :, :, 0, :])))
    o_inter = jnp.einsum('nbhcd,nbhdv->nbhcv', q_in, states)
    o = o_intra + o_inter
    return o.transpose(1, 0, 3, 2, 4).reshape(B, S, HG_HEADS, HG_DV)


def sliding_window_attention(q, k, v, rel_bias, sinks):
    B, S = q.shape[0], q.shape[1]
    nb = S // SWA_BLOCK
    scale = SWA_HEAD_DIM ** -0.5
    qb = q.astype(jnp.float32).reshape(B, nb, SWA_BLOCK, SWA_KV_HEADS, SWA_GROUP, SWA_HEAD_DIM) * scale
    kb = k.astype(jnp.float32).reshape(B, nb, SWA_BLOCK, SWA_KV_HEADS, SWA_HEAD_DIM)
    vb = v.astype(jnp.float32).reshape(B, nb, SWA_BLOCK, SWA_KV_HEADS, SWA_HEAD_DIM)
    pad = jnp.zeros_like(kb[:, :1])
    kk = jnp.concatenate([jnp.concatenate([pad, kb[:, :-1]], axis=1), kb], axis=2)
    vv = jnp.concatenate([jnp.concatenate([pad, vb[:, :-1]], axis=1), vb], axis=2)
    s = jnp.einsum('bnqkgd,bnskd->bkgnqs', qb, kk)
    qi = jnp.arange(SWA_BLOCK)[:, None] + SWA_BLOCK
    kj = jnp.arange(2 * SWA_BLOCK)[None, :]
    dist = qi - kj
    band = (dist >= 0) & (dist < SWA_WINDOW)
    valid = band[None] & ((jnp.arange(nb)[:, None, None] > 0) | (kj[None] >= SWA_BLOCK))
    bucket = t5_bucket(jnp.clip(dist, 0, SWA_WINDOW - 1))
    bias = rel_bias.astype(jnp.float32)[bucket].transpose(2, 0, 1)
    bias = bias.reshape(SWA_KV_HEADS, SWA_GROUP, 1, SWA_BLOCK, 2 * SWA_BLOCK)
    s = jnp.where(valid, s + bias, -jnp.inf)
    sink = sinks.astype(jnp.float32).reshape(SWA_KV_HEADS, SWA_GROUP, 1, 1, 1)
    m = jnp.maximum(jnp.max(s, axis=-1, keepdims=True), sink)
    p = jnp.exp(s - m)
    p = p / (jnp.sum(p, axis=-1, keepdims=True) + jnp.exp(sink - m))
    o = jnp.einsum('bkgnqs,bnskd->bnqkgd', p, vv)
    return o.reshape(B, S, SWA_WIDTH)


def memory_attention(q, mk, mv):
    B, S = q.shape[0], q.shape[1]
    M = mk.shape[1]
    qh = q.astype(jnp.float32).reshape(B, S, MEM_HEADS, MEM_HEAD_DIM) * (MEM_HEAD_DIM ** -0.5)
    kh = mk.astype(jnp.float32).reshape(B, M, MEM_HEADS, MEM_HEAD_DIM)
    vh = mv.astype(jnp.float32).reshape(B, M, MEM_HEADS, MEM_HEAD_DIM)
    p = jax.nn.softmax(jnp.einsum('bshd,bmhd->bhsm', qh, kh), axis=-1)
    return jnp.einsum('bhsm,bmhd->bshd', p, vh).reshape(B, S, MEM_WIDTH)


def setup_inputs(seed: int = 0) -> dict:
    key = jax.random.key(seed)
    ks = jax.random.split(key, 18)
    beta = (8.0 * DEPTH) ** -0.25

    def nrm(k, shape, scale):
        return jax.random.normal(k, shape, jnp.float32) * scale

    return {
        "x": nrm(ks[0], (BATCH, SEQ, D_MODEL), 1.0),
        "mem": nrm(ks[1], (BATCH, MEM_LEN, D_MODEL), 1.0),
        "w_in": nrm(ks[2], (DEPTH, D_MODEL, IN_COLS), D_MODEL ** -0.5),
        "lb_logits": nrm(ks[3], (DEPTH + 1, HG_WIDTH), 0.1),
        "hg_norm_gain": 1.0 + nrm(ks[4], (DEPTH, HG_VWIDTH), 0.02),
        "swa_sinks": nrm(ks[5], (DEPTH, SWA_HEADS), 0.5),
        "rel_bias": nrm(ks[6], (NUM_BUCKETS, SWA_HEADS), 0.5),
        "w_mem_kv": nrm(ks[7], (DEPTH, D_MODEL, 2 * MEM_WIDTH), D_MODEL ** -0.5),
        "w_branch_hg": nrm(ks[8], (DEPTH, HG_VWIDTH, D_MODEL), HG_VWIDTH ** -0.5),
        "w_branch_swa": nrm(ks[9], (DEPTH, SWA_WIDTH, D_MODEL), SWA_WIDTH ** -0.5),
        "w_branch_mem": nrm(ks[10], (DEPTH, MEM_WIDTH, D_MODEL), MEM_WIDTH ** -0.5),
        "w_out": nrm(ks[11], (DEPTH, D_MODEL, D_MODEL), (D_MODEL ** -0.5) * beta),
        "ln1_g": 1.0 + nrm(ks[12], (DEPTH, D_MODEL), 0.02),
        "ln1_b": nrm(ks[13], (DEPTH, D_MODEL), 0.02),
        "w_up": nrm(ks[14], (DEPTH, D_MODEL, D_FF), D_MODEL ** -0.5),
        "w_down": nrm(ks[15], (DEPTH, D_FF, D_MODEL), (D_FF ** -0.5) * beta),
        "ln2_g": 1.0 + nrm(ks[16], (DEPTH, D_MODEL), 0.02),
        "ln2_b": nrm(ks[17], (DEPTH, D_MODEL), 0.02),
    }


def reference(x, mem, w_in, lb_logits, hg_norm_gain, swa_sinks, rel_bias, w_mem_kv,
              w_branch_hg, w_branch_swa, w_branch_mem, w_out, ln1_g, ln1_b,
              w_up, w_down, ln2_g, ln2_b):
    B, S = x.shape[0], x.shape[1]
    out_dtype = x.dtype
    alpha = (2.0 * DEPTH) ** 0.25
    lb_all = jnp.cumsum(jax.nn.softmax(lb_logits.astype(jnp.float32), axis=0), axis=0)
    memf = mem.astype(jnp.float32)
    h = x.astype(jnp.float32)
    for layer in range(DEPTH):
        z = h @ w_in[layer]
        hq, hf, hi, hg, sq, sk, sv, mq, gl = split_cols(z, IN_SPLITS)
        o_a = hgrn2(hq.reshape(B, S, HG_HEADS, HG_DK), hf.reshape(B, S, HG_HEADS, HG_DK),
                    hi.reshape(B, S, HG_HEADS, HG_DV), lb_all[layer].reshape(HG_HEADS, HG_DK))
        o_a = o_a * lax.rsqrt(jnp.mean(o_a * o_a, axis=-1, keepdims=True) + RMS_EPS)
        o_a = o_a.reshape(B, S, HG_VWIDTH) * hg_norm_gain[layer].astype(jnp.float32) * jax.nn.silu(hg.astype(jnp.float32))
        o_b = sliding_window_attention(sq, sk, sv, rel_bias, swa_sinks[layer])
        mk, mv = split_cols(memf @ w_mem_kv[layer], (MEM_WIDTH, MEM_WIDTH))
        o_c = memory_attention(mq, mk, mv)
        gates = jax.nn.sigmoid(gl.astype(jnp.float32).reshape(B, S, N_BRANCHES, D_MODEL))
        merged = (gates[:, :, 0] * (o_a @ w_branch_hg[layer])
                  + gates[:, :, 1] * (o_b @ w_branch_swa[layer])
                  + gates[:, :, 2] * (o_c @ w_branch_mem[layer]))
        mix = merged @ w_out[layer]
        h = layer_norm(alpha * h + mix, ln1_g[layer], ln1_b[layer])
        ff = jnp.square(jax.nn.relu(h @ w_up[layer])) @ w_down[layer]
        h = layer_norm(alpha * h + ff, ln2_g[layer], ln2_b[layer])
    return h.astype(out_dtype)
```

```python
import contextlib
import numpy as np
import concourse.bass as bass
import concourse.mybir as mybir
from concourse.bass_utils import run_bass_kernel_spmd

F32 = mybir.dt.float32
BF16 = mybir.dt.bfloat16
AF = mybir.ActivationFunctionType
ALU = mybir.AluOpType

NCORES = 8
D = 1024
SEQ = 8192
TOK_CORE = 2048
NPASS = 2
T = 1024
HALO = 128
TT = 512
ALPHA = 2.0 ** 0.25
LN_EPS = 1e-5
RMS_EPS = 1e-6
NEG = -30000.0
NSLOT = 3
NPRE = 2


class Buf:
    __slots__ = ("name", "w", "rs")

    def __init__(self, name):
        self.name = name
        self.w = None
        self.rs = {}


class Op:
    __slots__ = ("eng", "fn", "deps", "idx", "inc", "val", "grp", "amt")


class Prog:
    def __init__(self, nc, es):
        self.nc = nc
        self.es = es
        self.ops = []
        self.last = {}
        self.pending = {}
        self.engs = {"pe": nc.tensor, "act": nc.scalar, "dve": nc.vector, "pool": nc.gpsimd, "sp": nc.sync}
        self.sems = {}
        self.cap = None
        self.spread = None
        self.in_spread = False
        self.spread_cnt = 0
        self.spread_every = 4

    def sem(self, key):
        if key not in self.sems:
            self.sems[key] = self.es.enter_context(self.nc.semaphore("s_" + key))
        return self.sems[key]

    def capture(self, body):
        self.cap = []
        body()
        ops, self.cap = self.cap, None
        return ops

    def interleave(self, lists):
        its = [list(l) for l in lists]
        n = max(len(l) for l in its)
        for i in range(n):
            for l in its:
                if i < len(l):
                    self.op(*l[i])

    def flush_spread(self):
        l, self.spread = self.spread, None
        for a in l or ():
            self.op(*a)

    def op(self, eng, fn, r=(), w=(), dma=None, amt=16):
        if self.cap is not None:
            self.cap.append((eng, fn, tuple(r), tuple(w), dma, amt))
            return None
        if self.spread and not self.in_spread:
            self.in_spread = True
            self.spread_cnt += 1
            if self.spread_cnt % self.spread_every == 0:
                self.op(*self.spread.pop(0))
            self.in_spread = False
        o = Op()
        o.amt = amt
        o.eng = eng
        o.fn = fn
        o.idx = len(self.ops)
        o.inc = False
        o.val = 0
        o.grp = dma
        key = ("dma", dma) if dma else eng
        deps = {}

        def add(d, raw):
            if d is None or d is o:
                return
            k = ("dma", d.grp) if d.grp else d.eng
            cur = deps.setdefault(k, [None, None])
            if cur[0] is None or cur[0].idx < d.idx:
                cur[0] = d
            if raw and (cur[1] is None or cur[1].idx < d.idx):
                cur[1] = d

        for b in r:
            add(b.w, True)
        for b in w:
            add(b.w, False)
            for d in b.rs.values():
                add(d, False)
        for d in self.pending.pop(eng, ()):
            add(d, True)
        o.deps = deps
        for b in w:
            b.w = o
            b.rs = {}
        for b in r:
            if b.w is not o:
                b.rs[key] = o
        self.ops.append(o)
        if not dma:
            self.last[eng] = o
        return o

    def barrier(self):
        lasts = [v for k, v in self.last.items()]
        for e in self.engs:
            self.pending[e] = list(lasts)

    def _semdeps(self, o):
        out = []
        for k, (dany, draw) in o.deps.items():
            if k[0] == "dma" if isinstance(k, tuple) else False:
                out.append(dany)
            elif o.grp is None and k == o.eng:
                if o.eng != "pe" and draw is not None:
                    out.append(draw)
            else:
                out.append(dany)
        return out

    def emit(self):
        for o in self.ops:
            for d in self._semdeps(o):
                if not d.grp:
                    d.inc = True
        cnt = {}
        for o in self.ops:
            if o.grp:
                cnt[("dma", o.grp)] = cnt.get(("dma", o.grp), 0) + o.amt
                o.val = cnt[("dma", o.grp)]
            elif o.inc:
                cnt[o.eng] = cnt.get(o.eng, 0) + 1
                o.val = cnt[o.eng]
        waited = {}
        nwait = 0
        for o in self.ops:
            q = self.engs[o.eng]
            for d in self._semdeps(o):
                sk = ("d_" + d.grp) if d.grp else ("e_" + d.eng)
                if waited.get((o.eng, sk), 0) >= d.val:
                    continue
                q.wait_ge(self.sem(sk), d.val)
                nwait += 1
                waited[(o.eng, sk)] = d.val
            inst = o.fn()
            if o.grp:
                inst.then_inc(self.sem("d_" + o.grp), o.amt)
            elif o.inc:
                inst.then_inc(self.sem("e_" + o.eng), 1)
        return nwait


def _t5_bucket_np(n):
    n = np.asarray(n, np.int32)
    nf = np.maximum(n, 1).astype(np.float32)
    large = 16 + (np.log(nf / np.float32(16)) / np.float32(np.log(128 / 16)) * np.float32(16)).astype(np.int32)
    large = np.minimum(large, 31)
    return np.where(n < 16, n, large)


def _tile_k(w, cols):
    K = w.shape[0]
    sub = w[:, cols]
    return sub.reshape(K // 128, 128, -1).transpose(1, 0, 2).reshape(128, -1)


def _build_units(w_in, w_mem_kv, w_bhg, w_bswa, w_bmem, w_out, w_up, w_down):
    parts = []
    off = [0]

    def add(a):
        a = np.ascontiguousarray(a, dtype=np.float32)
        u = (off[0], a.shape[1])
        parts.append(a)
        off[0] += a.shape[1]
        return u

    ar = np.arange
    once = []
    for u in range(4):
        once.append(add(_tile_k(w_mem_kv, ar(u * 512, (u + 1) * 512))))
    for hu in range(2):
        once.append(add(_tile_k(w_in, 1024 + ar(hu * 512, (hu + 1) * 512))))
    pu = []
    for i in range(2):
        pu.append(add(_tile_k(w_in, 2048 + ar(i * 512, (i + 1) * 512))))
    for h in range(8):
        cols = np.concatenate([h * 128 + ar(128), 1024 + h * 128 + ar(128), 3072 + h * 128 + ar(128)])
        pu.append(add(_tile_k(w_in, cols)))
    sk0 = 5120 + ar(64)
    sk1 = 5120 + 64 + ar(64)
    sv0 = 5248 + ar(64)
    sv1 = 5248 + 64 + ar(64)
    cols = np.concatenate([sk0, sk0, sk1, sk1, sv0, sv0, sv1, sv1])
    pu.append(add(_tile_k(w_in, cols)))
    for g in range(2):
        pu.append(add(_tile_k(w_in, 4096 + ar(g * 512, (g + 1) * 512))))
    for mh in range(4):
        pu.append(add(_tile_k(w_in, 5376 + ar(mh * 256, (mh + 1) * 256))))
    for cc in range(8):
        cols = np.concatenate([6400 + b * 1024 + cc * 128 + ar(128) for b in range(3)])
        pu.append(add(_tile_k(w_in, cols)))
        c2 = cc * 128 + ar(128)
        pu.append(add(np.concatenate([_tile_k(w_bhg, c2).reshape(128, 8, 128),
                                      _tile_k(w_bswa, c2).reshape(128, 8, 128),
                                      _tile_k(w_bmem, c2).reshape(128, 8, 128)], axis=2).reshape(128, -1)))
    for i in range(2):
        pu.append(add(_tile_k(w_out, ar(i * 512, (i + 1) * 512))))
    for u in range(8):
        pu.append(add(_tile_k(w_up, ar(u * 512, (u + 1) * 512))))
    for cc in range(8):
        pu.append(add(_tile_k(w_down, cc * 128 + ar(128))))
    wall = np.concatenate(parts, axis=1)
    return wall, once, pu


def _unit_sizes():
    once = [4096] * 6
    pu = [4096] * 2 + [3072] * 8 + [4096] + [4096] * 2 + [2048] * 4 + [3072, 3072] * 8 + [4096] * 2 + [4096] * 8 + [4096] * 8
    return once, pu


def _bias_table(rel_bias):
    k = np.arange(128)[:, None]
    q = np.arange(128)[None, :]
    dist_prev = q + 128 - k
    dist_cur = q - k
    tab = np.empty((128, 2, 2, 2, 4, 128), np.float32)
    for g in range(2):
        for r in range(2):
            for i in range(4):
                h = 8 * g + 2 * i + r
                for pc, dist in ((0, dist_prev), (1, dist_cur)):
                    valid = (dist >= 0) & (dist < 128)
                    bk = _t5_bucket_np(np.clip(dist, 0, 127))
                    tab[:, g, pc, r, i, :] = np.where(valid, rel_bias[bk, h], np.float32(NEG))
    return tab.reshape(128, 4096)


def _sink_table(sinks):
    tab = np.empty((128, 2, 2, 4, 128), np.float32)
    for g in range(2):
        for r in range(2):
            for i in range(4):
                tab[:, g, r, i, :] = sinks[8 * g + 2 * i + r]
    return tab.reshape(128, 2048)


def _masks():
    m = np.ones((128, 1024), np.float32)
    m[:, 0:512:64] = 0.0
    s = np.arange(128)[:, None]
    c = np.arange(128)[None, :]
    am = ((s // 64 == c // 64) & (c >= s)).astype(np.float32)
    m[:, 512:1024] = np.tile(am, (1, 4))
    sel = np.zeros((128, 128), np.float32)
    sel[0, 0:64] = 1.0
    sel[32, 0:64] = 1.0
    sel[64, 64:128] = 1.0
    sel[96, 64:128] = 1.0
    o0 = np.zeros((128, 128), np.float32)
    o0[:, 0:64] = 1.0
    o1 = np.zeros((128, 128), np.float32)
    o1[:, 64:128] = 1.0
    return np.concatenate([m, sel, o0, o1], axis=1)


def build_program(dbg=False):
    nc = bass.Bass("TRN2", target_bir_lowering=False)
    once_sz, pass_sz = _unit_sizes()
    tot = sum(once_sz) + sum(pass_sz)

    def din(name, shape, dt=F32):
        return nc.dram_tensor(name, shape, dt, kind="ExternalInput").ap()

    xT_d = din("xT", [NPASS, D, HALO + T])
    sel_d = din("sel", [128, 4])
    cc_in = nc.dram_tensor("cc_in", [128, 1032], F32)
    cc_out = nc.dram_tensor("cc_out", [512, 1032], F32)
    vt_d = nc.dram_tensor("vt_scratch", [NPASS, 128, 8 * 1024], BF16)
    memT_d = din("memT", [D, 256])
    wall_d = din("wall", [128, tot])
    lbT_d = din("lbT", [128, 16])
    vec_d = din("vecs", [128, 40])
    hm_d = din("hmask", [128, 1])
    bias_d = din("biasT", [128, 4096])
    sink_d = din("sinkT", [128, 2048])
    msk_d = din("masks", [128, 1408])
    id_d = din("ident", [128, 128])
    outT_d = nc.dram_tensor("outT", [D, TOK_CORE], F32, kind="ExternalOutput").ap()
    dbg_d = {}

    es = contextlib.ExitStack()
    with es:
        P = Prog(nc, es)

        uniq = {"n": 0}

        def sb(name, shape, dt, stack=es):
            uniq["n"] += 1
            return stack.enter_context(nc.sbuf_tensor("%s_%d" % (name, uniq["n"]), shape, dt))

        pb = [es.enter_context(nc.psum_tensor("pb%d" % i, [128, 512], F32)) for i in range(8)]
        Bpb = [Buf("pb%d" % i) for i in range(8)]
        slots = [sb("wslot%d" % i, [128, 4096], BF16) for i in range(NSLOT)]
        Bslot = [Buf("slot%d" % i) for i in range(NSLOT)]
        c_lb = sb("c_lb", [128, 16], F32)
        c_lbv = sb("c_lbv", [128, 8], F32)
        c_oml = sb("c_oml", [128, 8], F32)
        c_lnoml = sb("c_lnoml", [128, 8], F32)
        c_vec = sb("c_vec", [128, 40], F32)
        c_hm = sb("c_hm", [128, 1], F32)
        c_msk = sb("c_msk", [128, 1408], F32)
        c_cb = sb("c_cb", [128, 384], BF16)
        c_id = sb("c_id", [128, 128], BF16)
        c_on128 = sb("c_on128", [128, 128], BF16)
        c_on1024 = sb("c_on1024", [128, 128], BF16)
        c_one = sb("c_one", [128, 128], BF16)
        mkT = sb("mkT", [128, 8, 256], BF16)
        mvb = sb("mvb", [128, 2, 1024], BF16)
        s_carry = sb("s_carry", [128, 8, 128], F32)
        ostg = [sb("ostg%d" % i, [128, 512], F32) for i in range(2)]
        Bostg = [Buf("ostg%d" % i) for i in range(2)]
        Bconst = Buf("const")
        Bmk = Buf("mkT")
        Bmv = Buf("mvb")
        Bcarry = [Buf("carry%d" % h) for h in range(8)]
        Bsetup = Buf("setup")

        E = P.engs
        pe, act, dve, pool, sp = nc.tensor, nc.scalar, nc.vector, nc.gpsimd, nc.sync

        def mm(out, lhsT, rhs, start, stop, r, w):
            P.op("pe", lambda: pe.matmul(out, lhsT=lhsT, rhs=rhs, start=start, stop=stop), r=r, w=w)

        def A(out, in_, func, r, w, **kw):
            P.op("act", lambda: act.activation(out=out, in_=in_, func=func, **kw), r=r, w=w)

        def TTn(out, in0, in1, op, r, w):
            P.op("dve", lambda: dve.tensor_tensor(out=out, in0=in0, in1=in1, op=op), r=r, w=w)

        def TS(out, in0, s1, s2, op0, op1, r, w):
            if op1 is None:
                P.op("dve", lambda: dve.tensor_scalar(out=out, in0=in0, scalar1=s1, scalar2=None, op0=op0), r=r, w=w)
            else:
                P.op("dve", lambda: dve.tensor_scalar(out=out, in0=in0, scalar1=s1, scalar2=s2, op0=op0, op1=op1),
                     r=r, w=w)

        def STT(out, in0, scalar, in1, op0, op1, r, w):
            P.op("dve", lambda: dve.scalar_tensor_tensor(out=out, in0=in0, scalar=scalar, in1=in1, op0=op0, op1=op1),
                 r=r, w=w)

        def dbg_dump(name, ap, shape, dt, rbufs):
            if not dbg:
                return
            d = nc.dram_tensor("dbg_" + name, list(shape), dt, kind="ExternalOutput").ap()
            dbg_d[name] = d
            P.op("sp", lambda: sp.dma_start(out=d, in_=ap), r=rbufs, w=[Buf("dbgw")], dma="out")

        units = []
        o = 0
        for n in once_sz:
            units.append((o, n))
            o += n
        pass_units = []
        for n in pass_sz:
            pass_units.append((o, n))
            o += n
        f_units = units[4:6]
        kv_units = units[0:4]
        units = []
        for t in range(NPRE):
            units.extend([pass_units[0], pass_units[1], f_units[0], f_units[1]])
        pass_units = pass_units + pass_units[-8:]
        pu_a, pu_bc, pu_rest = pass_units[2:10], pass_units[10:17], pass_units[17:]
        units.extend(pu_bc[0:3] + kv_units + pu_bc[3:] + pu_a + pu_rest)
        for p in range(1, NPASS):
            units.extend(pu_a + pu_bc + pu_rest)
        wstate = {"next": 0, "cur": 0}

        def w_ensure(upto):
            while wstate["next"] <= min(upto, len(units) - 1):
                u = wstate["next"]
                s = u % NSLOT
                off, n = units[u]
                dst = slots[s][:, 0:n].rearrange("p (a b) -> p a b", b=1024)
                src = wall_d[:, off:off + n].rearrange("p (a b) -> p a b", b=1024)
                P.op("pool", lambda dst=dst, src=src: pool.dma_start(out=dst, in_=src), w=[Bslot[s]], dma="w%d" % s)
                wstate["next"] += 1

        def w_get(kc=8, ahead=NSLOT - 1):
            u = wstate["cur"]
            wstate["cur"] += 1
            w_ensure(u + ahead)
            s = u % NSLOT
            n = units[u][1]
            ap = slots[s][:, 0:n].rearrange("p (k c) -> p k c", k=kc)
            return ap, Bslot[s]

        def cload(dst, src):
            P.op("sp", lambda: sp.dma_start(out=dst, in_=src), w=[Bconst], dma="const")

        cload(c_lb[:], lbT_d[:, :])
        cload(c_vec[:], vec_d[:, :])
        cload(c_hm[:], hm_d[:, :])
        cload(c_msk[:], msk_d[:, :])
        Bcid = Buf("c_id")
        P.op("pool", lambda: pool.dma_start(out=c_id[:], in_=id_d[:, :]), w=[Bcid], dma="cid")
        P.op("dve", lambda: dve.tensor_copy(out=c_cb[:], in_=c_msk[:, 1024:1408]), r=[Bconst], w=[Bsetup])
        P.op("dve", lambda: dve.memset(c_on128[:], 1.0 / 128.0), w=[Bsetup])
        P.op("dve", lambda: dve.memset(c_on1024[:], 1.0 / 1024.0), w=[Bsetup])
        P.op("dve", lambda: dve.memset(c_one[:], 1.0), w=[Bsetup])
        P.op("dve", lambda: dve.memset(s_carry[:], 0.0), w=Bcarry)
        TTn(c_lbv[:], c_lb[:, 0:8], c_lb[:, 8:16], ALU.subtract, r=[Bconst], w=[Bsetup])
        A(c_lbv[:], c_lbv[:], AF.Sigmoid, r=[Bsetup], w=[Bsetup])
        A(c_oml[:], c_lbv[:], AF.Identity, r=[Bsetup], w=[Bsetup], scale=-1.0, bias=1.0)
        A(c_lnoml[:], c_oml[:], AF.Ln, r=[Bsetup], w=[Bsetup])
        gain = c_vec[:, 0:8]
        msk_scan = c_msk[:, 0:512]
        msk_attn = c_msk[:, 512:1024]

        evs = {"i": 0}

        def evac(out, in_, r, w):
            evs["i"] += 1
            if evs["i"] % 2 == 0:
                A(out, in_, AF.Copy, r=r, w=w)
            else:
                P.op("dve", lambda: dve.tensor_copy(out=out, in_=in_), r=r, w=w)

        p_cur = {"p": 0}

        def hgrn_phase(sa, xT, BxT, xoff, full, oaT, Boa, gsum=None, Bgsum=None):
            assert full
            vtok = sb("vtok", [128, 8, 1024], BF16, sa)
            Bvall = Buf("vtok_all")
            Bv = [Bvall] * 8
            pp = p_cur["p"]
            P.op("sp", lambda: sp.dma_start(out=vtok[:].rearrange("p b c -> p (b c)"), in_=vt_d[pp]),
                 r=[Bvt[pp]], w=[Bvall], dma="vtld")
            streams = []
            for si in range(2):
                st = {}
                fn = ("sf", "geb", "bt1", "enbln", "k", "kinf", "gs0", "gs1", "lnt1")
                for n in fn:
                    st[n] = sb("a%d_%s" % (si, n), [128, 512], F32, sa)
                bn = ("qin0", "qin1", "kin", "kout", "koutT", "attn", "sq")
                for n in bn:
                    st[n] = sb("a%d_%s" % (si, n), [128, 512], BF16, sa)
                st["dec"] = sb("a%d_dec" % si, [128, 8], F32, sa)
                st["sall"] = sb("a%d_sall" % si, [128, 8, 128], F32, sa)
                st["sbf"] = sb("a%d_sbf" % si, [128, 8, 128], BF16, sa)
                st["B"] = {n: Buf("a%d_%s" % (si, n)) for n in fn + bn + ("dec", "sbf")}
                st["Bsall"] = [Buf("a%d_sall%d" % (si, c)) for c in range(8)]
                st["bk"] = [4 * si + i for i in range(4)]
                streams.append(st)

            def proj(st, W, BW, col, bank, tt):
                tok0 = xoff + tt * TT
                for kc in range(8):
                    mm(pb[bank][:, :], W[:, kc, col:col + 128], xT[:, kc, tok0:tok0 + TT],
                       kc == 0, kc == 7, r=[BW, BxT], w=[Bpb[bank]])

            def head(st, h, tt):
                B = st["B"]
                X0, X1, X2, X3 = st["bk"]
                t_sf, t_g, t_b, t_enb, t_k, t_kinf = st["sf"], st["geb"], st["bt1"], st["enbln"], st["k"], st["kinf"]
                t_eb = t_g
                A(t_sf[:], pb[X1][:, :], AF.Exp, r=[Bpb[X1]], w=[B["sf"]], scale=-1.0)
                A(t_g[:], t_sf[:], AF.Ln, r=[B["sf"], Bsetup], w=[B["geb"]], scale=c_lbv[:, h:h + 1], bias=1.0)
                A(t_k[:], t_sf[:], AF.Ln, r=[B["sf"]], w=[B["k"]], bias=1.0)
                TTn(t_g[:], t_g[:], t_k[:], ALU.subtract, r=[B["geb"], B["k"]], w=[B["geb"]])
                P.op("dve", lambda: dve.tensor_tensor_scan(out=t_b[:], data0=msk_scan, data1=t_g[:],
                                                           initial=0.0, op0=ALU.mult, op1=ALU.add),
                     r=[B["geb"], Bconst], w=[B["bt1"]])
                A(t_eb[:], t_b[:], AF.Exp, r=[B["bt1"]], w=[B["geb"]])
                A(st["dec"][:], t_b[:, 63:512:64], AF.Exp, r=[B["bt1"]], w=[B["dec"]])
                TTn(t_enb[:], pb[X1][:, :], t_k[:], ALU.add, r=[Bpb[X1], B["k"]], w=[B["enbln"]])
                TTn(t_enb[:], t_enb[:], t_b[:], ALU.add, r=[B["enbln"], B["bt1"]], w=[B["enbln"]])
                A(t_kinf[:], t_enb[:], AF.Exp, r=[B["enbln"], Bsetup], w=[B["kinf"]], scale=-1.0,
                  bias=c_lnoml[:, h:h + 1])
                qn = "qin%d" % tt
                TTn(st[qn][:], pb[X0][:, :], t_eb[:], ALU.mult, r=[Bpb[X0], B["geb"]], w=[B[qn]])
                A(st["kin"][:], t_kinf[:], AF.Copy, r=[B["kinf"]], w=[B["kin"]])
                TTn(st["kout"][:].rearrange("p (c t) -> p c t", t=64),
                    t_kinf[:].rearrange("p (c t) -> p c t", t=64),
                    st["dec"][:, :].unsqueeze(2).broadcast_to([128, 8, 64]), ALU.mult,
                    r=[B["kinf"], B["dec"]], w=[B["kout"]])

            def mid(st, h, tt, W, BW, nxt):
                B = st["B"]
                X0, X1, X2, X3 = st["bk"]
                Bsall = st["Bsall"]
                s_all = st["sall"]
                trv = pb[X1][:].bitcast(BF16)
                tb0 = tt * 4
                if tt == 0:
                    for t2 in range(2):
                        proj(st, W, BW, 256, X2, t2)
                        A(st["gs%d" % t2][:], pb[X2][:, :], AF.Silu, r=[Bpb[X2]], w=[B["gs%d" % t2]])
                for j in range(4):
                    P.op("pe", lambda j=j: pe.transpose(out=trv[:, j * 128:(j + 1) * 128],
                                                        in_=st["kout"][:, j * 128:(j + 1) * 128],
                                                        identity=c_id[:]),
                         r=[B["kout"], Bcid], w=[Bpb[X1]])
                A(st["koutT"][:], trv[:, 0:512], AF.Copy, r=[Bpb[X1]], w=[B["koutT"]])
                for j in range(4):
                    sl = slice(j * 128, (j + 1) * 128)
                    mm(pb[X0][:, sl], st["kin"][:, sl], st["qin%d" % tt][:, sl], True, True,
                       r=[B["kin"], B["qin%d" % tt]], w=[Bpb[X0]])
                TTn(st["attn"][:], pb[X0][:, :], msk_attn, ALU.mult, r=[Bpb[X0], Bconst], w=[B["attn"]])
                for c in range(8):
                    j, rr = c // 2, c % 2
                    bk = (X2, X3)[rr]
                    rows = slice(rr * 64, rr * 64 + 64)
                    mm(pb[bk][:, j * 128:(j + 1) * 128], st["koutT"][rows, j * 128:(j + 1) * 128],
                       vtok[rows, tb0 + j, h * 128:(h + 1) * 128], True, True,
                       r=[B["koutT"], Bv[tb0 + j]], w=[Bpb[bk]])
                P.op("pool", lambda: pool.tensor_copy(out=s_all[:, 0, :], in_=s_carry[:, h, :]),
                     r=[Bcarry[h]], w=[Bsall[0]])
                if nxt is not None:
                    W2, BW2, tt2 = nxt
                    proj(st, W2, BW2, 128, X1, tt2)
                    proj(st, W2, BW2, 0, X0, tt2)
                for c in range(8):
                    j, rr = c // 2, c % 2
                    bk = (X2, X3)[rr]
                    if c < 7:
                        out, wb = s_all[:, c + 1, :], Bsall[c + 1]
                    else:
                        out, wb = s_carry[:, h, :], Bcarry[h]
                    STT(out, s_all[:, c, :], st["dec"][:, c:c + 1], pb[bk][:, j * 128:(j + 1) * 128],
                        ALU.mult, ALU.add, r=[Bsall[c], B["dec"], Bpb[bk]], w=[wb])

            def tail(st, h, tt):
                B = st["B"]
                X0, X1, X2, X3 = st["bk"]
                Bsall = st["Bsall"]
                s_all, t_sbf = st["sall"], st["sbf"]
                t_ln = t_t1 = st["lnt1"]
                qn = "qin%d" % tt
                tb0 = tt * 4
                A(t_sbf[:].rearrange("p c d -> p (c d)"), s_all[:].rearrange("p c d -> p (c d)"), AF.Copy,
                  r=Bsall, w=[B["sbf"]])
                for j in range(4):
                    sl = slice(j * 128, (j + 1) * 128)
                    mm(pb[X2][:, sl], vtok[:, tb0 + j, h * 128:(h + 1) * 128], st["attn"][:, sl], True, False,
                       r=[Bv[tb0 + j], B["attn"]], w=[Bpb[X2]])
                    for rr in range(2):
                        c = 2 * j + rr
                        cs = slice(c * 64, c * 64 + 64)
                        mm(pb[X2][:, cs], t_sbf[:, c, :], st[qn][:, cs], False, rr == 1,
                           r=[B["sbf"], B[qn]], w=[Bpb[X2]])
                A(st["sq"][:], pb[X2][:, :], AF.Square, r=[Bpb[X2]], w=[B["sq"]])
                mm(pb[X3][:, :], c_on128[:], st["sq"][:], True, True, r=[B["sq"], Bsetup], w=[Bpb[X3]])
                A(t_ln[:], pb[X3][:, :], AF.Ln, r=[Bpb[X3]], w=[B["lnt1"]], bias=RMS_EPS)
                A(t_ln[:], t_ln[:], AF.Exp, r=[B["lnt1"]], w=[B["lnt1"]], scale=-0.5)
                TTn(t_t1[:], pb[X2][:, :], t_ln[:], ALU.mult, r=[Bpb[X2], B["lnt1"]], w=[B["lnt1"]])
                STT(oaT[:, h, tt * TT:(tt + 1) * TT], t_t1[:], gain[:, h:h + 1], st["gs%d" % tt][:], ALU.mult,
                    ALU.mult, r=[B["lnt1"], B["gs%d" % tt], Bconst], w=[Boa[tt]])

            its = [(hp, tt) for hp in range(4) for tt in range(2)]
            Wp = {}
            Wp[0] = (w_get(), w_get(ahead=1))
            for si in range(2):
                W, BW = Wp[0][si]
                proj(streams[si], W, BW, 128, streams[si]["bk"][1], 0)
                proj(streams[si], W, BW, 0, streams[si]["bk"][0], 0)
            P.interleave([P.capture(lambda si=si: head(streams[si], si, 0)) for si in range(2)])
            for idx, (hp, tt) in enumerate(its):
                hs = (2 * hp, 2 * hp + 1)
                nxt = [None, None]
                if idx + 1 < len(its):
                    hp2, tt2 = its[idx + 1]
                    if hp2 not in Wp:
                        Wp[hp2] = (w_get(ahead=1), w_get(ahead=1))
                    nxt = [(Wp[hp2][si][0], Wp[hp2][si][1], tt2) for si in range(2)]
                P.interleave([P.capture(lambda si=si: mid(streams[si], hs[si], tt, Wp[hp][si][0], Wp[hp][si][1], nxt[si]))
                              for si in range(2)])
                lists = [P.capture(lambda si=si: tail(streams[si], hs[si], tt)) for si in range(2)]
                if idx + 1 < len(its):
                    hp2, tt2 = its[idx + 1]
                    hs2 = (2 * hp2, 2 * hp2 + 1)
                    lists += [P.capture(lambda si=si: head(streams[si], hs2[si], tt2)) for si in range(2)]
                P.interleave(lists)

        Bvt = [Buf('vt%d' % t) for t in range(NPASS)]

        def hgrn_state_phase(sa, xT, BxT, gsum, Bgsum, tile):
            vtok = sb("vtok", [128, 8, 1024], BF16, sa)
            Bv = [Buf("vtok%d" % t) for t in range(8)]
            for ui in range(2):
                W, BW = w_get()
                for tb in range(8):
                    bk = (ui * 8 + tb) % 8
                    c0 = tb * 128
                    for kc in range(8):
                        mm(pb[bk][:, :], xT[:, kc, c0:c0 + 128], W[:, kc, :], kc == 0, kc == 7,
                           r=[BW, BxT], w=[Bpb[bk]])
                    evac(vtok[:, tb, ui * 512:(ui + 1) * 512], pb[bk][:, :], r=[Bpb[bk]], w=[Bv[tb]])
            P.op("sp", lambda: sp.dma_start(out=vt_d[tile], in_=vtok[:].rearrange("p b c -> p (b c)")),
                 r=Bv, w=[Bvt[tile]], dma="vtst")
            c_one_f = sb("c_one_f", [128, 512], F32, sa)
            Bonef = Buf("onef")
            P.op("dve", lambda: dve.memset(c_one_f[:], 1.0), w=[Bonef])
            streams = []
            for si in range(2):
                st = {}
                for n in ("sf", "g", "b", "e", "k"):
                    st[n] = sb("x%d_%s" % (si, n), [128, 512], F32, sa)
                for n in ("kout", "koutT"):
                    st[n] = sb("x%d_%s" % (si, n), [128, 512], BF16, sa)
                st["dec"] = sb("x%d_dec" % si, [128, 2], F32, sa)
                st["B"] = {n: Buf("x%d_%s" % (si, n)) for n in ("sf", "g", "b", "e", "k", "kout", "koutT", "dec")}
                st["bk"] = [4 * si + i for i in range(4)]
                streams.append(st)

            def xproj(st, idx, h, tt, W, BW, col):
                bH = st["bk"][0] if idx % 2 == 0 else st["bk"][3]
                tok0 = tt * TT
                for kc in range(8):
                    mm(pb[bH][:, :], W[:, kc, col:col + 128], xT[:, kc, tok0:tok0 + TT],
                       kc == 0, kc == 7, r=[BW, BxT], w=[Bpb[bH]])

            def iteration(st, idx, h, tt, nxt):
                B = st["B"]
                bH = st["bk"][0] if idx % 2 == 0 else st["bk"][3]
                bT, bU = st["bk"][1], st["bk"][2]
                trv = pb[bT][:].bitcast(BF16)
                tb0 = tt * 4
                A(st["sf"][:], pb[bH][:, :], AF.Exp, r=[Bpb[bH]], w=[B["sf"]], scale=-1.0)
                if nxt is not None:
                    xproj(st, *nxt)
                A(st["g"][:], st["sf"][:], AF.Ln, r=[B["sf"], Bsetup], w=[B["g"]], scale=c_lbv[:, h:h + 1], bias=1.0)
                A(st["k"][:], st["sf"][:], AF.Ln, r=[B["sf"]], w=[B["k"]], bias=1.0)
                TTn(st["g"][:], st["g"][:], st["k"][:], ALU.subtract, r=[B["g"], B["k"]], w=[B["g"]])
                P.op("dve", lambda: dve.tensor_tensor_scan(out=st["b"][:], data0=c_one_f[:], data1=st["g"][:],
                                                           initial=0.0, op0=ALU.mult, op1=ALU.add),
                     r=[B["g"], Bonef], w=[B["b"]])
                TTn(st["e"][:], pb[bH][:, :], st["k"][:], ALU.add, r=[Bpb[bH], B["k"]], w=[B["e"]])
                TTn(st["e"][:], st["e"][:], st["b"][:], ALU.add, r=[B["e"], B["b"]], w=[B["e"]])
                TTn(st["dec"][:, 1:2], st["b"][:, 511:512], c_lnoml[:, h:h + 1], ALU.add, r=[B["b"], Bsetup],
                    w=[B["dec"]])
                A(st["dec"][:, 0:1], st["b"][:, 511:512], AF.Exp, r=[B["b"], B["dec"]], w=[B["dec"]])
                TTn(gsum[:, h:h + 1], gsum[:, h:h + 1], st["b"][:, 511:512], ALU.add, r=[B["b"], Bgsum], w=[Bgsum])
                A(st["kout"][:], st["e"][:], AF.Exp, r=[B["e"], B["dec"]], w=[B["kout"]], scale=-1.0,
                  bias=st["dec"][:, 1:2])
                for j in range(4):
                    P.op("pe", lambda j=j: pe.transpose(out=trv[:, j * 128:(j + 1) * 128],
                                                        in_=st["kout"][:, j * 128:(j + 1) * 128],
                                                        identity=c_id[:]),
                         r=[B["kout"], Bcid], w=[Bpb[bT]])
                P.op("dve", lambda: dve.tensor_copy(out=st["koutT"][:], in_=trv[:, 0:512]), r=[Bpb[bT]], w=[B["koutT"]])
                for j in range(4):
                    mm(pb[bU][:, 0:128], st["koutT"][:, j * 128:(j + 1) * 128],
                       vtok[:, tb0 + j, h * 128:(h + 1) * 128], j == 0, j == 3,
                       r=[B["koutT"], Bv[tb0 + j]], w=[Bpb[bU]])
                STT(s_carry[:, h, :], s_carry[:, h, :], st["dec"][:, 0:1], pb[bU][:, 0:128], ALU.mult, ALU.add,
                    r=[Bcarry[h], B["dec"], Bpb[bU]], w=[Bcarry[h]])

            its = [(hp, tt) for hp in range(4) for tt in range(2)]
            Wf = {}
            Wf[0] = w_get()
            for si in range(2):
                xproj(streams[si], 0, si, 0, Wf[0][0], Wf[0][1], (si % 4) * 128)
            for idx, (hp, tt) in enumerate(its):
                hs = (2 * hp, 2 * hp + 1)
                nxt = [None, None]
                if idx + 1 < len(its):
                    hp2, tt2 = its[idx + 1]
                    if hp2 // 2 not in Wf:
                        Wf[hp2 // 2] = w_get(ahead=1)
                    W2, BW2 = Wf[hp2 // 2]
                    nxt = [(idx + 1, 2 * hp2 + si, tt2, W2, BW2, ((2 * hp2 + si) % 4) * 128) for si in range(2)]
                lists = [P.capture(lambda si=si: iteration(streams[si], idx, hs[si], tt, nxt[si])) for si in range(2)]
                P.interleave(lists)
            fence = sb("x_fence", [128, 1], F32, sa)
            P.op("dve", lambda: dve.memset(fence[:], 0.0), r=[Bvt[tile]], w=[Buf("x_fence")])

        gsum = sb("gsum", [128, 8], F32)
        Bgsum = Buf("gsum")
        P.op("dve", lambda: dve.memset(gsum[:], 0.0), w=[Bgsum])
        with contextlib.ExitStack() as sxp:
            xTps = [sb("xTp%d" % t, [128, 8, T], BF16, sxp) for t in range(NPRE)]
            BxTps = [Buf("xTp%d" % t) for t in range(NPRE)]
            for t in range(NPRE):
                P.op("pool", lambda t=t: pool.dma_start(
                    out=xTps[t][:], in_=xT_d[t, :, HALO:HALO + T].rearrange("(k q) n -> q k n", q=128)),
                    w=[BxTps[t]], dma="xTp%d" % t)
            for t in range(NPRE):
                with contextlib.ExitStack() as spre:
                    hgrn_state_phase(spre, xTps[t], BxTps[t], gsum, Bgsum, t)
                    P.barrier()
        with contextlib.ExitStack() as sx:
            pay = sb("pay", [128, 1032], F32, sx)
            Bpay, Bgat, Bpfx, Bsel, Bccin, Bccout = [Buf(n) for n in ("pay", "gat", "pfx", "sel", "ccin", "ccout")]
            P.op("dve", lambda: dve.tensor_copy(out=pay[:, 0:1024], in_=s_carry[:].rearrange("p h d -> p (h d)")),
                 r=Bcarry, w=[Bpay])
            A(pay[:, 1024:1032], gsum[:], AF.Exp, r=[Bgsum, Bpay], w=[Bpay])
            P.op("pool", lambda: pool.dma_start(out=cc_in[:, :], in_=pay[:]), r=[Bpay], w=[Bccin], dma="ccin")
            P.op("pool", lambda: pool.collective_compute("AllGather", ALU.bypass,
                                                         replica_groups=[[0, 1, 2, 3], [4, 5, 6, 7]],
                                                         ins=[cc_in.ap().opt()], outs=[cc_out.ap().opt()]),
                 r=[Bccin], w=[Bccout], dma="cc", amt=1)
            P.barrier()

        def phase_0():
            with contextlib.ExitStack() as s0:
                memT = sb("memT", [128, 8, 256], BF16, s0)
                Bmem = Buf("memT")
                P.op("pool", lambda: pool.dma_start(out=memT[:], in_=memT_d.rearrange("(k p) m -> p k m", p=128)),
                     w=[Bmem], dma="memT")
                for u in range(4):
                    W, BW = w_get()
                    if u < 2:
                        for ci in range(4):
                            ch = u * 4 + ci
                            bk = ch % 8
                            for kc in range(8):
                                mm(pb[bk][:, 0:256], W[:, kc, ci * 128:(ci + 1) * 128], memT[:, kc, :], kc == 0, kc == 7,
                                   r=[BW, Bmem], w=[Bpb[bk]])
                            evac(mkT[:, ch, :], pb[bk][:, 0:256], r=[Bpb[bk]], w=[Bmk])
                    else:
                        for mb in range(2):
                            bk = (u * 2 + mb) % 8
                            for kc in range(8):
                                mm(pb[bk][:, :], memT[:, kc, mb * 128:(mb + 1) * 128], W[:, kc, :], kc == 0, kc == 7,
                                   r=[BW, Bmem], w=[Bpb[bk]])
                            evac(mvb[:, mb, (u - 2) * 512:(u - 1) * 512], pb[bk][:, :], r=[Bpb[bk]], w=[Bmv])
            P.barrier()

        def combine_states():
            with contextlib.ExitStack() as sx2:
                gat = sb("gat", [128, 4, 1032], F32, sx2)
                pfx = sb("pfx", [128, 2, 1024], F32, sx2)
                c_sel = sb("c_sel", [128, 4], F32, sx2)
                P.op("sp", lambda: sp.dma_start(out=c_sel[:], in_=sel_d[:, :]), w=[Bsel], dma="const2")
                P.op("pool", lambda: pool.dma_start(out=gat[:], in_=cc_out.ap().rearrange("(r p) c -> p r c", p=128)),
                     r=[Bccout], w=[Bgat], dma="ccback")
                for h in range(8):
                    hs = slice(h * 128, (h + 1) * 128)
                    STT(pfx[:, 0, hs], gat[:, 0, hs], gat[:, 1, 1024 + h:1025 + h], gat[:, 1, hs], ALU.mult, ALU.add,
                        r=[Bgat], w=[Bpfx])
                for h in range(8):
                    hs = slice(h * 128, (h + 1) * 128)
                    STT(pfx[:, 1, hs], pfx[:, 0, hs], gat[:, 2, 1024 + h:1025 + h], gat[:, 2, hs], ALU.mult, ALU.add,
                        r=[Bgat, Bpfx], w=[Bpfx])
                sc = s_carry[:].rearrange("p h d -> p (h d)")
                TS(sc, gat[:, 0, 0:1024], c_sel[:, 1:2], None, ALU.mult, None, r=[Bgat, Bsel], w=Bcarry)
                STT(sc, pfx[:, 0, :], c_sel[:, 2:3], sc, ALU.mult, ALU.add, r=[Bpfx, Bsel] + Bcarry, w=Bcarry)
                STT(sc, pfx[:, 1, :], c_sel[:, 3:4], sc, ALU.mult, ALU.add, r=[Bpfx, Bsel] + Bcarry, w=Bcarry)
                P.barrier()

        def run_pass(p):
            p_cur["p"] = p
            with contextlib.ExitStack() as sp_pass:
                merged = sb("merged", [128, 8, T], BF16, sp_pass)
                Bmerged = [Buf("merged%d" % t) for t in range(2)]
                with contextlib.ExitStack() as s1:
                    xT = sb("xT", [128, 8, HALO + T], BF16, s1)
                    oaT = sb("oaT", [128, 8, T], BF16, s1)
                    obT = sb("obT", [128, 8, T], BF16, s1)
                    ocT = sb("ocT", [128, 8, T], BF16, s1)
                    BxT = Buf("xT")
                    Boa = [Buf("oa%d" % t) for t in range(2)]
                    Bob = [Buf("ob%d" % t) for t in range(2)]
                    Boc = [Buf("oc%d" % t) for t in range(2)]
                    P.op("pool", lambda xT=xT, p=p: pool.dma_start(
                        out=xT[:], in_=xT_d[p].rearrange("(k q) t -> q k t", q=128)), w=[BxT], dma="xT")

                    def phase_a():
                        with contextlib.ExitStack() as sa:
                            hgrn_phase(sa, xT, BxT, HALO, True, oaT, Boa)
                            P.barrier()
                        if p == 0:
                            dbg_dump("oaT", oaT[:], [128, 8, T], BF16, Boa)

                    def phase_b():
                        with contextlib.ExitStack() as sbk:
                            biasT = sb("biasT", [128, 4096], F32, sbk)
                            esrow = [sb("esrow%d" % g, [128, 512], BF16, sbk) for g in range(2)]
                            Bbias = Buf("biasT")
                            Bes = Buf("esrow")
                            P.op("sp", lambda biasT=biasT: sp.dma_start(out=biasT[:], in_=bias_d[:, :]), w=[Bbias],
                                 dma="bias")
                            with contextlib.ExitStack() as ssk:
                                sinkt = sb("sinkt", [128, 2048], F32, ssk)
                                s_hi = sb("s_hi", [128, 2048], BF16, ssk)
                                s_lo = sb("s_lo", [128, 2048], BF16, ssk)
                                Bsink = Buf("sinkt")
                                P.op("sp", lambda sinkt=sinkt: sp.dma_start(out=sinkt[:], in_=sink_d[:, :]), w=[Bsink],
                                     dma="sink")
                                A(sinkt[:], sinkt[:], AF.Exp, r=[Bsink], w=[Bsink])
                                P.op("dve", lambda s_hi=s_hi, sinkt=sinkt: dve.tensor_copy(out=s_hi[:], in_=sinkt[:]),
                                     r=[Bsink], w=[Bes])
                                TTn(sinkt[:], sinkt[:], s_hi[:], ALU.subtract, r=[Bsink, Bes], w=[Bsink])
                                P.op("dve", lambda s_lo=s_lo, sinkt=sinkt: dve.tensor_copy(out=s_lo[:], in_=sinkt[:]),
                                     r=[Bsink], w=[Bes])
                                for g in range(2):
                                    P.op("dve", lambda g=g: dve.memset(esrow[g][:], 0.0), w=[Bes])
                                    for rr in range(2):
                                        c0 = (g * 2 + rr) * 512
                                        for src, prow in ((s_hi, 64 * rr), (s_lo, 64 * rr + 32)):
                                            P.op("dve", lambda g=g, src=src, prow=prow, c0=c0: dve.tensor_copy(
                                                out=esrow[g][prow:prow + 1, :], in_=src[prow:prow + 1, c0:c0 + 512]),
                                                r=[Bes], w=[Bes])
                                P.barrier()
                            KT = [sb("KT%d" % g, [128, HALO + T], BF16, sbk) for g in range(2)]
                            Vz = [[sb("Vz%d%d" % (g, rr), [128, 9, 128], BF16, sbk) for rr in range(2)] for g in range(2)]
                            BKT = [Buf("KT%d" % g) for g in range(2)]
                            BV2 = [Buf("V2%d" % g) for g in range(2)]
                            QT = sb("QT", [128, 4, T], BF16, sbk)
                            BQT = Buf("QT")
                            t_sb = [sb("b_sb%d" % i, [128, 2048], F32, sbk) for i in range(2)]
                            t_pt = [sb("b_pt%d" % i, [128, 2048], BF16, sbk) for i in range(2)]
                            t_l = [sb("b_l%d" % i, [128, 512], F32, sbk) for i in range(2)]
                            Bsb = [Buf("b_sb%d" % i) for i in range(2)]
                            Bpt = [Buf("b_pt%d" % i) for i in range(2)]
                            Bl = [Buf("b_l%d" % i) for i in range(2)]
                            for g in range(2):
                                for rr in range(2):
                                    P.op("dve", lambda g=g, rr=rr: dve.memset(Vz[g][rr][:], 0.0), w=[BV2[g]])
                            W, BW = w_get()
                            nb = 0
                            for g in range(2):
                                for (c0, c1) in ((0, 512), (512, 1024), (1024, 1152)):
                                    bk = nb % 8
                                    nb += 1
                                    for kc in range(8):
                                        mm(pb[bk][:, 0:c1 - c0], W[:, kc, g * 128:(g + 1) * 128], xT[:, kc, c0:c1],
                                           kc == 0, kc == 7, r=[BW, BxT], w=[Bpb[bk]])
                                    evac(KT[g][:, c0:c1], pb[bk][:, 0:c1 - c0], r=[Bpb[bk]], w=[BKT[g]])
                                for b0 in (0, 4, 8):
                                    nblk = min(4, 9 - b0)
                                    bk = nb % 8
                                    nb += 1
                                    for bi in range(nblk):
                                        blk = b0 + bi
                                        for kc in range(8):
                                            mm(pb[bk][:, bi * 128:(bi + 1) * 128], xT[:, kc, blk * 128:(blk + 1) * 128],
                                               W[:, kc, 256 + g * 128:256 + (g + 1) * 128], kc == 0, kc == 7,
                                               r=[BW, BxT], w=[Bpb[bk]])
                                    pv = pb[bk][:, 0:nblk * 128].rearrange("p (b d) -> p b d", d=128)
                                    for rr in range(2):
                                        evac(Vz[g][rr][:, b0:b0 + nblk, rr * 64:rr * 64 + 64], pv[:, :, rr * 64:rr * 64 + 64],
                                             r=[Bpb[bk]], w=[BV2[g]])
                            it = 0
                            for g in range(2):
                                W, BW = w_get()
                                for i in range(4):
                                    for tt in range(2):
                                        bk = nb % 8
                                        nb += 1
                                        for kc in range(8):
                                            mm(pb[bk][:, :], W[:, kc, i * 128:(i + 1) * 128],
                                               xT[:, kc, HALO + tt * TT:HALO + (tt + 1) * TT], kc == 0, kc == 7,
                                               r=[BW, BxT], w=[Bpb[bk]])
                                        evac(QT[:, i, tt * TT:(tt + 1) * TT], pb[bk][:, :], r=[Bpb[bk]], w=[BQT])

                                def stage1a(n, par, g=g):
                                    for pc in range(2):
                                        for rr in range(2):
                                            bk = pc * 2 + rr
                                            rows = slice(rr * 64, rr * 64 + 64)
                                            for i in range(4):
                                                mm(pb[bk][:, i * 128:(i + 1) * 128],
                                                   KT[g][rows, (n + pc) * 128:(n + pc + 1) * 128],
                                                   QT[rows, i, n * 128:(n + 1) * 128], True, True,
                                                   r=[BKT[g], BQT], w=[Bpb[bk]])

                                def stage1b(n, par, g=g):
                                    for pc in range(2):
                                        for rr in range(2):
                                            bk = pc * 2 + rr
                                            o0 = ((g * 2 + pc) * 2 + rr) * 512
                                            STT(t_sb[par][:, bk * 512:(bk + 1) * 512], pb[bk][:, :], 0.125,
                                                biasT[:, o0:o0 + 512], ALU.mult, ALU.add,
                                                r=[Bpb[bk], Bbias], w=[Bsb[par]])
                                    if p == 0 and n == 0:
                                        TS(t_sb[par][:, 0:1024], t_sb[par][:, 0:1024], c_hm[:, 0:1], None, ALU.add, None,
                                           r=[Bsb[par], Bconst], w=[Bsb[par]])
                                    A(t_pt[par][:], t_sb[par][:], AF.Exp, r=[Bsb[par]], w=[Bpt[par]])

                                def stage2(n, par, g=g):
                                    bo, bd = 4 + par * 2, 5 + par * 2
                                    k = 0
                                    for pc in range(2):
                                        for rr in range(2):
                                            bk = pc * 2 + rr
                                            mm(pb[bo][:, :], Vz[g][rr][:, n + pc, :], t_pt[par][:, bk * 512:(bk + 1) * 512],
                                               k == 0, k == 3, r=[BV2[g], Bpt[par]], w=[Bpb[bo]])
                                            k += 1
                                    k = 0
                                    for pc in range(2):
                                        for rr in range(2):
                                            bk = pc * 2 + rr
                                            mm(pb[bd][:, :], c_cb[:, 128 + rr * 128:256 + rr * 128],
                                               t_pt[par][:, bk * 512:(bk + 1) * 512], k == 0, False,
                                               r=[Bsetup, Bpt[par]], w=[Bpb[bd]])
                                            k += 1
                                    mm(pb[bd][:, :], c_cb[:, 0:128], esrow[g][:], False, True, r=[Bsetup, Bes], w=[Bpb[bd]])
                                    A(t_l[par][:], pb[bd][:, :], AF.Ln, r=[Bpb[bd]], w=[Bl[par]])
                                    A(t_l[par][:], t_l[par][:], AF.Exp, r=[Bl[par]], w=[Bl[par]], scale=-1.0)
                                    TTn(obT[:, 4 * g:4 * g + 4, n * 128:(n + 1) * 128],
                                        pb[bo][:, :].rearrange("p (i q) -> p i q", q=128),
                                        t_l[par][:].rearrange("p (i q) -> p i q", q=128), ALU.mult,
                                        r=[Bpb[bo], Bl[par]], w=[Bob[n // 4]])

                                prev = None
                                for n in range(8):
                                    par = it % 2
                                    it += 1
                                    stage1a(n, par)
                                    l1 = P.capture(lambda: stage1b(n, par))
                                    if prev is None:
                                        P.interleave([l1])
                                    else:
                                        l2 = P.capture(lambda: stage2(*prev))
                                        P.interleave([l1, l2])
                                    prev = (n, par)
                                P.interleave([P.capture(lambda: stage2(*prev))])
                            P.barrier()
                        if p == 0:
                            dbg_dump("obT", obT[:], [128, 8, T], BF16, Bob)

                    def phase_c():
                        with contextlib.ExitStack() as sc:
                            t_mq = [sb("c_mq%d" % i, [128, 2, 512], BF16, sc) for i in range(2)]
                            t_pm = [sb("c_pm%d" % i, [128, 2, 512], BF16, sc) for i in range(2)]
                            t_rc = [sb("c_rc%d" % i, [128, 512], F32, sc) for i in range(2)]
                            Bmq = [Buf("c_mq%d" % i) for i in range(2)]
                            Bpm = [Buf("c_pm%d" % i) for i in range(2)]
                            Brc = [Buf("c_rc%d" % i) for i in range(2)]

                            def cstage1(mh, tt, par, W, BW):
                                tok0 = HALO + tt * TT
                                for dc in range(2):
                                    bk = dc
                                    for kc in range(8):
                                        mm(pb[bk][:, :], W[:, kc, dc * 128:(dc + 1) * 128], xT[:, kc, tok0:tok0 + TT],
                                           kc == 0, kc == 7, r=[BW, BxT], w=[Bpb[bk]])
                                    evac(t_mq[par][:, dc, :], pb[bk][:, :], r=[Bpb[bk]], w=[Bmq[par]])
                                for mb in range(2):
                                    bk = 2 + mb
                                    for dc in range(2):
                                        mm(pb[bk][:, :], mkT[:, mh * 2 + dc, mb * 128:(mb + 1) * 128],
                                           t_mq[par][:, dc, :], dc == 0, dc == 1, r=[Bmk, Bmq[par]], w=[Bpb[bk]])
                                    A(t_pm[par][:, mb, :], pb[bk][:, :], AF.Exp, r=[Bpb[bk]], w=[Bpm[par]],
                                      scale=1.0 / 16.0)

                            def cstage2(mh, tt, par, W, BW):
                                for mb in range(2):
                                    mm(pb[4][:, :], c_one[:], t_pm[par][:, mb, :], mb == 0, mb == 1,
                                       r=[Bsetup, Bpm[par]], w=[Bpb[4]])
                                for vc in range(2):
                                    bk = 5 + vc
                                    for mb in range(2):
                                        mm(pb[bk][:, :], mvb[:, mb, mh * 256 + vc * 128:mh * 256 + (vc + 1) * 128],
                                           t_pm[par][:, mb, :], mb == 0, mb == 1, r=[Bmv, Bpm[par]], w=[Bpb[bk]])
                                A(t_rc[par][:], pb[4][:, :], AF.Ln, r=[Bpb[4]], w=[Brc[par]])
                                A(t_rc[par][:], t_rc[par][:], AF.Exp, r=[Brc[par]], w=[Brc[par]], scale=-1.0)
                                for vc in range(2):
                                    TTn(ocT[:, mh * 2 + vc, tt * TT:(tt + 1) * TT], pb[5 + vc][:, :], t_rc[par][:],
                                        ALU.mult, r=[Bpb[5 + vc], Brc[par]], w=[Boc[tt]])

                            it = 0
                            prev = None
                            for mh in range(4):
                                W, BW = w_get()
                                for tt in range(2):
                                    par = it % 2
                                    it += 1
                                    cur = (mh, tt, par, W, BW)
                                    l1 = P.capture(lambda: cstage1(*cur))
                                    if prev is None:
                                        P.interleave([l1])
                                    else:
                                        P.interleave([l1, P.capture(lambda: cstage2(*prev))])
                                    prev = cur
                            P.interleave([P.capture(lambda: cstage2(*prev))])
                            P.barrier()
                        if p == 0:
                            dbg_dump("ocT", ocT[:], [128, 8, T], BF16, Boc)

                    if p == 0:
                        phase_b()
                        phase_0()
                        phase_c()
                        combine_states()
                        phase_a()
                    else:
                        phase_a()
                        phase_b()
                        phase_c()

                    with contextlib.ExitStack() as sd:
                        t_sg = [sb("d_sg%d" % i, [128, 512], F32, sd) for i in range(4)]
                        t_m = [sb("d_m%d" % i, [128, 512], F32, sd) for i in range(2)]
                        t_tmp = [sb("d_tmp%d" % i, [128, 512], F32, sd) for i in range(4)]
                        Bsg = [Buf("d_sg%d" % i) for i in range(4)]
                        Bm = [Buf("d_m%d" % i) for i in range(2)]
                        Btmp = [Buf("d_tmp%d" % i) for i in range(4)]
                        srcs = ((oaT, Boa), (obT, Bob), (ocT, Boc))
                        it = 0
                        jt = 0
                        for cc in range(8):
                            WG, BWG = w_get()
                            WB, BWB = w_get(ahead=1)
                            for tt in range(2):
                                mp = jt % 2
                                jt += 1
                                tok0 = HALO + tt * TT
                                for b in range(3):
                                    par = it % 4
                                    it += 1
                                    bg, by = par * 2, par * 2 + 1
                                    for kc in range(8):
                                        mm(pb[bg][:, :], WG[:, kc, b * 128:(b + 1) * 128], xT[:, kc, tok0:tok0 + TT],
                                           kc == 0, kc == 7, r=[BWG, BxT], w=[Bpb[bg]])
                                    src, Bsrc = srcs[b]
                                    for kc in range(8):
                                        mm(pb[by][:, :], WB[:, kc, b * 128:(b + 1) * 128],
                                           src[:, kc, tt * TT:(tt + 1) * TT], kc == 0, kc == 7,
                                           r=[BWB, Bsrc[tt]], w=[Bpb[by]])
                                    A(t_sg[par][:], pb[bg][:, :], AF.Sigmoid, r=[Bpb[bg]], w=[Bsg[par]])
                                    if b == 0:
                                        TTn(t_m[mp][:], pb[by][:, :], t_sg[par][:], ALU.mult,
                                            r=[Bpb[by], Bsg[par]], w=[Bm[mp]])
                                    else:
                                        TTn(t_tmp[par][:], pb[by][:, :], t_sg[par][:], ALU.mult,
                                            r=[Bpb[by], Bsg[par]], w=[Btmp[par]])
                                        if b == 1:
                                            TTn(t_m[mp][:], t_m[mp][:], t_tmp[par][:], ALU.add,
                                                r=[Bm[mp], Btmp[par]], w=[Bm[mp]])
                                        else:
                                            TTn(merged[:, cc, tt * TT:(tt + 1) * TT], t_m[mp][:], t_tmp[par][:], ALU.add,
                                                r=[Bm[mp], Btmp[par]], w=[Bmerged[tt]])
                        P.barrier()
                if p == 0:
                    dbg_dump("merged", merged[:], [128, 8, T], BF16, Bmerged)

                with contextlib.ExitStack() as s2:
                    h1T = sb("h1T", [128, 8, T], F32, s2)
                    h1b = sb("h1b", [128, 8, T], BF16, s2)
                    Bh1 = [Buf("h1T%d" % t) for t in range(2)]
                    Bh1b = [Buf("h1b%d" % t) for t in range(2)]
                    xres = [sb("xres%d" % i, [128, 512], F32, s2) for i in range(2)]
                    Bxres = [Buf("xres%d" % i) for i in range(2)]
                    l_sq = [sb("l_sq%d" % i, [128, 512], BF16, s2) for i in range(2)]
                    l_hb = [sb("l_hb%d" % i, [128, 512], BF16, s2) for i in range(2)]
                    Blsq = [Buf("l_sq%d" % i) for i in range(2)]
                    Blhb = [Buf("l_hb%d" % i) for i in range(2)]
                    l_mean = sb("l_mean", [128, 512], F32, s2)
                    l_var = sb("l_var", [128, 512], F32, s2)
                    l_A = sb("l_A", [128, 512], F32, s2)
                    l_Bm = sb("l_Bm", [128, 512], F32, s2)
                    l_t = [sb("l_t%d" % i, [128, 512], F32, s2) for i in range(2)]
                    Bl = {n: Buf("l_" + n) for n in ("mean", "var", "A", "Bm")}
                    Blt = [Buf("l_t%d" % i) for i in range(2)]
                    lnc = {"i": 0, "o": 0}

                    def layernorm(src, Bsrc, tt, goff, boff, final):
                        ts = slice(tt * TT, (tt + 1) * TT)
                        bm, bq = 6, 7
                        for cc in range(8):
                            par = lnc["i"] % 2
                            lnc["i"] += 1
                            A(l_sq[par][:], src[:, cc, ts], AF.Square, r=[Bsrc], w=[Blsq[par]])
                            P.op("dve", lambda par=par, cc=cc: dve.tensor_copy(out=l_hb[par][:], in_=src[:, cc, ts]),
                                 r=[Bsrc], w=[Blhb[par]])
                            mm(pb[bm][:, :], c_on1024[:], l_hb[par][:], cc == 0, cc == 7, r=[Bsetup, Blhb[par]],
                               w=[Bpb[bm]])
                            mm(pb[bq][:, :], c_on1024[:], l_sq[par][:], cc == 0, cc == 7, r=[Bsetup, Blsq[par]],
                               w=[Bpb[bq]])
                        P.op("dve", lambda: dve.tensor_copy(out=l_mean[:], in_=pb[bm][:, :]), r=[Bpb[bm]],
                             w=[Bl["mean"]])
                        TTn(l_var[:], l_mean[:], l_mean[:], ALU.mult, r=[Bl["mean"]], w=[Bl["var"]])
                        TTn(l_var[:], pb[bq][:, :], l_var[:], ALU.subtract, r=[Bpb[bq], Bl["var"]], w=[Bl["var"]])
                        A(l_var[:], l_var[:], AF.Ln, r=[Bl["var"]], w=[Bl["var"]], bias=LN_EPS)
                        A(l_A[:], l_var[:], AF.Exp, r=[Bl["var"]], w=[Bl["A"]], scale=-0.5)
                        STT(l_Bm[:], l_mean[:], -1.0, l_A[:], ALU.mult, ALU.mult, r=[Bl["mean"], Bl["A"]], w=[Bl["Bm"]])
                        for cc in range(8):
                            par = lnc["i"] % 2
                            lnc["i"] += 1
                            TTn(l_t[par][:], src[:, cc, ts], l_A[:], ALU.mult, r=[Bsrc, Bl["A"]], w=[Blt[par]])
                            TTn(l_t[par][:], l_t[par][:], l_Bm[:], ALU.add, r=[Blt[par], Bl["Bm"]], w=[Blt[par]])
                            gsc = c_vec[:, goff + cc:goff + cc + 1]
                            bsc = c_vec[:, boff + cc:boff + cc + 1]
                            if not final:
                                A(src[:, cc, ts], l_t[par][:], AF.Identity, r=[Blt[par], Bconst], w=[Bsrc],
                                  scale=gsc, bias=bsc)
                                A(h1b[:, cc, ts], l_t[par][:], AF.Identity, r=[Blt[par], Bconst], w=[Bh1b[tt]],
                                  scale=gsc, bias=bsc)
                            else:
                                so = lnc["o"] % 2
                                lnc["o"] += 1
                                A(ostg[so][:], l_t[par][:], AF.Identity, r=[Blt[par], Bconst], w=[Bostg[so]],
                                  scale=gsc, bias=bsc)
                                dst = outT_d[cc * 128:(cc + 1) * 128, p * T + tt * TT:p * T + (tt + 1) * TT]
                                P.op("sp", lambda so=so, dst=dst: sp.dma_start(out=dst, in_=ostg[so][:]),
                                     r=[Bostg[so]], w=[Buf("o")], dma="out%d" % so)

                    WO = [w_get(), w_get(ahead=1)]
                    xi = {"i": 0}

                    def d2_tile(tt):
                        for cc in range(8):
                            W, BW = WO[cc // 4]
                            ci = cc % 4
                            bk = cc % 6
                            xp = xi["i"] % 2
                            xi["i"] += 1
                            srcx = xT_d[p, cc * 128:(cc + 1) * 128, HALO + tt * TT:HALO + (tt + 1) * TT]
                            P.op("sp", lambda xp=xp, srcx=srcx: sp.dma_start(out=xres[xp][:], in_=srcx),
                                 w=[Bxres[xp]], dma="xres%d" % xp)
                            for kc in range(8):
                                mm(pb[bk][:, :], W[:, kc, ci * 128:(ci + 1) * 128],
                                   merged[:, kc, tt * TT:(tt + 1) * TT], kc == 0, kc == 7,
                                   r=[BW, Bmerged[tt]], w=[Bpb[bk]])
                            STT(h1T[:, cc, tt * TT:(tt + 1) * TT], xres[xp][:], ALPHA, pb[bk][:, :], ALU.mult,
                                ALU.add, r=[Bxres[xp], Bpb[bk]], w=[Bh1[tt]])

                    d2_tile(0)
                    P.interleave([P.capture(lambda: d2_tile(1)),
                                  P.capture(lambda: layernorm(h1T, Bh1[0], 0, 8, 16, False))])
                    layernorm(h1T, Bh1[1], 1, 8, 16, False)
                    if p == 0:
                        dbg_dump("h1T", h1T[:], [128, 8, T], F32, Bh1)

                    with contextlib.ExitStack() as se:
                        aT = sb("aT", [128, 32, T], BF16, se)
                        BaT = [Buf("aT%d" % t) for t in range(2)]
                        t_r = [sb("e_r%d" % i, [128, 512], F32, se) for i in range(2)]
                        Br = [Buf("e_r%d" % i) for i in range(2)]
                        it = 0
                        for u in range(8):
                            W, BW = w_get()
                            for fi in range(4):
                                fc = 4 * u + fi
                                for tt in range(2):
                                    par = it % 2
                                    bk = it % 6
                                    it += 1
                                    for kc in range(8):
                                        mm(pb[bk][:, :], W[:, kc, fi * 128:(fi + 1) * 128],
                                           h1b[:, kc, tt * TT:(tt + 1) * TT], kc == 0, kc == 7,
                                           r=[BW, Bh1b[tt]], w=[Bpb[bk]])
                                    A(t_r[par][:], pb[bk][:, :], AF.Relu, r=[Bpb[bk]], w=[Br[par]])
                                    TTn(aT[:, fc, tt * TT:(tt + 1) * TT], t_r[par][:], t_r[par][:], ALU.mult,
                                        r=[Br[par]], w=[BaT[tt]])
                        def down_tile(tt):
                            for cc in range(8):
                                W, BW = w_get(kc=32)
                                bk = cc % 6
                                for fc in range(32):
                                    mm(pb[bk][:, :], W[:, fc, :], aT[:, fc, tt * TT:(tt + 1) * TT], fc == 0, fc == 31,
                                       r=[BW, BaT[tt]], w=[Bpb[bk]])
                                STT(h1T[:, cc, tt * TT:(tt + 1) * TT], h1T[:, cc, tt * TT:(tt + 1) * TT], ALPHA,
                                    pb[bk][:, :], ALU.mult, ALU.add, r=[Bh1[tt], Bpb[bk]], w=[Bh1[tt]])

                        down_tile(0)
                        l_ln = P.capture(lambda: layernorm(h1T, Bh1[0], 0, 24, 32, True))
                        P.spread = l_ln
                        down_tile(1)
                        P.flush_spread()
                        layernorm(h1T, Bh1[1], 1, 24, 32, True)
                        P.barrier()
        for p in range(NPASS):
            run_pass(p)
        P.emit()
        for k in list(P.sems):
            pass
        cnt = {}
        for o2 in P.ops:
            if o2.grp and (o2.grp.startswith("out")):
                cnt[o2.grp] = o2.val
        for g, v in cnt.items():
            sp.wait_ge(P.sem("d_" + g), v)
    return nc, dbg_d


def _prep_inputs(x, mem, w_in, lb_logits, hg_norm_gain, swa_sinks, rel_bias, w_mem_kv, w_branch_hg, w_branch_swa,
                 w_branch_mem, w_out, ln1_g, ln1_b, w_up, w_down, ln2_g, ln2_b):
    f = lambda a: np.asarray(a, dtype=np.float32)
    x, mem = f(x), f(mem)
    wall, _, _ = _build_units(f(w_in)[0], f(w_mem_kv)[0], f(w_branch_hg)[0], f(w_branch_swa)[0], f(w_branch_mem)[0],
                              f(w_out)[0], f(w_up)[0], f(w_down)[0])
    lbT = np.ascontiguousarray(f(lb_logits).reshape(2, 8, 128).transpose(2, 0, 1).reshape(128, 16))
    v = lambda a: f(a).reshape(8, 128).T
    vecs = np.ascontiguousarray(np.concatenate([v(hg_norm_gain), v(ln1_g), v(ln1_b), v(ln2_g), v(ln2_b)], axis=1))
    biasT = _bias_table(f(rel_bias))
    sinkT = _sink_table(f(swa_sinks)[0])
    masks = _masks()
    ident = np.eye(128, dtype=np.float32)
    in_maps = []
    for c in range(NCORES):
        b, j = c // 4, c % 4
        t0 = j * TOK_CORE
        xt = np.zeros((NPASS, D, HALO + T), np.float32)
        for p in range(NPASS):
            s = t0 + p * T - HALO
            if s < 0:
                xt[p, :, HALO:] = x[b, 0:T, :].T
            else:
                xt[p] = x[b, s:s + HALO + T, :].T
        sel = np.zeros((128, 4), np.float32)
        sel[:, j] = 1.0
        hm = np.full((128, 1), NEG if j == 0 else 0.0, np.float32)
        in_maps.append({"xT": xt, "sel": sel, "memT": np.ascontiguousarray(mem[b].T), "wall": wall, "lbT": lbT, "vecs": vecs,
                        "hmask": hm, "biasT": biasT, "sinkT": sinkT, "masks": masks, "ident": ident})
    return in_maps


def kernel(**inputs):
    in_maps = _prep_inputs(**inputs)
    nc, _ = build_program(False)
    res = run_bass_kernel_spmd(nc, in_maps, core_ids=list(range(NCORES)))
    out = np.empty((2, SEQ, D), np.float32)
    for c in range(NCORES):
        b, j = c // 4, c % 4
        out[b, j * TOK_CORE:(j + 1) * TOK_CORE, :] = res.results[c]["outT"].T
    return out
```

```python
import contextlib
import numpy as np
import concourse.bass as bass
import concourse.mybir as mybir
from concourse.bass_utils import run_bass_kernel_spmd

F32 = mybir.dt.float32
BF16 = mybir.dt.bfloat16
AF = mybir.ActivationFunctionType
ALU = mybir.AluOpType

NCORES = 8
D = 1024
SEQ = 8192
TOK_CORE = 2048
NPASS = 2
T = 1024
HALO = 128
TT = 512
ALPHA = 2.0 ** 0.25
LN_EPS = 1e-5
RMS_EPS = 1e-6
NEG = -30000.0
NSLOT = 3
NPRE = 2


class Buf:
    __slots__ = ("name", "w", "rs")

    def __init__(self, name):
        self.name = name
        self.w = None
        self.rs = {}


class Op:
    __slots__ = ("eng", "fn", "deps", "idx", "inc", "val", "grp", "amt")


class Prog:
    def __init__(self, nc, es):
        self.nc = nc
        self.es = es
        self.ops = []
        self.last = {}
        self.pending = {}
        self.engs = {"pe": nc.tensor, "act": nc.scalar, "dve": nc.vector, "pool": nc.gpsimd, "sp": nc.sync}
        self.sems = {}
        self.cap = None
        self.spread = None
        self.in_spread = False
        self.spread_cnt = 0
        self.spread_every = 4

    def sem(self, key):
        if key not in self.sems:
            self.sems[key] = self.es.enter_context(self.nc.semaphore("s_" + key))
        return self.sems[key]

    def capture(self, body):
        self.cap = []
        body()
        ops, self.cap = self.cap, None
        return ops

    def interleave(self, lists):
        its = [list(l) for l in lists]
        n = max(len(l) for l in its)
        for i in range(n):
            for l in its:
                if i < len(l):
                    self.op(*l[i])

    def flush_spread(self):
        l, self.spread = self.spread, None
        for a in l or ():
            self.op(*a)

    def op(self, eng, fn, r=(), w=(), dma=None, amt=16):
        if self.cap is not None:
            self.cap.append((eng, fn, tuple(r), tuple(w), dma, amt))
            return None
        if self.spread and not self.in_spread:
            self.in_spread = True
            self.spread_cnt += 1
            if self.spread_cnt % self.spread_every == 0:
                self.op(*self.spread.pop(0))
            self.in_spread = False
        o = Op()
        o.amt = amt
        o.eng = eng
        o.fn = fn
        o.idx = len(self.ops)
        o.inc = False
        o.val = 0
        o.grp = dma
        key = ("dma", dma) if dma else eng
        deps = {}

        def add(d, raw):
            if d is None or d is o:
                return
            k = ("dma", d.grp) if d.grp else d.eng
            cur = deps.setdefault(k, [None, None])
            if cur[0] is None or cur[0].idx < d.idx:
                cur[0] = d
            if raw and (cur[1] is None or cur[1].idx < d.idx):
                cur[1] = d

        for b in r:
            add(b.w, True)
        for b in w:
            add(b.w, False)
            for d in b.rs.values():
                add(d, False)
        for d in self.pending.pop(eng, ()):
            add(d, True)
        o.deps = deps
        for b in w:
            b.w = o
            b.rs = {}
        for b in r:
            if b.w is not o:
                b.rs[key] = o
        self.ops.append(o)
        if not dma:
            self.last[eng] = o
        return o

    def barrier(self):
        lasts = [v for k, v in self.last.items()]
        for e in self.engs:
            self.pending[e] = list(lasts)

    def _semdeps(self, o):
        out = []
        for k, (dany, draw) in o.deps.items():
            if k[0] == "dma" if isinstance(k, tuple) else False:
                out.append(dany)
            elif o.grp is None and k == o.eng:
                if o.eng != "pe" and draw is not None:
                    out.append(draw)
            else:
                out.append(dany)
        return out

    def emit(self):
        for o in self.ops:
            for d in self._semdeps(o):
                if not d.grp:
                    d.inc = True
        cnt = {}
        for o in self.ops:
            if o.grp:
                cnt[("dma", o.grp)] = cnt.get(("dma", o.grp), 0) + o.amt
                o.val = cnt[("dma", o.grp)]
            elif o.inc:
                cnt[o.eng] = cnt.get(o.eng, 0) + 1
                o.val = cnt[o.eng]
        waited = {}
        nwait = 0
        for o in self.ops:
            q = self.engs[o.eng]
            for d in self._semdeps(o):
                sk = ("d_" + d.grp) if d.grp else ("e_" + d.eng)
                if waited.get((o.eng, sk), 0) >= d.val:
                    continue
                q.wait_ge(self.sem(sk), d.val)
                nwait += 1
                waited[(o.eng, sk)] = d.val
            inst = o.fn()
            if o.grp:
                inst.then_inc(self.sem("d_" + o.grp), o.amt)
            elif o.inc:
                inst.then_inc(self.sem("e_" + o.eng), 1)
        return nwait


def _t5_bucket_np(n):
    n = np.asarray(n, np.int32)
    nf = np.maximum(n, 1).astype(np.float32)
    large = 16 + (np.log(nf / np.float32(16)) / np.float32(np.log(128 / 16)) * np.float32(16)).astype(np.int32)
    large = np.minimum(large, 31)
    return np.where(n < 16, n, large)


def _tile_k(w, cols):
    K = w.shape[0]
    sub = w[:, cols]
    return sub.reshape(K // 128, 128, -1).transpose(1, 0, 2).reshape(128, -1)


def _build_units(w_in, w_mem_kv, w_bhg, w_bswa, w_bmem, w_out, w_up, w_down):
    parts = []
    off = [0]

    def add(a):
        a = np.ascontiguousarray(a, dtype=np.float32)
        u = (off[0], a.shape[1])
        parts.append(a)
        off[0] += a.shape[1]
        return u

    ar = np.arange
    once = []
    for u in range(4):
        once.append(add(_tile_k(w_mem_kv, ar(u * 512, (u + 1) * 512))))
    for hu in range(2):
        once.append(add(_tile_k(w_in, 1024 + ar(hu * 512, (hu + 1) * 512))))
    pu = []
    for i in range(2):
        pu.append(add(_tile_k(w_in, 2048 + ar(i * 512, (i + 1) * 512))))
    for h in range(8):
        cols = np.concatenate([h * 128 + ar(128), 1024 + h * 128 + ar(128), 3072 + h * 128 + ar(128)])
        pu.append(add(_tile_k(w_in, cols)))
    sk0 = 5120 + ar(64)
    sk1 = 5120 + 64 + ar(64)
    sv0 = 5248 + ar(64)
    sv1 = 5248 + 64 + ar(64)
    cols = np.concatenate([sk0, sk0, sk1, sk1, sv0, sv0, sv1, sv1])
    pu.append(add(_tile_k(w_in, cols)))
    for g in range(2):
        pu.append(add(_tile_k(w_in, 4096 + ar(g * 512, (g + 1) * 512))))
    for mh in range(4):
        pu.append(add(_tile_k(w_in, 5376 + ar(mh * 256, (mh + 1) * 256))))
    for cc in range(8):
        cols = np.concatenate([6400 + b * 1024 + cc * 128 + ar(128) for b in range(3)])
        pu.append(add(_tile_k(w_in, cols)))
        c2 = cc * 128 + ar(128)
        pu.append(add(np.concatenate([_tile_k(w_bhg, c2).reshape(128, 8, 128),
                                      _tile_k(w_bswa, c2).reshape(128, 8, 128),
                                      _tile_k(w_bmem, c2).reshape(128, 8, 128)], axis=2).reshape(128, -1)))
    for i in range(2):
        pu.append(add(_tile_k(w_out, ar(i * 512, (i + 1) * 512))))
    for u in range(8):
        pu.append(add(_tile_k(w_up, ar(u * 512, (u + 1) * 512))))
    for cc in range(8):
        pu.append(add(_tile_k(w_down, cc * 128 + ar(128))))
    wall = np.concatenate(parts, axis=1)
    return wall, once, pu


def _unit_sizes():
    once = [4096] * 6
    pu = [4096] * 2 + [3072] * 8 + [4096] + [4096] * 2 + [2048] * 4 + [3072, 3072] * 8 + [4096] * 2 + [4096] * 8 + [4096] * 8
    return once, pu


def _bias_table(rel_bias):
    k = np.arange(128)[:, None]
    q = np.arange(128)[None, :]
    dist_prev = q + 128 - k
    dist_cur = q - k
    tab = np.empty((128, 2, 2, 2, 4, 128), np.float32)
    for g in range(2):
        for r in range(2):
            for i in range(4):
                h = 8 * g + 2 * i + r
                for pc, dist in ((0, dist_prev), (1, dist_cur)):
                    valid = (dist >= 0) & (dist < 128)
                    bk = _t5_bucket_np(np.clip(dist, 0, 127))
                    tab[:, g, pc, r, i, :] = np.where(valid, rel_bias[bk, h], np.float32(NEG))
    return tab.reshape(128, 4096)


def _sink_table(sinks):
    tab = np.empty((128, 2, 2, 4, 128), np.float32)
    for g in range(2):
        for r in range(2):
            for i in range(4):
                tab[:, g, r, i, :] = sinks[8 * g + 2 * i + r]
    return tab.reshape(128, 2048)


def _masks():
    m = np.ones((128, 1024), np.float32)
    m[:, 0:512:64] = 0.0
    s = np.arange(128)[:, None]
    c = np.arange(128)[None, :]
    am = ((s // 64 == c // 64) & (c >= s)).astype(np.float32)
    m[:, 512:1024] = np.tile(am, (1, 4))
    sel = np.zeros((128, 128), np.float32)
    sel[0, 0:64] = 1.0
    sel[32, 0:64] = 1.0
    sel[64, 64:128] = 1.0
    sel[96, 64:128] = 1.0
    o0 = np.zeros((128, 128), np.float32)
    o0[:, 0:64] = 1.0
    o1 = np.zeros((128, 128), np.float32)
    o1[:, 64:128] = 1.0
    return np.concatenate([m, sel, o0, o1], axis=1)


def build_program(dbg=False):
    nc = bass.Bass("TRN2", target_bir_lowering=False)
    once_sz, pass_sz = _unit_sizes()
    tot = sum(once_sz) + sum(pass_sz)

    def din(name, shape, dt=F32):
        return nc.dram_tensor(name, shape, dt, kind="ExternalInput").ap()

    xT_d = din("xT", [NPASS, D, HALO + T])
    sel_d = din("sel", [128, 4])
    cc_in = nc.dram_tensor("cc_in", [128, 1032], F32)
    cc_out = nc.dram_tensor("cc_out", [512, 1032], F32)
    vt_d = nc.dram_tensor("vt_scratch", [NPASS, 128, 8 * 1024], BF16)
    memT_d = din("memT", [D, 256])
    wall_d = din("wall", [128, tot])
    lbT_d = din("lbT", [128, 16])
    vec_d = din("vecs", [128, 40])
    hm_d = din("hmask", [128, 1])
    bias_d = din("biasT", [128, 4096])
    sink_d = din("sinkT", [128, 2048])
    msk_d = din("masks", [128, 1408])
    id_d = din("ident", [128, 128])
    outT_d = nc.dram_tensor("outT", [D, TOK_CORE], F32, kind="ExternalOutput").ap()
    dbg_d = {}

    es = contextlib.ExitStack()
    with es:
        P = Prog(nc, es)

        uniq = {"n": 0}

        def sb(name, shape, dt, stack=es):
            uniq["n"] += 1
            return stack.enter_context(nc.sbuf_tensor("%s_%d" % (name, uniq["n"]), shape, dt))

        pb = [es.enter_context(nc.psum_tensor("pb%d" % i, [128, 512], F32)) for i in range(8)]
        Bpb = [Buf("pb%d" % i) for i in range(8)]
        slots = [sb("wslot%d" % i, [128, 4096], BF16) for i in range(NSLOT)]
        Bslot = [Buf("slot%d" % i) for i in range(NSLOT)]
        c_lb = sb("c_lb", [128, 16], F32)
        c_lbv = sb("c_lbv", [128, 8], F32)
        c_oml = sb("c_oml", [128, 8], F32)
        c_lnoml = sb("c_lnoml", [128, 8], F32)
        c_vec = sb("c_vec", [128, 40], F32)
        c_hm = sb("c_hm", [128, 1], F32)
        c_msk = sb("c_msk", [128, 1408], F32)
        c_cb = sb("c_cb", [128, 384], BF16)
        c_id = sb("c_id", [128, 128], BF16)
        c_on128 = sb("c_on128", [128, 128], BF16)
        c_on1024 = sb("c_on1024", [128, 128], BF16)
        c_one = sb("c_one", [128, 128], BF16)
        mkT = sb("mkT", [128, 8, 256], BF16)
        mvb = sb("mvb", [128, 2, 1024], BF16)
        s_carry = sb("s_carry", [128, 8, 128], F32)
        ostg = [sb("ostg%d" % i, [128, 512], F32) for i in range(2)]
        Bostg = [Buf("ostg%d" % i) for i in range(2)]
        Bconst = Buf("const")
        Bmk = Buf("mkT")
        Bmv = Buf("mvb")
        Bcarry = [Buf("carry%d" % h) for h in range(8)]
        Bsetup = Buf("setup")

        E = P.engs
        pe, act, dve, pool, sp = nc.tensor, nc.scalar, nc.vector, nc.gpsimd, nc.sync

        def mm(out, lhsT, rhs, start, stop, r, w):
            P.op("pe", lambda: pe.matmul(out, lhsT=lhsT, rhs=rhs, start=start, stop=stop), r=r, w=w)

        def A(out, in_, func, r, w, **kw):
            P.op("act", lambda: act.activation(out=out, in_=in_, func=func, **kw), r=r, w=w)

        def TTn(out, in0, in1, op, r, w):
            P.op("dve", lambda: dve.tensor_tensor(out=out, in0=in0, in1=in1, op=op), r=r, w=w)

        def TS(out, in0, s1, s2, op0, op1, r, w):
            if op1 is None:
                P.op("dve", lambda: dve.tensor_scalar(out=out, in0=in0, scalar1=s1, scalar2=None, op0=op0), r=r, w=w)
            else:
                P.op("dve", lambda: dve.tensor_scalar(out=out, in0=in0, scalar1=s1, scalar2=s2, op0=op0, op1=op1),
                     r=r, w=w)

        def STT(out, in0, scalar, in1, op0, op1, r, w):
            P.op("dve", lambda: dve.scalar_tensor_tensor(out=out, in0=in0, scalar=scalar, in1=in1, op0=op0, op1=op1),
                 r=r, w=w)

        def dbg_dump(name, ap, shape, dt, rbufs):
            if not dbg:
                return
            d = nc.dram_tensor("dbg_" + name, list(shape), dt, kind="ExternalOutput").ap()
            dbg_d[name] = d
            P.op("sp", lambda: sp.dma_start(out=d, in_=ap), r=rbufs, w=[Buf("dbgw")], dma="out")

        units = []
        o = 0
        for n in once_sz:
            units.append((o, n))
            o += n
        pass_units = []
        for n in pass_sz:
            pass_units.append((o, n))
            o += n
        f_units = units[4:6]
        kv_units = units[0:4]
        units = []
        for t in range(NPRE):
            units.extend([pass_units[0], pass_units[1], f_units[0], f_units[1]])
        pass_units = pass_units + pass_units[-8:]
        pu_a, pu_bc, pu_rest = pass_units[2:10], pass_units[10:17], pass_units[17:]
        units.extend(pu_bc[0:3] + kv_units + pu_bc[3:] + pu_a + pu_rest)
        for p in range(1, NPASS):
            units.extend(pu_a + pu_bc + pu_rest)
        wstate = {"next": 0, "cur": 0}

        def w_ensure(upto):
            while wstate["next"] <= min(upto, len(units) - 1):
                u = wstate["next"]
                s = u % NSLOT
                off, n = units[u]
                dst = slots[s][:, 0:n].rearrange("p (a b) -> p a b", b=1024)
                src = wall_d[:, off:off + n].rearrange("p (a b) -> p a b", b=1024)
                P.op("pool", lambda dst=dst, src=src: pool.dma_start(out=dst, in_=src), w=[Bslot[s]], dma="w%d" % s)
                wstate["next"] += 1

        def w_get(kc=8, ahead=NSLOT - 1):
            u = wstate["cur"]
            wstate["cur"] += 1
            w_ensure(u + ahead)
            s = u % NSLOT
            n = units[u][1]
            ap = slots[s][:, 0:n].rearrange("p (k c) -> p k c", k=kc)
            return ap, Bslot[s]

        def cload(dst, src):
            P.op("sp", lambda: sp.dma_start(out=dst, in_=src), w=[Bconst], dma="const")

        cload(c_lb[:], lbT_d[:, :])
        cload(c_vec[:], vec_d[:, :])
        cload(c_hm[:], hm_d[:, :])
        cload(c_msk[:], msk_d[:, :])
        Bcid = Buf("c_id")
        P.op("pool", lambda: pool.dma_start(out=c_id[:], in_=id_d[:, :]), w=[Bcid], dma="cid")
        P.op("dve", lambda: dve.tensor_copy(out=c_cb[:], in_=c_msk[:, 1024:1408]), r=[Bconst], w=[Bsetup])
        P.op("dve", lambda: dve.memset(c_on128[:], 1.0 / 128.0), w=[Bsetup])
        P.op("dve", lambda: dve.memset(c_on1024[:], 1.0 / 1024.0), w=[Bsetup])
        P.op("dve", lambda: dve.memset(c_one[:], 1.0), w=[Bsetup])
        P.op("dve", lambda: dve.memset(s_carry[:], 0.0), w=Bcarry)
        TTn(c_lbv[:], c_lb[:, 0:8], c_lb[:, 8:16], ALU.subtract, r=[Bconst], w=[Bsetup])
        A(c_lbv[:], c_lbv[:], AF.Sigmoid, r=[Bsetup], w=[Bsetup])
        A(c_oml[:], c_lbv[:], AF.Identity, r=[Bsetup], w=[Bsetup], scale=-1.0, bias=1.0)
        A(c_lnoml[:], c_oml[:], AF.Ln, r=[Bsetup], w=[Bsetup])
        gain = c_vec[:, 0:8]
        msk_scan = c_msk[:, 0:512]
        msk_attn = c_msk[:, 512:1024]

        evs = {"i": 0}

        def evac(out, in_, r, w):
            evs["i"] += 1
            if evs["i"] % 2 == 0:
                A(out, in_, AF.Copy, r=r, w=w)
            else:
                P.op("dve", lambda: dve.tensor_copy(out=out, in_=in_), r=r, w=w)

        p_cur = {"p": 0}

        def hgrn_phase(sa, xT, BxT, xoff, full, oaT, Boa, gsum=None, Bgsum=None):
            assert full
            vtok = sb("vtok", [128, 8, 1024], BF16, sa)
            Bvall = Buf("vtok_all")
            Bv = [Bvall] * 8
            pp = p_cur["p"]
            P.op("sp", lambda: sp.dma_start(out=vtok[:].rearrange("p b c -> p (b c)"), in_=vt_d[pp]),
                 r=[Bvt[pp]], w=[Bvall], dma="vtld")
            streams = []
            for si in range(2):
                st = {}
                fn = ("sf", "geb", "bt1", "enbln", "k", "kinf", "gs0", "gs1", "lnt1")
                for n in fn:
                    st[n] = sb("a%d_%s" % (si, n), [128, 512], F32, sa)
                bn = ("qin0", "qin1", "kin", "kout", "koutT", "attn", "sq")
                for n in bn:
                    st[n] = sb("a%d_%s" % (si, n), [128, 512], BF16, sa)
                st["dec"] = sb("a%d_dec" % si, [128, 8], F32, sa)
                st["sall"] = sb("a%d_sall" % si, [128, 8, 128], F32, sa)
                st["sbf"] = sb("a%d_sbf" % si, [128, 8, 128], BF16, sa)
                st["B"] = {n: Buf("a%d_%s" % (si, n)) for n in fn + bn + ("dec", "sbf")}
                st["Bsall"] = [Buf("a%d_sall%d" % (si, c)) for c in range(8)]
                st["bk"] = [4 * si + i for i in range(4)]
                streams.append(st)

            def proj(st, W, BW, col, bank, tt):
                tok0 = xoff + tt * TT
                for kc in range(8):
                    mm(pb[bank][:, :], W[:, kc, col:col + 128], xT[:, kc, tok0:tok0 + TT],
                       kc == 0, kc == 7, r=[BW, BxT], w=[Bpb[bank]])

            def head(st, h, tt):
                B = st["B"]
                X0, X1, X2, X3 = st["bk"]
                t_sf, t_g, t_b, t_enb, t_k, t_kinf = st["sf"], st["geb"], st["bt1"], st["enbln"], st["k"], st["kinf"]
                t_eb = t_g
                A(t_sf[:], pb[X1][:, :], AF.Exp, r=[Bpb[X1]], w=[B["sf"]], scale=-1.0)
                A(t_g[:], t_sf[:], AF.Ln, r=[B["sf"], Bsetup], w=[B["geb"]], scale=c_lbv[:, h:h + 1], bias=1.0)
                A(t_k[:], t_sf[:], AF.Ln, r=[B["sf"]], w=[B["k"]], bias=1.0)
                TTn(t_g[:], t_g[:], t_k[:], ALU.subtract, r=[B["geb"], B["k"]], w=[B["geb"]])
                P.op("dve", lambda: dve.tensor_tensor_scan(out=t_b[:], data0=msk_scan, data1=t_g[:],
                                                           initial=0.0, op0=ALU.mult, op1=ALU.add),
                     r=[B["geb"], Bconst], w=[B["bt1"]])
                A(t_eb[:], t_b[:], AF.Exp, r=[B["bt1"]], w=[B["geb"]])
                A(st["dec"][:], t_b[:, 63:512:64], AF.Exp, r=[B["bt1"]], w=[B["dec"]])
                TTn(t_enb[:], pb[X1][:, :], t_k[:], ALU.add, r=[Bpb[X1], B["k"]], w=[B["enbln"]])
                TTn(t_enb[:], t_enb[:], t_b[:], ALU.add, r=[B["enbln"], B["bt1"]], w=[B["enbln"]])
                A(t_kinf[:], t_enb[:], AF.Exp, r=[B["enbln"], Bsetup], w=[B["kinf"]], scale=-1.0,
                  bias=c_lnoml[:, h:h + 1])
                qn = "qin%d" % tt
                TTn(st[qn][:], pb[X0][:, :], t_eb[:], ALU.mult, r=[Bpb[X0], B["geb"]], w=[B[qn]])
                A(st["kin"][:], t_kinf[:], AF.Copy, r=[B["kinf"]], w=[B["kin"]])
                TTn(st["kout"][:].rearrange("p (c t) -> p c t", t=64),
                    t_kinf[:].rearrange("p (c t) -> p c t", t=64),
                    st["dec"][:, :].unsqueeze(2).broadcast_to([128, 8, 64]), ALU.mult,
                    r=[B["kinf"], B["dec"]], w=[B["kout"]])

            def mid(st, h, tt, W, BW, nxt):
                B = st["B"]
                X0, X1, X2, X3 = st["bk"]
                Bsall = st["Bsall"]
                s_all = st["sall"]
                trv = pb[X1][:].bitcast(BF16)
                tb0 = tt * 4
                if tt == 0:
                    for t2 in range(2):
                        proj(st, W, BW, 256, X2, t2)
                        A(st["gs%d" % t2][:], pb[X2][:, :], AF.Silu, r=[Bpb[X2]], w=[B["gs%d" % t2]])
                for j in range(4):
                    P.op("pe", lambda j=j: pe.transpose(out=trv[:, j * 128:(j + 1) * 128],
                                                        in_=st["kout"][:, j * 128:(j + 1) * 128],
                                                        identity=c_id[:]),
                         r=[B["kout"], Bcid], w=[Bpb[X1]])
                A(st["koutT"][:], trv[:, 0:512], AF.Copy, r=[Bpb[X1]], w=[B["koutT"]])
                for j in range(4):
                    sl = slice(j * 128, (j + 1) * 128)
                    mm(pb[X0][:, sl], st["kin"][:, sl], st["qin%d" % tt][:, sl], True, True,
                       r=[B["kin"], B["qin%d" % tt]], w=[Bpb[X0]])
                TTn(st["attn"][:], pb[X0][:, :], msk_attn, ALU.mult, r=[Bpb[X0], Bconst], w=[B["attn"]])
                for c in range(8):
                    j, rr = c // 2, c % 2
                    bk = (X2, X3)[rr]
                    rows = slice(rr * 64, rr * 64 + 64)
                    mm(pb[bk][:, j * 128:(j + 1) * 128], st["koutT"][rows, j * 128:(j + 1) * 128],
                       vtok[rows, tb0 + j, h * 128:(h + 1) * 128], True, True,
                       r=[B["koutT"], Bv[tb0 + j]], w=[Bpb[bk]])
                P.op("pool", lambda: pool.tensor_copy(out=s_all[:, 0, :], in_=s_carry[:, h, :]),
                     r=[Bcarry[h]], w=[Bsall[0]])
                if nxt is not None:
                    W2, BW2, tt2 = nxt
                    proj(st, W2, BW2, 128, X1, tt2)
                    proj(st, W2, BW2, 0, X0, tt2)
                for c in range(8):
                    j, rr = c // 2, c % 2
                    bk = (X2, X3)[rr]
                    if c < 7:
                        out, wb = s_all[:, c + 1, :], Bsall[c + 1]
                    else:
                        out, wb = s_carry[:, h, :], Bcarry[h]
                    STT(out, s_all[:, c, :], st["dec"][:, c:c + 1], pb[bk][:, j * 128:(j + 1) * 128],
                        ALU.mult, ALU.add, r=[Bsall[c], B["dec"], Bpb[bk]], w=[wb])

            def tail(st, h, tt):
                B = st["B"]
                X0, X1, X2, X3 = st["bk"]
                Bsall = st["Bsall"]
                s_all, t_sbf = st["sall"], st["sbf"]
                t_ln = t_t1 = st["lnt1"]
                qn = "qin%d" % tt
                tb0 = tt * 4
                A(t_sbf[:].rearrange("p c d -> p (c d)"), s_all[:].rearrange("p c d -> p (c d)"), AF.Copy,
                  r=Bsall, w=[B["sbf"]])
                for j in range(4):
                    sl = slice(j * 128, (j + 1) * 128)
                    mm(pb[X2][:, sl], vtok[:, tb0 + j, h * 128:(h + 1) * 128], st["attn"][:, sl], True, False,
                       r=[Bv[tb0 + j], B["attn"]], w=[Bpb[X2]])
                    for rr in range(2):
                        c = 2 * j + rr
                        cs = slice(c * 64, c * 64 + 64)
                        mm(pb[X2][:, cs], t_sbf[:, c, :], st[qn][:, cs], False, rr == 1,
                           r=[B["sbf"], B[qn]], w=[Bpb[X2]])
                A(st["sq"][:], pb[X2][:, :], AF.Square, r=[Bpb[X2]], w=[B["sq"]])
                mm(pb[X3][:, :], c_on128[:], st["sq"][:], True, True, r=[B["sq"], Bsetup], w=[Bpb[X3]])
                A(t_ln[:], pb[X3][:, :], AF.Ln, r=[Bpb[X3]], w=[B["lnt1"]], bias=RMS_EPS)
                A(t_ln[:], t_ln[:], AF.Exp, r=[B["lnt1"]], w=[B["lnt1"]], scale=-0.5)
                TTn(t_t1[:], pb[X2][:, :], t_ln[:], ALU.mult, r=[Bpb[X2], B["lnt1"]], w=[B["lnt1"]])
                STT(oaT[:, h, tt * TT:(tt + 1) * TT], t_t1[:], gain[:, h:h + 1], st["gs%d" % tt][:], ALU.mult,
                    ALU.mult, r=[B["lnt1"], B["gs%d" % tt], Bconst], w=[Boa[tt]])

            its = [(hp, tt) for hp in range(4) for tt in range(2)]
            Wp = {}
            Wp[0] = (w_get(), w_get(ahead=1))
            for si in range(2):
                W, BW = Wp[0][si]
                proj(streams[si], W, BW, 128, streams[si]["bk"][1], 0)
                proj(streams[si], W, BW, 0, streams[si]["bk"][0], 0)
            P.interleave([P.capture(lambda si=si: head(streams[si], si, 0)) for si in range(2)])
            for idx, (hp, tt) in enumerate(its):
                hs = (2 * hp, 2 * hp + 1)
                nxt = [None, None]
                if idx + 1 < len(its):
                    hp2, tt2 = its[idx + 1]
                    if hp2 not in Wp:
                        Wp[hp2] = (w_get(ahead=1), w_get(ahead=1))
                    nxt = [(Wp[hp2][si][0], Wp[hp2][si][1], tt2) for si in range(2)]
                P.interleave([P.capture(lambda si=si: mid(streams[si], hs[si], tt, Wp[hp][si][0], Wp[hp][si][1], nxt[si]))
                              for si in range(2)])
                lists = [P.capture(lambda si=si: tail(streams[si], hs[si], tt)) for si in range(2)]
                if idx + 1 < len(its):
                    hp2, tt2 = its[idx + 1]
                    hs2 = (2 * hp2, 2 * hp2 + 1)
                    lists += [P.capture(lambda si=si: head(streams[si], hs2[si], tt2)) for si in range(2)]
                P.interleave(lists)

        Bvt = [Buf('vt%d' % t) for t in range(NPASS)]

        def hgrn_state_phase(sa, xT, BxT, gsum, Bgsum, tile):
            vtok = sb("vtok", [128, 8, 1024], BF16, sa)
            Bv = [Buf("vtok%d" % t) for t in range(8)]
            for ui in range(2):
                W, BW = w_get()
                for tb in range(8):
                    bk = (ui * 8 + tb) % 8
                    c0 = tb * 128
                    for kc in range(8):
                        mm(pb[bk][:, :], xT[:, kc, c0:c0 + 128], W[:, kc, :], kc == 0, kc == 7,
                           r=[BW, BxT], w=[Bpb[bk]])
                    evac(vtok[:, tb, ui * 512:(ui + 1) * 512], pb[bk][:, :], r=[Bpb[bk]], w=[Bv[tb]])
            P.op("sp", lambda: sp.dma_start(out=vt_d[tile], in_=vtok[:].rearrange("p b c -> p (b c)")),
                 r=Bv, w=[Bvt[tile]], dma="vtst")
            c_one_f = sb("c_one_f", [128, 512], F32, sa)
            Bonef = Buf("onef")
            P.op("dve", lambda: dve.memset(c_one_f[:], 1.0), w=[Bonef])
            streams = []
            for si in range(2):
                st = {}
                for n in ("sf", "g", "b", "e", "k"):
                    st[n] = sb("x%d_%s" % (si, n), [128, 512], F32, sa)
                for n in ("kout", "koutT"):
                    st[n] = sb("x%d_%s" % (si, n), [128, 512], BF16, sa)
                st["dec"] = sb("x%d_dec" % si, [128, 2], F32, sa)
                st["B"] = {n: Buf("x%d_%s" % (si, n)) for n in ("sf", "g", "b", "e", "k", "kout", "koutT", "dec")}
                st["bk"] = [4 * si + i for i in range(4)]
                streams.append(st)

            def xproj(st, idx, h, tt, W, BW, col):
                bH = st["bk"][0] if idx % 2 == 0 else st["bk"][3]
                tok0 = tt * TT
                for kc in range(8):
                    mm(pb[bH][:, :], W[:, kc, col:col + 128], xT[:, kc, tok0:tok0 + TT],
                       kc == 0, kc == 7, r=[BW, BxT], w=[Bpb[bH]])

            def iteration(st, idx, h, tt, nxt):
                B = st["B"]
                bH = st["bk"][0] if idx % 2 == 0 else st["bk"][3]
                bT, bU = st["bk"][1], st["bk"][2]
                trv = pb[bT][:].bitcast(BF16)
                tb0 = tt * 4
                A(st["sf"][:], pb[bH][:, :], AF.Exp, r=[Bpb[bH]], w=[B["sf"]], scale=-1.0)
                if nxt is not None:
                    xproj(st, *nxt)
                A(st["g"][:], st["sf"][:], AF.Ln, r=[B["sf"], Bsetup], w=[B["g"]], scale=c_lbv[:, h:h + 1], bias=1.0)
                A(st["k"][:], st["sf"][:], AF.Ln, r=[B["sf"]], w=[B["k"]], bias=1.0)
                TTn(st["g"][:], st["g"][:], st["k"][:], ALU.subtract, r=[B["g"], B["k"]], w=[B["g"]])
                P.op("dve", lambda: dve.tensor_tensor_scan(out=st["b"][:], data0=c_one_f[:], data1=st["g"][:],
                                                           initial=0.0, op0=ALU.mult, op1=ALU.add),
                     r=[B["g"], Bonef], w=[B["b"]])
                TTn(st["e"][:], pb[bH][:, :], st["k"][:], ALU.add, r=[Bpb[bH], B["k"]], w=[B["e"]])
                TTn(st["e"][:], st["e"][:], st["b"][:], ALU.add, r=[B["e"], B["b"]], w=[B["e"]])
                TTn(st["dec"][:, 1:2], st["b"][:, 511:512], c_lnoml[:, h:h + 1], ALU.add, r=[B["b"], Bsetup],
                    w=[B["dec"]])
                A(st["dec"][:, 0:1], st["b"][:, 511:512], AF.Exp, r=[B["b"], B["dec"]], w=[B["dec"]])
                TTn(gsum[:, h:h + 1], gsum[:, h:h + 1], st["b"][:, 511:512], ALU.add, r=[B["b"], Bgsum], w=[Bgsum])
                A(st["kout"][:], st["e"][:], AF.Exp, r=[B["e"], B["dec"]], w=[B["kout"]], scale=-1.0,
                  bias=st["dec"][:, 1:2])
                for j in range(4):
                    P.op("pe", lambda j=j: pe.transpose(out=trv[:, j * 128:(j + 1) * 128],
                                                        in_=st["kout"][:, j * 128:(j + 1) * 128],
                                                        identity=c_id[:]),
                         r=[B["kout"], Bcid], w=[Bpb[bT]])
                P.op("dve", lambda: dve.tensor_copy(out=st["koutT"][:], in_=trv[:, 0:512]), r=[Bpb[bT]], w=[B["koutT"]])
                for j in range(4):
                    mm(pb[bU][:, 0:128], st["koutT"][:, j * 128:(j + 1) * 128],
                       vtok[:, tb0 + j, h * 128:(h + 1) * 128], j == 0, j == 3,
                       r=[B["koutT"], Bv[tb0 + j]], w=[Bpb[bU]])
                STT(s_carry[:, h, :], s_carry[:, h, :], st["dec"][:, 0:1], pb[bU][:, 0:128], ALU.mult, ALU.add,
                    r=[Bcarry[h], B["dec"], Bpb[bU]], w=[Bcarry[h]])

            its = [(hp, tt) for hp in range(4) for tt in range(2)]
            Wf = {}
            Wf[0] = w_get()
            for si in range(2):
                xproj(streams[si], 0, si, 0, Wf[0][0], Wf[0][1], (si % 4) * 128)
            for idx, (hp, tt) in enumerate(its):
                hs = (2 * hp, 2 * hp + 1)
                nxt = [None, None]
                if idx + 1 < len(its):
                    hp2, tt2 = its[idx + 1]
                    if hp2 // 2 not in Wf:
                        Wf[hp2 // 2] = w_get(ahead=1)
                    W2, BW2 = Wf[hp2 // 2]
                    nxt = [(idx + 1, 2 * hp2 + si, tt2, W2, BW2, ((2 * hp2 + si) % 4) * 128) for si in range(2)]
                lists = [P.capture(lambda si=si: iteration(streams[si], idx, hs[si], tt, nxt[si])) for si in range(2)]
                P.interleave(lists)
            fence = sb("x_fence", [128, 1], F32, sa)
            P.op("dve", lambda: dve.memset(fence[:], 0.0), r=[Bvt[tile]], w=[Buf("x_fence")])

        gsum = sb("gsum", [128, 8], F32)
        Bgsum = Buf("gsum")
        P.op("dve", lambda: dve.memset(gsum[:], 0.0), w=[Bgsum])
        with contextlib.ExitStack() as sxp:
            xTps = [sb("xTp%d" % t, [128, 8, T], BF16, sxp) for t in range(NPRE)]
            BxTps = [Buf("xTp%d" % t) for t in range(NPRE)]
            for t in range(NPRE):
                P.op("pool", lambda t=t: pool.dma_start(
                    out=xTps[t][:], in_=xT_d[t, :, HALO:HALO + T].rearrange("(k q) n -> q k n", q=128)),
                    w=[BxTps[t]], dma="xTp%d" % t)
                if t == 0:
                    w_ensure(1)
            for t in range(NPRE):
                with contextlib.ExitStack() as spre:
                    hgrn_state_phase(spre, xTps[t], BxTps[t], gsum, Bgsum, t)
                    P.barrier()
        with contextlib.ExitStack() as sx:
            pay = sb("pay", [128, 1032], F32, sx)
            Bpay, Bgat, Bpfx, Bsel, Bccin, Bccout = [Buf(n) for n in ("pay", "gat", "pfx", "sel", "ccin", "ccout")]
            P.op("dve", lambda: dve.tensor_copy(out=pay[:, 0:1024], in_=s_carry[:].rearrange("p h d -> p (h d)")),
                 r=Bcarry, w=[Bpay])
            A(pay[:, 1024:1032], gsum[:], AF.Exp, r=[Bgsum, Bpay], w=[Bpay])
            P.op("pool", lambda: pool.dma_start(out=cc_in[:, :], in_=pay[:]), r=[Bpay], w=[Bccin], dma="ccin")
            P.op("pool", lambda: pool.collective_compute("AllGather", ALU.bypass,
                                                         replica_groups=[[0, 1, 2, 3], [4, 5, 6, 7]],
                                                         ins=[cc_in.ap().opt()], outs=[cc_out.ap().opt()]),
                 r=[Bccin], w=[Bccout], dma="cc", amt=1)
            P.barrier()

        def phase_0():
            with contextlib.ExitStack() as s0:
                memT = sb("memT", [128, 8, 256], BF16, s0)
                Bmem = Buf("memT")
                P.op("pool", lambda: pool.dma_start(out=memT[:], in_=memT_d.rearrange("(k p) m -> p k m", p=128)),
                     w=[Bmem], dma="memT")
                for u in range(4):
                    W, BW = w_get()
                    if u < 2:
                        for ci in range(4):
                            ch = u * 4 + ci
                            bk = ch % 8
                            for kc in range(8):
                                mm(pb[bk][:, 0:256], W[:, kc, ci * 128:(ci + 1) * 128], memT[:, kc, :], kc == 0, kc == 7,
                                   r=[BW, Bmem], w=[Bpb[bk]])
                            evac(mkT[:, ch, :], pb[bk][:, 0:256], r=[Bpb[bk]], w=[Bmk])
                    else:
                        for mb in range(2):
                            bk = (u * 2 + mb) % 8
                            for kc in range(8):
                                mm(pb[bk][:, :], memT[:, kc, mb * 128:(mb + 1) * 128], W[:, kc, :], kc == 0, kc == 7,
                                   r=[BW, Bmem], w=[Bpb[bk]])
                            evac(mvb[:, mb, (u - 2) * 512:(u - 1) * 512], pb[bk][:, :], r=[Bpb[bk]], w=[Bmv])
            P.barrier()

        def combine_states():
            with contextlib.ExitStack() as sx2:
                gat = sb("gat", [128, 4, 1032], F32, sx2)
                pfx = sb("pfx", [128, 2, 1024], F32, sx2)
                c_sel = sb("c_sel", [128, 4], F32, sx2)
                P.op("sp", lambda: sp.dma_start(out=c_sel[:], in_=sel_d[:, :]), w=[Bsel], dma="const2")
                P.op("pool", lambda: pool.dma_start(out=gat[:], in_=cc_out.ap().rearrange("(r p) c -> p r c", p=128)),
                     r=[Bccout], w=[Bgat], dma="ccback")
                for h in range(8):
                    hs = slice(h * 128, (h + 1) * 128)
                    STT(pfx[:, 0, hs], gat[:, 0, hs], gat[:, 1, 1024 + h:1025 + h], gat[:, 1, hs], ALU.mult, ALU.add,
                        r=[Bgat], w=[Bpfx])
                for h in range(8):
                    hs = slice(h * 128, (h + 1) * 128)
                    STT(pfx[:, 1, hs], pfx[:, 0, hs], gat[:, 2, 1024 + h:1025 + h], gat[:, 2, hs], ALU.mult, ALU.add,
                        r=[Bgat, Bpfx], w=[Bpfx])
                sc = s_carry[:].rearrange("p h d -> p (h d)")
                TS(sc, gat[:, 0, 0:1024], c_sel[:, 1:2], None, ALU.mult, None, r=[Bgat, Bsel], w=Bcarry)
                STT(sc, pfx[:, 0, :], c_sel[:, 2:3], sc, ALU.mult, ALU.add, r=[Bpfx, Bsel] + Bcarry, w=Bcarry)
                STT(sc, pfx[:, 1, :], c_sel[:, 3:4], sc, ALU.mult, ALU.add, r=[Bpfx, Bsel] + Bcarry, w=Bcarry)
                P.barrier()

        def run_pass(p):
            p_cur["p"] = p
            with contextlib.ExitStack() as sp_pass:
                merged = sb("merged", [128, 8, T], BF16, sp_pass)
                Bmerged = [Buf("merged%d" % t) for t in range(2)]
                with contextlib.ExitStack() as s1:
                    xT = sb("xT", [128, 8, HALO + T], BF16, s1)
                    oaT = sb("oaT", [128, 8, T], BF16, s1)
                    obT = sb("obT", [128, 8, T], BF16, s1)
                    ocT = sb("ocT", [128, 8, T], BF16, s1)
                    BxT = Buf("xT")
                    Boa = [Buf("oa%d" % t) for t in range(2)]
                    Bob = [Buf("ob%d" % t) for t in range(2)]
                    Boc = [Buf("oc%d" % t) for t in range(2)]
                    P.op("pool", lambda xT=xT, p=p: pool.dma_start(
                        out=xT[:], in_=xT_d[p].rearrange("(k q) t -> q k t", q=128)), w=[BxT], dma="xT")

                    def phase_a():
                        with contextlib.ExitStack() as sa:
                            hgrn_phase(sa, xT, BxT, HALO, True, oaT, Boa)
                            P.barrier()
                        if p == 0:
                            dbg_dump("oaT", oaT[:], [128, 8, T], BF16, Boa)

                    def phase_b():
                        with contextlib.ExitStack() as sbk:
                            biasT = sb("biasT", [128, 4096], F32, sbk)
                            esrow = [sb("esrow%d" % g, [128, 512], BF16, sbk) for g in range(2)]
                            Bbias = Buf("biasT")
                            Bes = Buf("esrow")
                            P.op("sp", lambda biasT=biasT: sp.dma_start(out=biasT[:], in_=bias_d[:, :]), w=[Bbias],
                                 dma="bias")
                            with contextlib.ExitStack() as ssk:
                                sinkt = sb("sinkt", [128, 2048], F32, ssk)
                                s_hi = sb("s_hi", [128, 2048], BF16, ssk)
                                s_lo = sb("s_lo", [128, 2048], BF16, ssk)
                                Bsink = Buf("sinkt")
                                P.op("sp", lambda sinkt=sinkt: sp.dma_start(out=sinkt[:], in_=sink_d[:, :]), w=[Bsink],
                                     dma="sink")
                                A(sinkt[:], sinkt[:], AF.Exp, r=[Bsink], w=[Bsink])
                                P.op("dve", lambda s_hi=s_hi, sinkt=sinkt: dve.tensor_copy(out=s_hi[:], in_=sinkt[:]),
                                     r=[Bsink], w=[Bes])
                                TTn(sinkt[:], sinkt[:], s_hi[:], ALU.subtract, r=[Bsink, Bes], w=[Bsink])
                                P.op("dve", lambda s_lo=s_lo, sinkt=sinkt: dve.tensor_copy(out=s_lo[:], in_=sinkt[:]),
                                     r=[Bsink], w=[Bes])
                                for g in range(2):
                                    P.op("dve", lambda g=g: dve.memset(esrow[g][:], 0.0), w=[Bes])
                                    for rr in range(2):
                                        c0 = (g * 2 + rr) * 512
                                        for src, prow in ((s_hi, 64 * rr), (s_lo, 64 * rr + 32)):
                                            P.op("dve", lambda g=g, src=src, prow=prow, c0=c0: dve.tensor_copy(
                                                out=esrow[g][prow:prow + 1, :], in_=src[prow:prow + 1, c0:c0 + 512]),
                                                r=[Bes], w=[Bes])
                                P.barrier()
                            KT = [sb("KT%d" % g, [128, HALO + T], BF16, sbk) for g in range(2)]
                            Vz = [[sb("Vz%d%d" % (g, rr), [128, 9, 128], BF16, sbk) for rr in range(2)] for g in range(2)]
                            BKT = [Buf("KT%d" % g) for g in range(2)]
                            BV2 = [Buf("V2%d" % g) for g in range(2)]
                            QT = sb("QT", [128, 4, T], BF16, sbk)
                            BQT = Buf("QT")
                            t_sb = [sb("b_sb%d" % i, [128, 2048], F32, sbk) for i in range(2)]
                            t_pt = [sb("b_pt%d" % i, [128, 2048], BF16, sbk) for i in range(2)]
                            t_l = [sb("b_l%d" % i, [128, 512], F32, sbk) for i in range(2)]
                            Bsb = [Buf("b_sb%d" % i) for i in range(2)]
                            Bpt = [Buf("b_pt%d" % i) for i in range(2)]
                            Bl = [Buf("b_l%d" % i) for i in range(2)]
                            for g in range(2):
                                for rr in range(2):
                                    P.op("dve", lambda g=g, rr=rr: dve.memset(Vz[g][rr][:], 0.0), w=[BV2[g]])
                            W, BW = w_get()
                            nb = 0
                            for g in range(2):
                                for (c0, c1) in ((0, 512), (512, 1024), (1024, 1152)):
                                    bk = nb % 8
                                    nb += 1
                                    for kc in range(8):
                                        mm(pb[bk][:, 0:c1 - c0], W[:, kc, g * 128:(g + 1) * 128], xT[:, kc, c0:c1],
                                           kc == 0, kc == 7, r=[BW, BxT], w=[Bpb[bk]])
                                    evac(KT[g][:, c0:c1], pb[bk][:, 0:c1 - c0], r=[Bpb[bk]], w=[BKT[g]])
                                for b0 in (0, 4, 8):
                                    nblk = min(4, 9 - b0)
                                    bk = nb % 8
                                    nb += 1
                                    for bi in range(nblk):
                                        blk = b0 + bi
                                        for kc in range(8):
                                            mm(pb[bk][:, bi * 128:(bi + 1) * 128], xT[:, kc, blk * 128:(blk + 1) * 128],
                                               W[:, kc, 256 + g * 128:256 + (g + 1) * 128], kc == 0, kc == 7,
                                               r=[BW, BxT], w=[Bpb[bk]])
                                    pv = pb[bk][:, 0:nblk * 128].rearrange("p (b d) -> p b d", d=128)
                                    for rr in range(2):
                                        evac(Vz[g][rr][:, b0:b0 + nblk, rr * 64:rr * 64 + 64], pv[:, :, rr * 64:rr * 64 + 64],
                                             r=[Bpb[bk]], w=[BV2[g]])
                            it = 0
                            for g in range(2):
                                W, BW = w_get()
                                for i in range(4):
                                    for tt in range(2):
                                        bk = nb % 8
                                        nb += 1
                                        for kc in range(8):
                                            mm(pb[bk][:, :], W[:, kc, i * 128:(i + 1) * 128],
                                               xT[:, kc, HALO + tt * TT:HALO + (tt + 1) * TT], kc == 0, kc == 7,
                                               r=[BW, BxT], w=[Bpb[bk]])
                                        evac(QT[:, i, tt * TT:(tt + 1) * TT], pb[bk][:, :], r=[Bpb[bk]], w=[BQT])

                                def stage1a(n, par, g=g):
                                    for pc in range(2):
                                        for rr in range(2):
                                            bk = pc * 2 + rr
                                            rows = slice(rr * 64, rr * 64 + 64)
                                            for i in range(4):
                                                mm(pb[bk][:, i * 128:(i + 1) * 128],
                                                   KT[g][rows, (n + pc) * 128:(n + pc + 1) * 128],
                                                   QT[rows, i, n * 128:(n + 1) * 128], True, True,
                                                   r=[BKT[g], BQT], w=[Bpb[bk]])

                                def stage1b(n, par, g=g):
                                    for pc in range(2):
                                        for rr in range(2):
                                            bk = pc * 2 + rr
                                            o0 = ((g * 2 + pc) * 2 + rr) * 512
                                            STT(t_sb[par][:, bk * 512:(bk + 1) * 512], pb[bk][:, :], 0.125,
                                                biasT[:, o0:o0 + 512], ALU.mult, ALU.add,
                                                r=[Bpb[bk], Bbias], w=[Bsb[par]])
                                    if p == 0 and n == 0:
                                        TS(t_sb[par][:, 0:1024], t_sb[par][:, 0:1024], c_hm[:, 0:1], None, ALU.add, None,
                                           r=[Bsb[par], Bconst], w=[Bsb[par]])
                                    A(t_pt[par][:], t_sb[par][:], AF.Exp, r=[Bsb[par]], w=[Bpt[par]])

                                def stage2(n, par, g=g):
                                    bo, bd = 4 + par * 2, 5 + par * 2
                                    k = 0
                                    for pc in range(2):
                                        for rr in range(2):
                                            bk = pc * 2 + rr
                                            mm(pb[bo][:, :], Vz[g][rr][:, n + pc, :], t_pt[par][:, bk * 512:(bk + 1) * 512],
                                               k == 0, k == 3, r=[BV2[g], Bpt[par]], w=[Bpb[bo]])
                                            k += 1
                                    k = 0
                                    for pc in range(2):
                                        for rr in range(2):
                                            bk = pc * 2 + rr
                                            mm(pb[bd][:, :], c_cb[:, 128 + rr * 128:256 + rr * 128],
                                               t_pt[par][:, bk * 512:(bk + 1) * 512], k == 0, False,
                                               r=[Bsetup, Bpt[par]], w=[Bpb[bd]])
                                            k += 1
                                    mm(pb[bd][:, :], c_cb[:, 0:128], esrow[g][:], False, True, r=[Bsetup, Bes], w=[Bpb[bd]])
                                    A(t_l[par][:], pb[bd][:, :], AF.Ln, r=[Bpb[bd]], w=[Bl[par]])
                                    A(t_l[par][:], t_l[par][:], AF.Exp, r=[Bl[par]], w=[Bl[par]], scale=-1.0)
                                    TTn(obT[:, 4 * g:4 * g + 4, n * 128:(n + 1) * 128],
                                        pb[bo][:, :].rearrange("p (i q) -> p i q", q=128),
                                        t_l[par][:].rearrange("p (i q) -> p i q", q=128), ALU.mult,
                                        r=[Bpb[bo], Bl[par]], w=[Bob[n // 4]])

                                prev = None
                                for n in range(8):
                                    par = it % 2
                                    it += 1
                                    stage1a(n, par)
                                    l1 = P.capture(lambda: stage1b(n, par))
                                    if prev is None:
                                        P.interleave([l1])
                                    else:
                                        l2 = P.capture(lambda: stage2(*prev))
                                        P.interleave([l1, l2])
                                    prev = (n, par)
                                P.interleave([P.capture(lambda: stage2(*prev))])
                            P.barrier()
                        if p == 0:
                            dbg_dump("obT", obT[:], [128, 8, T], BF16, Bob)

                    def phase_c():
                        with contextlib.ExitStack() as sc:
                            t_mq = [sb("c_mq%d" % i, [128, 2, 512], BF16, sc) for i in range(2)]
                            t_pm = [sb("c_pm%d" % i, [128, 2, 512], BF16, sc) for i in range(2)]
                            t_rc = [sb("c_rc%d" % i, [128, 512], F32, sc) for i in range(2)]
                            Bmq = [Buf("c_mq%d" % i) for i in range(2)]
                            Bpm = [Buf("c_pm%d" % i) for i in range(2)]
                            Brc = [Buf("c_rc%d" % i) for i in range(2)]

                            def cstage1(mh, tt, par, W, BW):
                                tok0 = HALO + tt * TT
                                for dc in range(2):
                                    bk = dc
                                    for kc in range(8):
                                        mm(pb[bk][:, :], W[:, kc, dc * 128:(dc + 1) * 128], xT[:, kc, tok0:tok0 + TT],
                                           kc == 0, kc == 7, r=[BW, BxT], w=[Bpb[bk]])
                                    evac(t_mq[par][:, dc, :], pb[bk][:, :], r=[Bpb[bk]], w=[Bmq[par]])
                                for mb in range(2):
                                    bk = 2 + mb
                                    for dc in range(2):
                                        mm(pb[bk][:, :], mkT[:, mh * 2 + dc, mb * 128:(mb + 1) * 128],
                                           t_mq[par][:, dc, :], dc == 0, dc == 1, r=[Bmk, Bmq[par]], w=[Bpb[bk]])
                                    A(t_pm[par][:, mb, :], pb[bk][:, :], AF.Exp, r=[Bpb[bk]], w=[Bpm[par]],
                                      scale=1.0 / 16.0)

                            def cstage2(mh, tt, par, W, BW):
                                for mb in range(2):
                                    mm(pb[4][:, :], c_one[:], t_pm[par][:, mb, :], mb == 0, mb == 1,
                                       r=[Bsetup, Bpm[par]], w=[Bpb[4]])
                                for vc in range(2):
                                    bk = 5 + vc
                                    for mb in range(2):
                                        mm(pb[bk][:, :], mvb[:, mb, mh * 256 + vc * 128:mh * 256 + (vc + 1) * 128],
                                           t_pm[par][:, mb, :], mb == 0, mb == 1, r=[Bmv, Bpm[par]], w=[Bpb[bk]])
                                A(t_rc[par][:], pb[4][:, :], AF.Ln, r=[Bpb[4]], w=[Brc[par]])
                                A(t_rc[par][:], t_rc[par][:], AF.Exp, r=[Brc[par]], w=[Brc[par]], scale=-1.0)
                                for vc in range(2):
                                    TTn(ocT[:, mh * 2 + vc, tt * TT:(tt + 1) * TT], pb[5 + vc][:, :], t_rc[par][:],
                                        ALU.mult, r=[Bpb[5 + vc], Brc[par]], w=[Boc[tt]])

                            it = 0
                            prev = None
                            for mh in range(4):
                                W, BW = w_get()
                                for tt in range(2):
                                    par = it % 2
                                    it += 1
                                    cur = (mh, tt, par, W, BW)
                                    l1 = P.capture(lambda: cstage1(*cur))
                                    if prev is None:
                                        P.interleave([l1])
                                    else:
                                        P.interleave([l1, P.capture(lambda: cstage2(*prev))])
                                    prev = cur
                            P.interleave([P.capture(lambda: cstage2(*prev))])
                            P.barrier()
                        if p == 0:
                            dbg_dump("ocT", ocT[:], [128, 8, T], BF16, Boc)

                    if p == 0:
                        phase_b()
                        phase_0()
                        phase_c()
                        combine_states()
                        phase_a()
                    else:
                        phase_a()
                        phase_b()
                        phase_c()

                    with contextlib.ExitStack() as sd:
                        t_sg = [sb("d_sg%d" % i, [128, 512], F32, sd) for i in range(4)]
                        t_m = [sb("d_m%d" % i, [128, 512], F32, sd) for i in range(2)]
                        t_tmp = [sb("d_tmp%d" % i, [128, 512], F32, sd) for i in range(4)]
                        Bsg = [Buf("d_sg%d" % i) for i in range(4)]
                        Bm = [Buf("d_m%d" % i) for i in range(2)]
                        Btmp = [Buf("d_tmp%d" % i) for i in range(4)]
                        srcs = ((oaT, Boa), (obT, Bob), (ocT, Boc))
                        it = 0
                        jt = 0
                        for cc in range(8):
                            WG, BWG = w_get()
                            WB, BWB = w_get(ahead=1)
                            for tt in range(2):
                                mp = jt % 2
                                jt += 1
                                tok0 = HALO + tt * TT
                                for b in range(3):
                                    par = it % 4
                                    it += 1
                                    bg, by = par * 2, par * 2 + 1
                                    for kc in range(8):
                                        mm(pb[bg][:, :], WG[:, kc, b * 128:(b + 1) * 128], xT[:, kc, tok0:tok0 + TT],
                                           kc == 0, kc == 7, r=[BWG, BxT], w=[Bpb[bg]])
                                    src, Bsrc = srcs[b]
                                    for kc in range(8):
                                        mm(pb[by][:, :], WB[:, kc, b * 128:(b + 1) * 128],
                                           src[:, kc, tt * TT:(tt + 1) * TT], kc == 0, kc == 7,
                                           r=[BWB, Bsrc[tt]], w=[Bpb[by]])
                                    A(t_sg[par][:], pb[bg][:, :], AF.Sigmoid, r=[Bpb[bg]], w=[Bsg[par]])
                                    if b == 0:
                                        TTn(t_m[mp][:], pb[by][:, :], t_sg[par][:], ALU.mult,
                                            r=[Bpb[by], Bsg[par]], w=[Bm[mp]])
                                    else:
                                        TTn(t_tmp[par][:], pb[by][:, :], t_sg[par][:], ALU.mult,
                                            r=[Bpb[by], Bsg[par]], w=[Btmp[par]])
                                        if b == 1:
                                            TTn(t_m[mp][:], t_m[mp][:], t_tmp[par][:], ALU.add,
                                                r=[Bm[mp], Btmp[par]], w=[Bm[mp]])
                                        else:
                                            TTn(merged[:, cc, tt * TT:(tt + 1) * TT], t_m[mp][:], t_tmp[par][:], ALU.add,
                                                r=[Bm[mp], Btmp[par]], w=[Bmerged[tt]])
                        P.barrier()
                if p == 0:
                    dbg_dump("merged", merged[:], [128, 8, T], BF16, Bmerged)

                with contextlib.ExitStack() as s2:
                    h1T = sb("h1T", [128, 8, T], F32, s2)
                    h1b = sb("h1b", [128, 8, T], BF16, s2)
                    Bh1 = [Buf("h1T%d" % t) for t in range(2)]
                    Bh1b = [Buf("h1b%d" % t) for t in range(2)]
                    xres = [sb("xres%d" % i, [128, 512], F32, s2) for i in range(2)]
                    Bxres = [Buf("xres%d" % i) for i in range(2)]
                    l_sq = [sb("l_sq%d" % i, [128, 512], BF16, s2) for i in range(2)]
                    l_hb = [sb("l_hb%d" % i, [128, 512], BF16, s2) for i in range(2)]
                    Blsq = [Buf("l_sq%d" % i) for i in range(2)]
                    Blhb = [Buf("l_hb%d" % i) for i in range(2)]
                    l_mean = sb("l_mean", [128, 512], F32, s2)
                    l_var = sb("l_var", [128, 512], F32, s2)
                    l_A = sb("l_A", [128, 512], F32, s2)
                    l_Bm = sb("l_Bm", [128, 512], F32, s2)
                    l_t = [sb("l_t%d" % i, [128, 512], F32, s2) for i in range(2)]
                    Bl = {n: Buf("l_" + n) for n in ("mean", "var", "A", "Bm")}
                    Blt = [Buf("l_t%d" % i) for i in range(2)]
                    lnc = {"i": 0, "o": 0}

                    def layernorm(src, Bsrc, tt, goff, boff, final):
                        ts = slice(tt * TT, (tt + 1) * TT)
                        bm, bq = 6, 7
                        for cc in range(8):
                            par = lnc["i"] % 2
                            lnc["i"] += 1
                            A(l_sq[par][:], src[:, cc, ts], AF.Square, r=[Bsrc], w=[Blsq[par]])
                            P.op("dve", lambda par=par, cc=cc: dve.tensor_copy(out=l_hb[par][:], in_=src[:, cc, ts]),
                                 r=[Bsrc], w=[Blhb[par]])
                            mm(pb[bm][:, :], c_on1024[:], l_hb[par][:], cc == 0, cc == 7, r=[Bsetup, Blhb[par]],
                               w=[Bpb[bm]])
                            mm(pb[bq][:, :], c_on1024[:], l_sq[par][:], cc == 0, cc == 7, r=[Bsetup, Blsq[par]],
                               w=[Bpb[bq]])
                        P.op("dve", lambda: dve.tensor_copy(out=l_mean[:], in_=pb[bm][:, :]), r=[Bpb[bm]],
                             w=[Bl["mean"]])
                        TTn(l_var[:], l_mean[:], l_mean[:], ALU.mult, r=[Bl["mean"]], w=[Bl["var"]])
                        TTn(l_var[:], pb[bq][:, :], l_var[:], ALU.subtract, r=[Bpb[bq], Bl["var"]], w=[Bl["var"]])
                        A(l_var[:], l_var[:], AF.Ln, r=[Bl["var"]], w=[Bl["var"]], bias=LN_EPS)
                        A(l_A[:], l_var[:], AF.Exp, r=[Bl["var"]], w=[Bl["A"]], scale=-0.5)
                        STT(l_Bm[:], l_mean[:], -1.0, l_A[:], ALU.mult, ALU.mult, r=[Bl["mean"], Bl["A"]], w=[Bl["Bm"]])
                        for cc in range(8):
                            par = lnc["i"] % 2
                            lnc["i"] += 1
                            TTn(l_t[par][:], src[:, cc, ts], l_A[:], ALU.mult, r=[Bsrc, Bl["A"]], w=[Blt[par]])
                            TTn(l_t[par][:], l_t[par][:], l_Bm[:], ALU.add, r=[Blt[par], Bl["Bm"]], w=[Blt[par]])
                            gsc = c_vec[:, goff + cc:goff + cc + 1]
                            bsc = c_vec[:, boff + cc:boff + cc + 1]
                            if not final:
                                A(src[:, cc, ts], l_t[par][:], AF.Identity, r=[Blt[par], Bconst], w=[Bsrc],
                                  scale=gsc, bias=bsc)
                                A(h1b[:, cc, ts], l_t[par][:], AF.Identity, r=[Blt[par], Bconst], w=[Bh1b[tt]],
                                  scale=gsc, bias=bsc)
                            else:
                                so = lnc["o"] % 2
                                lnc["o"] += 1
                                A(ostg[so][:], l_t[par][:], AF.Identity, r=[Blt[par], Bconst], w=[Bostg[so]],
                                  scale=gsc, bias=bsc)
                                dst = outT_d[cc * 128:(cc + 1) * 128, p * T + tt * TT:p * T + (tt + 1) * TT]
                                P.op("sp", lambda so=so, dst=dst: sp.dma_start(out=dst, in_=ostg[so][:]),
                                     r=[Bostg[so]], w=[Buf("o")], dma="out%d" % so)

                    WO = [w_get(), w_get(ahead=1)]
                    xi = {"i": 0}

                    def d2_tile(tt):
                        for cc in range(8):
                            W, BW = WO[cc // 4]
                            ci = cc % 4
                            bk = cc % 6
                            xp = xi["i"] % 2
                            xi["i"] += 1
                            srcx = xT_d[p, cc * 128:(cc + 1) * 128, HALO + tt * TT:HALO + (tt + 1) * TT]
                            P.op("sp", lambda xp=xp, srcx=srcx: sp.dma_start(out=xres[xp][:], in_=srcx),
                                 w=[Bxres[xp]], dma="xres%d" % xp)
                            for kc in range(8):
                                mm(pb[bk][:, :], W[:, kc, ci * 128:(ci + 1) * 128],
                                   merged[:, kc, tt * TT:(tt + 1) * TT], kc == 0, kc == 7,
                                   r=[BW, Bmerged[tt]], w=[Bpb[bk]])
                            STT(h1T[:, cc, tt * TT:(tt + 1) * TT], xres[xp][:], ALPHA, pb[bk][:, :], ALU.mult,
                                ALU.add, r=[Bxres[xp], Bpb[bk]], w=[Bh1[tt]])

                    d2_tile(0)
                    P.interleave([P.capture(lambda: d2_tile(1)),
                                  P.capture(lambda: layernorm(h1T, Bh1[0], 0, 8, 16, False))])
                    layernorm(h1T, Bh1[1], 1, 8, 16, False)
                    if p == 0:
                        dbg_dump("h1T", h1T[:], [128, 8, T], F32, Bh1)

                    with contextlib.ExitStack() as se:
                        aT = sb("aT", [128, 32, T], BF16, se)
                        BaT = [Buf("aT%d" % t) for t in range(2)]
                        t_r = [sb("e_r%d" % i, [128, 512], F32, se) for i in range(2)]
                        Br = [Buf("e_r%d" % i) for i in range(2)]
                        it = 0
                        for u in range(8):
                            W, BW = w_get()
                            for fi in range(4):
                                fc = 4 * u + fi
                                for tt in range(2):
                                    par = it % 2
                                    bk = it % 6
                                    it += 1
                                    for kc in range(8):
                                        mm(pb[bk][:, :], W[:, kc, fi * 128:(fi + 1) * 128],
                                           h1b[:, kc, tt * TT:(tt + 1) * TT], kc == 0, kc == 7,
                                           r=[BW, Bh1b[tt]], w=[Bpb[bk]])
                                    A(t_r[par][:], pb[bk][:, :], AF.Relu, r=[Bpb[bk]], w=[Br[par]])
                                    TTn(aT[:, fc, tt * TT:(tt + 1) * TT], t_r[par][:], t_r[par][:], ALU.mult,
                                        r=[Br[par]], w=[BaT[tt]])
                        def down_tile(tt):
                            for cc in range(8):
                                W, BW = w_get(kc=32)
                                bk = cc % 6
                                for fc in range(32):
                                    mm(pb[bk][:, :], W[:, fc, :], aT[:, fc, tt * TT:(tt + 1) * TT], fc == 0, fc == 31,
                                       r=[BW, BaT[tt]], w=[Bpb[bk]])
                                STT(h1T[:, cc, tt * TT:(tt + 1) * TT], h1T[:, cc, tt * TT:(tt + 1) * TT], ALPHA,
                                    pb[bk][:, :], ALU.mult, ALU.add, r=[Bh1[tt], Bpb[bk]], w=[Bh1[tt]])

                        down_tile(0)
                        l_ln = P.capture(lambda: layernorm(h1T, Bh1[0], 0, 24, 32, True))
                        P.spread = l_ln
                        down_tile(1)
                        P.flush_spread()
                        layernorm(h1T, Bh1[1], 1, 24, 32, True)
                        P.barrier()
        for p in range(NPASS):
            run_pass(p)
        P.emit()
        for k in list(P.sems):
            pass
        cnt = {}
        for o2 in P.ops:
            if o2.grp and (o2.grp.startswith("out")):
                cnt[o2.grp] = o2.val
        for g, v in cnt.items():
            sp.wait_ge(P.sem("d_" + g), v)
    return nc, dbg_d


def _prep_inputs(x, mem, w_in, lb_logits, hg_norm_gain, swa_sinks, rel_bias, w_mem_kv, w_branch_hg, w_branch_swa,
                 w_branch_mem, w_out, ln1_g, ln1_b, w_up, w_down, ln2_g, ln2_b):
    f = lambda a: np.asarray(a, dtype=np.float32)
    x, mem = f(x), f(mem)
    wall, _, _ = _build_units(f(w_in)[0], f(w_mem_kv)[0], f(w_branch_hg)[0], f(w_branch_swa)[0], f(w_branch_mem)[0],
                              f(w_out)[0], f(w_up)[0], f(w_down)[0])
    lbT = np.ascontiguousarray(f(lb_logits).reshape(2, 8, 128).transpose(2, 0, 1).reshape(128, 16))
    v = lambda a: f(a).reshape(8, 128).T
    vecs = np.ascontiguousarray(np.concatenate([v(hg_norm_gain), v(ln1_g), v(ln1_b), v(ln2_g), v(ln2_b)], axis=1))
    biasT = _bias_table(f(rel_bias))
    sinkT = _sink_table(f(swa_sinks)[0])
    masks = _masks()
    ident = np.eye(128, dtype=np.float32)
    in_maps = []
    for c in range(NCORES):
        b, j = c // 4, c % 4
        t0 = j * TOK_CORE
        xt = np.zeros((NPASS, D, HALO + T), np.float32)
        for p in range(NPASS):
            s = t0 + p * T - HALO
            if s < 0:
                xt[p, :, HALO:] = x[b, 0:T, :].T
            else:
                xt[p] = x[b, s:s + HALO + T, :].T
        sel = np.zeros((128, 4), np.float32)
        sel[:, j] = 1.0
        hm = np.full((128, 1), NEG if j == 0 else 0.0, np.float32)
        in_maps.append({"xT": xt, "sel": sel, "memT": np.ascontiguousarray(mem[b].T), "wall": wall, "lbT": lbT, "vecs": vecs,
                        "hmask": hm, "biasT": biasT, "sinkT": sinkT, "masks": masks, "ident": ident})
    return in_maps


def kernel(**inputs):
    in_maps = _prep_inputs(**inputs)
    nc, _ = build_program(False)
    res = run_bass_kernel_spmd(nc, in_maps, core_ids=list(range(NCORES)))
    out = np.empty((2, SEQ, D), np.float32)
    for c in range(NCORES):
        b, j = c // 4, c % 4
        out[b, j * TOK_CORE:(j + 1) * TOK_CORE, :] = res.results[c]["outT"].T
    return out
```

```python
import contextlib
import numpy as np
import concourse.bass as bass
import concourse.mybir as mybir
from concourse.bass_utils import run_bass_kernel_spmd

F32 = mybir.dt.float32
BF16 = mybir.dt.bfloat16
AF = mybir.ActivationFunctionType
ALU = mybir.AluOpType

NCORES = 8
D = 1024
SEQ = 8192
TOK_CORE = 2048
NPASS = 2
T = 1024
HALO = 128
TT = 512
ALPHA = 2.0 ** 0.25
LN_EPS = 1e-5
RMS_EPS = 1e-6
NEG = -30000.0
NSLOT = 3
NPRE = 2


class Buf:
    __slots__ = ("name", "w", "rs")

    def __init__(self, name):
        self.name = name
        self.w = None
        self.rs = {}


class Op:
    __slots__ = ("eng", "fn", "deps", "idx", "inc", "val", "grp", "amt")


class Prog:
    def __init__(self, nc, es):
        self.nc = nc
        self.es = es
        self.ops = []
        self.last = {}
        self.pending = {}
        self.engs = {"pe": nc.tensor, "act": nc.scalar, "dve": nc.vector, "pool": nc.gpsimd, "sp": nc.sync}
        self.sems = {}
        self.cap = None
        self.spread = None
        self.in_spread = False
        self.spread_cnt = 0
        self.spread_every = 4

    def sem(self, key):
        if key not in self.sems:
            self.sems[key] = self.es.enter_context(self.nc.semaphore("s_" + key))
        return self.sems[key]

    def capture(self, body):
        self.cap = []
        body()
        ops, self.cap = self.cap, None
        return ops

    def interleave(self, lists):
        its = [list(l) for l in lists]
        n = max(len(l) for l in its)
        for i in range(n):
            for l in its:
                if i < len(l):
                    self.op(*l[i])

    def flush_spread(self):
        l, self.spread = self.spread, None
        for a in l or ():
            self.op(*a)

    def op(self, eng, fn, r=(), w=(), dma=None, amt=16):
        if self.cap is not None:
            self.cap.append((eng, fn, tuple(r), tuple(w), dma, amt))
            return None
        if self.spread and not self.in_spread:
            self.in_spread = True
            self.spread_cnt += 1
            if self.spread_cnt % self.spread_every == 0:
                self.op(*self.spread.pop(0))
            self.in_spread = False
        o = Op()
        o.amt = amt
        o.eng = eng
        o.fn = fn
        o.idx = len(self.ops)
        o.inc = False
        o.val = 0
        o.grp = dma
        key = ("dma", dma) if dma else eng
        deps = {}

        def add(d, raw):
            if d is None or d is o:
                return
            k = ("dma", d.grp) if d.grp else d.eng
            cur = deps.setdefault(k, [None, None])
            if cur[0] is None or cur[0].idx < d.idx:
                cur[0] = d
            if raw and (cur[1] is None or cur[1].idx < d.idx):
                cur[1] = d

        for b in r:
            add(b.w, True)
        for b in w:
            add(b.w, False)
            for d in b.rs.values():
                add(d, False)
        for d in self.pending.pop(eng, ()):
            add(d, True)
        o.deps = deps
        for b in w:
            b.w = o
            b.rs = {}
        for b in r:
            if b.w is not o:
                b.rs[key] = o
        self.ops.append(o)
        if not dma:
            self.last[eng] = o
        return o

    def barrier(self):
        lasts = [v for k, v in self.last.items()]
        for e in self.engs:
            self.pending[e] = list(lasts)

    def _semdeps(self, o):
        out = []
        for k, (dany, draw) in o.deps.items():
            if k[0] == "dma" if isinstance(k, tuple) else False:
                out.append(dany)
            elif o.grp is None and k == o.eng:
                if o.eng != "pe" and draw is not None:
                    out.append(draw)
            else:
                out.append(dany)
        return out

    def emit(self):
        for o in self.ops:
            for d in self._semdeps(o):
                if not d.grp:
                    d.inc = True
        cnt = {}
        for o in self.ops:
            if o.grp:
                cnt[("dma", o.grp)] = cnt.get(("dma", o.grp), 0) + o.amt
                o.val = cnt[("dma", o.grp)]
            elif o.inc:
                cnt[o.eng] = cnt.get(o.eng, 0) + 1
                o.val = cnt[o.eng]
        waited = {}
        nwait = 0
        for o in self.ops:
            q = self.engs[o.eng]
            for d in self._semdeps(o):
                sk = ("d_" + d.grp) if d.grp else ("e_" + d.eng)
                if waited.get((o.eng, sk), 0) >= d.val:
                    continue
                q.wait_ge(self.sem(sk), d.val)
                nwait += 1
                waited[(o.eng, sk)] = d.val
            inst = o.fn()
            if o.grp:
                inst.then_inc(self.sem("d_" + o.grp), o.amt)
            elif o.inc:
                inst.then_inc(self.sem("e_" + o.eng), 1)
        return nwait


def _t5_bucket_np(n):
    n = np.asarray(n, np.int32)
    nf = np.maximum(n, 1).astype(np.float32)
    large = 16 + (np.log(nf / np.float32(16)) / np.float32(np.log(128 / 16)) * np.float32(16)).astype(np.int32)
    large = np.minimum(large, 31)
    return np.where(n < 16, n, large)


def _tile_k(w, cols):
    K = w.shape[0]
    sub = w[:, cols]
    return sub.reshape(K // 128, 128, -1).transpose(1, 0, 2).reshape(128, -1)


def _build_units(w_in, w_mem_kv, w_bhg, w_bswa, w_bmem, w_out, w_up, w_down):
    parts = []
    off = [0]

    def add(a):
        a = np.ascontiguousarray(a, dtype=np.float32)
        u = (off[0], a.shape[1])
        parts.append(a)
        off[0] += a.shape[1]
        return u

    ar = np.arange
    once = []
    for u in range(4):
        once.append(add(_tile_k(w_mem_kv, ar(u * 512, (u + 1) * 512))))
    for hu in range(2):
        once.append(add(_tile_k(w_in, 1024 + ar(hu * 512, (hu + 1) * 512))))
    pu = []
    for i in range(2):
        pu.append(add(_tile_k(w_in, 2048 + ar(i * 512, (i + 1) * 512))))
    for h in range(8):
        cols = np.concatenate([h * 128 + ar(128), 1024 + h * 128 + ar(128), 3072 + h * 128 + ar(128)])
        pu.append(add(_tile_k(w_in, cols)))
    sk0 = 5120 + ar(64)
    sk1 = 5120 + 64 + ar(64)
    sv0 = 5248 + ar(64)
    sv1 = 5248 + 64 + ar(64)
    cols = np.concatenate([sk0, sk0, sk1, sk1, sv0, sv0, sv1, sv1])
    pu.append(add(_tile_k(w_in, cols)))
    for g in range(2):
        pu.append(add(_tile_k(w_in, 4096 + ar(g * 512, (g + 1) * 512))))
    for mh in range(4):
        pu.append(add(_tile_k(w_in, 5376 + ar(mh * 256, (mh + 1) * 256))))
    for cc in range(8):
        cols = np.concatenate([6400 + b * 1024 + cc * 128 + ar(128) for b in range(3)])
        pu.append(add(_tile_k(w_in, cols)))
        c2 = cc * 128 + ar(128)
        pu.append(add(np.concatenate([_tile_k(w_bhg, c2).reshape(128, 8, 128),
                                      _tile_k(w_bswa, c2).reshape(128, 8, 128),
                                      _tile_k(w_bmem, c2).reshape(128, 8, 128)], axis=2).reshape(128, -1)))
    for i in range(2):
        pu.append(add(_tile_k(w_out, ar(i * 512, (i + 1) * 512))))
    for u in range(8):
        pu.append(add(_tile_k(w_up, ar(u * 512, (u + 1) * 512))))
    for cc in range(8):
        pu.append(add(_tile_k(w_down, cc * 128 + ar(128))))
    wall = np.concatenate(parts, axis=1)
    return wall, once, pu


def _unit_sizes():
    once = [4096] * 6
    pu = [4096] * 2 + [3072] * 8 + [4096] + [4096] * 2 + [2048] * 4 + [3072, 3072] * 8 + [4096] * 2 + [4096] * 8 + [4096] * 8
    return once, pu


def _bias_table(rel_bias):
    k = np.arange(128)[:, None]
    q = np.arange(128)[None, :]
    dist_prev = q + 128 - k
    dist_cur = q - k
    tab = np.empty((128, 2, 2, 2, 4, 128), np.float32)
    for g in range(2):
        for r in range(2):
            for i in range(4):
                h = 8 * g + 2 * i + r
                for pc, dist in ((0, dist_prev), (1, dist_cur)):
                    valid = (dist >= 0) & (dist < 128)
                    bk = _t5_bucket_np(np.clip(dist, 0, 127))
                    tab[:, g, pc, r, i, :] = np.where(valid, rel_bias[bk, h], np.float32(NEG))
    return tab.reshape(128, 4096)


def _sink_table(sinks):
    tab = np.empty((128, 2, 2, 4, 128), np.float32)
    for g in range(2):
        for r in range(2):
            for i in range(4):
                tab[:, g, r, i, :] = sinks[8 * g + 2 * i + r]
    return tab.reshape(128, 2048)


def _masks():
    m = np.ones((128, 1024), np.float32)
    m[:, 0:512:64] = 0.0
    s = np.arange(128)[:, None]
    c = np.arange(128)[None, :]
    am = ((s // 64 == c // 64) & (c >= s)).astype(np.float32)
    m[:, 512:1024] = np.tile(am, (1, 4))
    sel = np.zeros((128, 128), np.float32)
    sel[0, 0:64] = 1.0
    sel[32, 0:64] = 1.0
    sel[64, 64:128] = 1.0
    sel[96, 64:128] = 1.0
    o0 = np.zeros((128, 128), np.float32)
    o0[:, 0:64] = 1.0
    o1 = np.zeros((128, 128), np.float32)
    o1[:, 64:128] = 1.0
    return np.concatenate([m, sel, o0, o1], axis=1)


def build_program(dbg=False):
    nc = bass.Bass("TRN2", target_bir_lowering=False)
    once_sz, pass_sz = _unit_sizes()
    tot = sum(once_sz) + sum(pass_sz)

    def din(name, shape, dt=F32):
        return nc.dram_tensor(name, shape, dt, kind="ExternalInput").ap()

    xT_d = din("xT", [NPASS, D, HALO + T])
    sel_d = din("sel", [128, 4])
    cc_in = nc.dram_tensor("cc_in", [128, 1032], F32)
    cc_out = nc.dram_tensor("cc_out", [512, 1032], F32)
    vt_d = nc.dram_tensor("vt_scratch", [NPASS, 128, 8 * 1024], BF16)
    memT_d = din("memT", [D, 256])
    wall_d = din("wall", [128, tot])
    lbT_d = din("lbT", [128, 16])
    vec_d = din("vecs", [128, 40])
    hm_d = din("hmask", [128, 1])
    bias_d = din("biasT", [128, 4096])
    sink_d = din("sinkT", [128, 2048])
    msk_d = din("masks", [128, 1408])
    id_d = din("ident", [128, 128])
    outT_d = nc.dram_tensor("outT", [D, TOK_CORE], F32, kind="ExternalOutput").ap()
    dbg_d = {}

    es = contextlib.ExitStack()
    with es:
        P = Prog(nc, es)

        uniq = {"n": 0}

        def sb(name, shape, dt, stack=es):
            uniq["n"] += 1
            return stack.enter_context(nc.sbuf_tensor("%s_%d" % (name, uniq["n"]), shape, dt))

        pb = [es.enter_context(nc.psum_tensor("pb%d" % i, [128, 512], F32)) for i in range(8)]
        Bpb = [Buf("pb%d" % i) for i in range(8)]
        slots = [sb("wslot%d" % i, [128, 4096], BF16) for i in range(NSLOT)]
        Bslot = [Buf("slot%d" % i) for i in range(NSLOT)]
        c_lb = sb("c_lb", [128, 16], F32)
        c_lbv = sb("c_lbv", [128, 8], F32)
        c_oml = sb("c_oml", [128, 8], F32)
        c_lnoml = sb("c_lnoml", [128, 8], F32)
        c_vec = sb("c_vec", [128, 40], F32)
        c_hm = sb("c_hm", [128, 1], F32)
        c_msk = sb("c_msk", [128, 1408], F32)
        c_cb = sb("c_cb", [128, 384], BF16)
        c_id = sb("c_id", [128, 128], BF16)
        c_on128 = sb("c_on128", [128, 128], BF16)
        c_on1024 = sb("c_on1024", [128, 128], BF16)
        c_one = sb("c_one", [128, 128], BF16)
        mkT = sb("mkT", [128, 8, 256], BF16)
        mvb = sb("mvb", [128, 2, 1024], BF16)
        s_carry = sb("s_carry", [128, 8, 128], F32)
        ostg = [sb("ostg%d" % i, [128, 512], F32) for i in range(2)]
        Bostg = [Buf("ostg%d" % i) for i in range(2)]
        Bconst = Buf("const")
        Bmk = Buf("mkT")
        Bmv = Buf("mvb")
        Bcarry = [Buf("carry%d" % h) for h in range(8)]
        Bsetup = Buf("setup")

        E = P.engs
        pe, act, dve, pool, sp = nc.tensor, nc.scalar, nc.vector, nc.gpsimd, nc.sync

        def mm(out, lhsT, rhs, start, stop, r, w):
            P.op("pe", lambda: pe.matmul(out, lhsT=lhsT, rhs=rhs, start=start, stop=stop), r=r, w=w)

        def A(out, in_, func, r, w, **kw):
            P.op("act", lambda: act.activation(out=out, in_=in_, func=func, **kw), r=r, w=w)

        def TTn(out, in0, in1, op, r, w):
            P.op("dve", lambda: dve.tensor_tensor(out=out, in0=in0, in1=in1, op=op), r=r, w=w)

        def TS(out, in0, s1, s2, op0, op1, r, w):
            if op1 is None:
                P.op("dve", lambda: dve.tensor_scalar(out=out, in0=in0, scalar1=s1, scalar2=None, op0=op0), r=r, w=w)
            else:
                P.op("dve", lambda: dve.tensor_scalar(out=out, in0=in0, scalar1=s1, scalar2=s2, op0=op0, op1=op1),
                     r=r, w=w)

        def STT(out, in0, scalar, in1, op0, op1, r, w):
            P.op("dve", lambda: dve.scalar_tensor_tensor(out=out, in0=in0, scalar=scalar, in1=in1, op0=op0, op1=op1),
                 r=r, w=w)

        def dbg_dump(name, ap, shape, dt, rbufs):
            if not dbg:
                return
            d = nc.dram_tensor("dbg_" + name, list(shape), dt, kind="ExternalOutput").ap()
            dbg_d[name] = d
            P.op("sp", lambda: sp.dma_start(out=d, in_=ap), r=rbufs, w=[Buf("dbgw")], dma="out")

        units = []
        o = 0
        for n in once_sz:
            units.append((o, n))
            o += n
        pass_units = []
        for n in pass_sz:
            pass_units.append((o, n))
            o += n
        f_units = units[4:6]
        kv_units = units[0:4]
        units = []
        for t in range(NPRE):
            units.extend([pass_units[0], pass_units[1], f_units[0], f_units[1]])
        pass_units = pass_units + pass_units[-8:]
        pu_a, pu_bc, pu_rest = pass_units[2:10], pass_units[10:17], pass_units[17:]
        units.extend(pu_bc[0:3] + kv_units + pu_bc[3:] + pu_a + pu_rest)
        for p in range(1, NPASS):
            units.extend(pu_a + pu_bc + pu_rest)
        wstate = {"next": 0, "cur": 0}

        def w_ensure(upto):
            while wstate["next"] <= min(upto, len(units) - 1):
                u = wstate["next"]
                s = u % NSLOT
                off, n = units[u]
                dst = slots[s][:, 0:n].rearrange("p (a b) -> p a b", b=1024)
                src = wall_d[:, off:off + n].rearrange("p (a b) -> p a b", b=1024)
                P.op("pool", lambda dst=dst, src=src: pool.dma_start(out=dst, in_=src), w=[Bslot[s]], dma="w%d" % s)
                wstate["next"] += 1

        def w_get(kc=8, ahead=NSLOT - 1):
            u = wstate["cur"]
            wstate["cur"] += 1
            w_ensure(u + ahead)
            s = u % NSLOT
            n = units[u][1]
            ap = slots[s][:, 0:n].rearrange("p (k c) -> p k c", k=kc)
            return ap, Bslot[s]

        def cload(dst, src):
            P.op("sp", lambda: sp.dma_start(out=dst, in_=src), w=[Bconst], dma="const")

        cload(c_lb[:], lbT_d[:, :])
        cload(c_vec[:], vec_d[:, :])
        cload(c_hm[:], hm_d[:, :])
        cload(c_msk[:], msk_d[:, :])
        Bcid = Buf("c_id")
        P.op("pool", lambda: pool.dma_start(out=c_id[:], in_=id_d[:, :]), w=[Bcid], dma="cid")
        P.op("dve", lambda: dve.tensor_copy(out=c_cb[:], in_=c_msk[:, 1024:1408]), r=[Bconst], w=[Bsetup])
        P.op("dve", lambda: dve.memset(c_on128[:], 1.0 / 128.0), w=[Bsetup])
        P.op("dve", lambda: dve.memset(c_on1024[:], 1.0 / 1024.0), w=[Bsetup])
        P.op("dve", lambda: dve.memset(c_one[:], 1.0), w=[Bsetup])
        P.op("dve", lambda: dve.memset(s_carry[:], 0.0), w=Bcarry)
        TTn(c_lbv[:], c_lb[:, 0:8], c_lb[:, 8:16], ALU.subtract, r=[Bconst], w=[Bsetup])
        A(c_lbv[:], c_lbv[:], AF.Sigmoid, r=[Bsetup], w=[Bsetup])
        A(c_oml[:], c_lbv[:], AF.Identity, r=[Bsetup], w=[Bsetup], scale=-1.0, bias=1.0)
        A(c_lnoml[:], c_oml[:], AF.Ln, r=[Bsetup], w=[Bsetup])
        gain = c_vec[:, 0:8]
        msk_scan = c_msk[:, 0:512]
        msk_attn = c_msk[:, 512:1024]

        evs = {"i": 0}

        def evac(out, in_, r, w):
            evs["i"] += 1
            if evs["i"] % 2 == 0:
                A(out, in_, AF.Copy, r=r, w=w)
            else:
                P.op("dve", lambda: dve.tensor_copy(out=out, in_=in_), r=r, w=w)

        p_cur = {"p": 0}

        def hgrn_phase(sa, xT, BxT, xoff, full, oaT, Boa, gsum=None, Bgsum=None):
            assert full
            vtok = sb("vtok", [128, 8, 1024], BF16, sa)
            Bvall = Buf("vtok_all")
            Bv = [Bvall] * 8
            pp = p_cur["p"]
            P.op("sp", lambda: sp.dma_start(out=vtok[:].rearrange("p b c -> p (b c)"), in_=vt_d[pp]),
                 r=[Bvt[pp]], w=[Bvall], dma="vtld")
            streams = []
            for si in range(2):
                st = {}
                fn = ("sf", "geb", "bt1", "enbln", "k", "kinf", "gs0", "gs1", "lnt1")
                for n in fn:
                    st[n] = sb("a%d_%s" % (si, n), [128, 512], F32, sa)
                bn = ("qin0", "qin1", "kin", "kout", "koutT", "attn", "sq")
                for n in bn:
                    st[n] = sb("a%d_%s" % (si, n), [128, 512], BF16, sa)
                st["dec"] = sb("a%d_dec" % si, [128, 8], F32, sa)
                st["sall"] = sb("a%d_sall" % si, [128, 8, 128], F32, sa)
                st["sbf"] = sb("a%d_sbf" % si, [128, 8, 128], BF16, sa)
                st["B"] = {n: Buf("a%d_%s" % (si, n)) for n in fn + bn + ("dec", "sbf")}
                st["Bsall"] = [Buf("a%d_sall%d" % (si, c)) for c in range(8)]
                st["bk"] = [4 * si + i for i in range(4)]
                streams.append(st)

            def proj(st, W, BW, col, bank, tt):
                tok0 = xoff + tt * TT
                for kc in range(8):
                    mm(pb[bank][:, :], W[:, kc, col:col + 128], xT[:, kc, tok0:tok0 + TT],
                       kc == 0, kc == 7, r=[BW, BxT], w=[Bpb[bank]])

            def head(st, h, tt):
                B = st["B"]
                X0, X1, X2, X3 = st["bk"]
                t_sf, t_g, t_b, t_enb, t_k, t_kinf = st["sf"], st["geb"], st["bt1"], st["enbln"], st["k"], st["kinf"]
                t_eb = t_g
                A(t_sf[:], pb[X1][:, :], AF.Exp, r=[Bpb[X1]], w=[B["sf"]], scale=-1.0)
                A(t_g[:], t_sf[:], AF.Ln, r=[B["sf"], Bsetup], w=[B["geb"]], scale=c_lbv[:, h:h + 1], bias=1.0)
                A(t_k[:], t_sf[:], AF.Ln, r=[B["sf"]], w=[B["k"]], bias=1.0)
                TTn(t_g[:], t_g[:], t_k[:], ALU.subtract, r=[B["geb"], B["k"]], w=[B["geb"]])
                P.op("dve", lambda: dve.tensor_tensor_scan(out=t_b[:], data0=msk_scan, data1=t_g[:],
                                                           initial=0.0, op0=ALU.mult, op1=ALU.add),
                     r=[B["geb"], Bconst], w=[B["bt1"]])
                A(t_eb[:], t_b[:], AF.Exp, r=[B["bt1"]], w=[B["geb"]])
                A(st["dec"][:], t_b[:, 63:512:64], AF.Exp, r=[B["bt1"]], w=[B["dec"]])
                TTn(t_enb[:], pb[X1][:, :], t_k[:], ALU.add, r=[Bpb[X1], B["k"]], w=[B["enbln"]])
                TTn(t_enb[:], t_enb[:], t_b[:], ALU.add, r=[B["enbln"], B["bt1"]], w=[B["enbln"]])
                A(t_kinf[:], t_enb[:], AF.Exp, r=[B["enbln"], Bsetup], w=[B["kinf"]], scale=-1.0,
                  bias=c_lnoml[:, h:h + 1])
                qn = "qin%d" % tt
                TTn(st[qn][:], pb[X0][:, :], t_eb[:], ALU.mult, r=[Bpb[X0], B["geb"]], w=[B[qn]])
                A(st["kin"][:], t_kinf[:], AF.Copy, r=[B["kinf"]], w=[B["kin"]])
                TTn(st["kout"][:].rearrange("p (c t) -> p c t", t=64),
                    t_kinf[:].rearrange("p (c t) -> p c t", t=64),
                    st["dec"][:, :].unsqueeze(2).broadcast_to([128, 8, 64]), ALU.mult,
                    r=[B["kinf"], B["dec"]], w=[B["kout"]])

            def mid(st, h, tt, W, BW, nxt):
                B = st["B"]
                X0, X1, X2, X3 = st["bk"]
                Bsall = st["Bsall"]
                s_all = st["sall"]
                trv = pb[X1][:].bitcast(BF16)
                tb0 = tt * 4
                if tt == 0:
                    for t2 in range(2):
                        proj(st, W, BW, 256, X2, t2)
                        A(st["gs%d" % t2][:], pb[X2][:, :], AF.Silu, r=[Bpb[X2]], w=[B["gs%d" % t2]])
                for j in range(4):
                    P.op("pe", lambda j=j: pe.transpose(out=trv[:, j * 128:(j + 1) * 128],
                                                        in_=st["kout"][:, j * 128:(j + 1) * 128],
                                                        identity=c_id[:]),
                         r=[B["kout"], Bcid], w=[Bpb[X1]])
                A(st["koutT"][:], trv[:, 0:512], AF.Copy, r=[Bpb[X1]], w=[B["koutT"]])
                for j in range(4):
                    sl = slice(j * 128, (j + 1) * 128)
                    mm(pb[X0][:, sl], st["kin"][:, sl], st["qin%d" % tt][:, sl], True, True,
                       r=[B["kin"], B["qin%d" % tt]], w=[Bpb[X0]])
                TTn(st["attn"][:], pb[X0][:, :], msk_attn, ALU.mult, r=[Bpb[X0], Bconst], w=[B["attn"]])
                for c in range(8):
                    j, rr = c // 2, c % 2
                    bk = (X2, X3)[rr]
                    rows = slice(rr * 64, rr * 64 + 64)
                    mm(pb[bk][:, j * 128:(j + 1) * 128], st["koutT"][rows, j * 128:(j + 1) * 128],
                       vtok[rows, tb0 + j, h * 128:(h + 1) * 128], True, True,
                       r=[B["koutT"], Bv[tb0 + j]], w=[Bpb[bk]])
                P.op("pool", lambda: pool.tensor_copy(out=s_all[:, 0, :], in_=s_carry[:, h, :]),
                     r=[Bcarry[h]], w=[Bsall[0]])
                if nxt is not None:
                    W2, BW2, tt2 = nxt
                    proj(st, W2, BW2, 128, X1, tt2)
                    proj(st, W2, BW2, 0, X0, tt2)
                for c in range(8):
                    j, rr = c // 2, c % 2
                    bk = (X2, X3)[rr]
                    if c < 7:
                        out, wb = s_all[:, c + 1, :], Bsall[c + 1]
                    else:
                        out, wb = s_carry[:, h, :], Bcarry[h]
                    STT(out, s_all[:, c, :], st["dec"][:, c:c + 1], pb[bk][:, j * 128:(j + 1) * 128],
                        ALU.mult, ALU.add, r=[Bsall[c], B["dec"], Bpb[bk]], w=[wb])

            def tail(st, h, tt):
                B = st["B"]
                X0, X1, X2, X3 = st["bk"]
                Bsall = st["Bsall"]
                s_all, t_sbf = st["sall"], st["sbf"]
                t_ln = t_t1 = st["lnt1"]
                qn = "qin%d" % tt
                tb0 = tt * 4
                A(t_sbf[:].rearrange("p c d -> p (c d)"), s_all[:].rearrange("p c d -> p (c d)"), AF.Copy,
                  r=Bsall, w=[B["sbf"]])
                for j in range(4):
                    sl = slice(j * 128, (j + 1) * 128)
                    mm(pb[X2][:, sl], vtok[:, tb0 + j, h * 128:(h + 1) * 128], st["attn"][:, sl], True, False,
                       r=[Bv[tb0 + j], B["attn"]], w=[Bpb[X2]])
                    for rr in range(2):
                        c = 2 * j + rr
                        cs = slice(c * 64, c * 64 + 64)
                        mm(pb[X2][:, cs], t_sbf[:, c, :], st[qn][:, cs], False, rr == 1,
                           r=[B["sbf"], B[qn]], w=[Bpb[X2]])
                A(st["sq"][:], pb[X2][:, :], AF.Square, r=[Bpb[X2]], w=[B["sq"]])
                mm(pb[X3][:, :], c_on128[:], st["sq"][:], True, True, r=[B["sq"], Bsetup], w=[Bpb[X3]])
                A(t_ln[:], pb[X3][:, :], AF.Ln, r=[Bpb[X3]], w=[B["lnt1"]], bias=RMS_EPS)
                A(t_ln[:], t_ln[:], AF.Exp, r=[B["lnt1"]], w=[B["lnt1"]], scale=-0.5)
                TTn(t_t1[:], pb[X2][:, :], t_ln[:], ALU.mult, r=[Bpb[X2], B["lnt1"]], w=[B["lnt1"]])
                STT(oaT[:, h, tt * TT:(tt + 1) * TT], t_t1[:], gain[:, h:h + 1], st["gs%d" % tt][:], ALU.mult,
                    ALU.mult, r=[B["lnt1"], B["gs%d" % tt], Bconst], w=[Boa[tt]])

            its = [(hp, tt) for hp in range(4) for tt in range(2)]
            Wp = {}
            Wp[0] = (w_get(), w_get(ahead=1))
            for si in range(2):
                W, BW = Wp[0][si]
                proj(streams[si], W, BW, 128, streams[si]["bk"][1], 0)
                proj(streams[si], W, BW, 0, streams[si]["bk"][0], 0)
            P.interleave([P.capture(lambda si=si: head(streams[si], si, 0)) for si in range(2)])
            for idx, (hp, tt) in enumerate(its):
                hs = (2 * hp, 2 * hp + 1)
                nxt = [None, None]
                if idx + 1 < len(its):
                    hp2, tt2 = its[idx + 1]
                    if hp2 not in Wp:
                        Wp[hp2] = (w_get(ahead=1), w_get(ahead=1))
                    nxt = [(Wp[hp2][si][0], Wp[hp2][si][1], tt2) for si in range(2)]
                P.interleave([P.capture(lambda si=si: mid(streams[si], hs[si], tt, Wp[hp][si][0], Wp[hp][si][1], nxt[si]))
                              for si in range(2)])
                lists = [P.capture(lambda si=si: tail(streams[si], hs[si], tt)) for si in range(2)]
                if idx + 1 < len(its):
                    hp2, tt2 = its[idx + 1]
                    hs2 = (2 * hp2, 2 * hp2 + 1)
                    lists += [P.capture(lambda si=si: head(streams[si], hs2[si], tt2)) for si in range(2)]
                P.interleave(lists)

        Bvt = [Buf('vt%d' % t) for t in range(NPASS)]

        def hgrn_state_phase(sa, xT, BxT, gsum, Bgsum, tile, xoff=0):
            vtok = sb("vtok", [128, 8, 1024], BF16, sa)
            Bv = [Buf("vtok%d" % t) for t in range(8)]
            for ui in range(2):
                W, BW = w_get()
                for tb in range(8):
                    bk = (ui * 8 + tb) % 8
                    c0 = xoff + tb * 128
                    for kc in range(8):
                        mm(pb[bk][:, :], xT[:, kc, c0:c0 + 128], W[:, kc, :], kc == 0, kc == 7,
                           r=[BW, BxT], w=[Bpb[bk]])
                    evac(vtok[:, tb, ui * 512:(ui + 1) * 512], pb[bk][:, :], r=[Bpb[bk]], w=[Bv[tb]])
            P.op("sp", lambda: sp.dma_start(out=vt_d[tile], in_=vtok[:].rearrange("p b c -> p (b c)")),
                 r=Bv, w=[Bvt[tile]], dma="vtst")
            c_one_f = sb("c_one_f", [128, 512], F32, sa)
            Bonef = Buf("onef")
            P.op("dve", lambda: dve.memset(c_one_f[:], 1.0), w=[Bonef])
            streams = []
            for si in range(2):
                st = {}
                for n in ("sf", "g", "b", "e", "k"):
                    st[n] = sb("x%d_%s" % (si, n), [128, 512], F32, sa)
                for n in ("kout", "koutT"):
                    st[n] = sb("x%d_%s" % (si, n), [128, 512], BF16, sa)
                st["dec"] = sb("x%d_dec" % si, [128, 2], F32, sa)
                st["B"] = {n: Buf("x%d_%s" % (si, n)) for n in ("sf", "g", "b", "e", "k", "kout", "koutT", "dec")}
                st["bk"] = [4 * si + i for i in range(4)]
                streams.append(st)

            def xproj(st, idx, h, tt, W, BW, col):
                bH = st["bk"][0] if idx % 2 == 0 else st["bk"][3]
                tok0 = xoff + tt * TT
                for kc in range(8):
                    mm(pb[bH][:, :], W[:, kc, col:col + 128], xT[:, kc, tok0:tok0 + TT],
                       kc == 0, kc == 7, r=[BW, BxT], w=[Bpb[bH]])

            def iteration(st, idx, h, tt, nxt):
                B = st["B"]
                bH = st["bk"][0] if idx % 2 == 0 else st["bk"][3]
                bT, bU = st["bk"][1], st["bk"][2]
                trv = pb[bT][:].bitcast(BF16)
                tb0 = tt * 4
                A(st["sf"][:], pb[bH][:, :], AF.Exp, r=[Bpb[bH]], w=[B["sf"]], scale=-1.0)
                if nxt is not None:
                    xproj(st, *nxt)
                A(st["g"][:], st["sf"][:], AF.Ln, r=[B["sf"], Bsetup], w=[B["g"]], scale=c_lbv[:, h:h + 1], bias=1.0)
                A(st["k"][:], st["sf"][:], AF.Ln, r=[B["sf"]], w=[B["k"]], bias=1.0)
                TTn(st["g"][:], st["g"][:], st["k"][:], ALU.subtract, r=[B["g"], B["k"]], w=[B["g"]])
                P.op("dve", lambda: dve.tensor_tensor_scan(out=st["b"][:], data0=c_one_f[:], data1=st["g"][:],
                                                           initial=0.0, op0=ALU.mult, op1=ALU.add),
                     r=[B["g"], Bonef], w=[B["b"]])
                TTn(st["e"][:], pb[bH][:, :], st["k"][:], ALU.add, r=[Bpb[bH], B["k"]], w=[B["e"]])
                TTn(st["e"][:], st["e"][:], st["b"][:], ALU.add, r=[B["e"], B["b"]], w=[B["e"]])
                TTn(st["dec"][:, 1:2], st["b"][:, 511:512], c_lnoml[:, h:h + 1], ALU.add, r=[B["b"], Bsetup],
                    w=[B["dec"]])
                A(st["dec"][:, 0:1], st["b"][:, 511:512], AF.Exp, r=[B["b"], B["dec"]], w=[B["dec"]])
                TTn(gsum[:, h:h + 1], gsum[:, h:h + 1], st["b"][:, 511:512], ALU.add, r=[B["b"], Bgsum], w=[Bgsum])
                A(st["kout"][:], st["e"][:], AF.Exp, r=[B["e"], B["dec"]], w=[B["kout"]], scale=-1.0,
                  bias=st["dec"][:, 1:2])
                for j in range(4):
                    P.op("pe", lambda j=j: pe.transpose(out=trv[:, j * 128:(j + 1) * 128],
                                                        in_=st["kout"][:, j * 128:(j + 1) * 128],
                                                        identity=c_id[:]),
                         r=[B["kout"], Bcid], w=[Bpb[bT]])
                P.op("dve", lambda: dve.tensor_copy(out=st["koutT"][:], in_=trv[:, 0:512]), r=[Bpb[bT]], w=[B["koutT"]])
                for j in range(4):
                    mm(pb[bU][:, 0:128], st["koutT"][:, j * 128:(j + 1) * 128],
                       vtok[:, tb0 + j, h * 128:(h + 1) * 128], j == 0, j == 3,
                       r=[B["koutT"], Bv[tb0 + j]], w=[Bpb[bU]])
                STT(s_carry[:, h, :], s_carry[:, h, :], st["dec"][:, 0:1], pb[bU][:, 0:128], ALU.mult, ALU.add,
                    r=[Bcarry[h], B["dec"], Bpb[bU]], w=[Bcarry[h]])

            its = [(hp, tt) for hp in range(4) for tt in range(2)]
            Wf = {}
            Wf[0] = w_get()
            for si in range(2):
                xproj(streams[si], 0, si, 0, Wf[0][0], Wf[0][1], (si % 4) * 128)
            for idx, (hp, tt) in enumerate(its):
                hs = (2 * hp, 2 * hp + 1)
                nxt = [None, None]
                if idx + 1 < len(its):
                    hp2, tt2 = its[idx + 1]
                    if hp2 // 2 not in Wf:
                        Wf[hp2 // 2] = w_get(ahead=1)
                    W2, BW2 = Wf[hp2 // 2]
                    nxt = [(idx + 1, 2 * hp2 + si, tt2, W2, BW2, ((2 * hp2 + si) % 4) * 128) for si in range(2)]
                lists = [P.capture(lambda si=si: iteration(streams[si], idx, hs[si], tt, nxt[si])) for si in range(2)]
                P.interleave(lists)
            fence = sb("x_fence", [128, 1], F32, sa)
            P.op("dve", lambda: dve.memset(fence[:], 0.0), r=[Bvt[tile]], w=[Buf("x_fence")])

        gsum = sb("gsum", [128, 8], F32)
        Bgsum = Buf("gsum")
        P.op("dve", lambda: dve.memset(gsum[:], 0.0), w=[Bgsum])
        Bpay, Bgat, Bpfx, Bsel, Bccin, Bccout = [Buf(n) for n in ("pay", "gat", "pfx", "sel", "ccin", "ccout")]

        def phase_x(xT0, BxT0):
            with contextlib.ExitStack() as sxp:
                xTp1 = sb("xTp1", [128, 8, T], BF16, sxp)
                BxTp1 = Buf("xTp1")
                w_ensure(1)
                P.op("pool", lambda: pool.dma_start(
                    out=xTp1[:], in_=xT_d[1, :, HALO:HALO + T].rearrange("(k q) n -> q k n", q=128)),
                    w=[BxTp1], dma="xTp1")
                with contextlib.ExitStack() as spre:
                    hgrn_state_phase(spre, xT0, BxT0, gsum, Bgsum, 0, HALO)
                    P.barrier()
                with contextlib.ExitStack() as spre:
                    hgrn_state_phase(spre, xTp1, BxTp1, gsum, Bgsum, 1, 0)
                    P.barrier()
            with contextlib.ExitStack() as sx:
                pay = sb("pay", [128, 1032], F32, sx)
                P.op("dve", lambda: dve.tensor_copy(out=pay[:, 0:1024], in_=s_carry[:].rearrange("p h d -> p (h d)")),
                     r=Bcarry, w=[Bpay])
                A(pay[:, 1024:1032], gsum[:], AF.Exp, r=[Bgsum, Bpay], w=[Bpay])
                P.op("pool", lambda: pool.dma_start(out=cc_in[:, :], in_=pay[:]), r=[Bpay], w=[Bccin], dma="ccin")
                P.op("pool", lambda: pool.collective_compute("AllGather", ALU.bypass,
                                                             replica_groups=[[0, 1, 2, 3], [4, 5, 6, 7]],
                                                             ins=[cc_in.ap().opt()], outs=[cc_out.ap().opt()]),
                     r=[Bccin], w=[Bccout], dma="cc", amt=1)
                P.op("dve", lambda: dve.memset(pay[:, 0:1], 0.0), r=[Bccin], w=[Bpay])
                P.barrier()

        def phase_0():
            with contextlib.ExitStack() as s0:
                memT = sb("memT", [128, 8, 256], BF16, s0)
                Bmem = Buf("memT")
                P.op("pool", lambda: pool.dma_start(out=memT[:], in_=memT_d.rearrange("(k p) m -> p k m", p=128)),
                     w=[Bmem], dma="memT")
                for u in range(4):
                    W, BW = w_get()
                    if u < 2:
                        for ci in range(4):
                            ch = u * 4 + ci
                            bk = ch % 8
                            for kc in range(8):
                                mm(pb[bk][:, 0:256], W[:, kc, ci * 128:(ci + 1) * 128], memT[:, kc, :], kc == 0, kc == 7,
                                   r=[BW, Bmem], w=[Bpb[bk]])
                            evac(mkT[:, ch, :], pb[bk][:, 0:256], r=[Bpb[bk]], w=[Bmk])
                    else:
                        for mb in range(2):
                            bk = (u * 2 + mb) % 8
                            for kc in range(8):
                                mm(pb[bk][:, :], memT[:, kc, mb * 128:(mb + 1) * 128], W[:, kc, :], kc == 0, kc == 7,
                                   r=[BW, Bmem], w=[Bpb[bk]])
                            evac(mvb[:, mb, (u - 2) * 512:(u - 1) * 512], pb[bk][:, :], r=[Bpb[bk]], w=[Bmv])
            P.barrier()

        def combine_states():
            with contextlib.ExitStack() as sx2:
                gat = sb("gat", [128, 4, 1032], F32, sx2)
                pfx = sb("pfx", [128, 2, 1024], F32, sx2)
                c_sel = sb("c_sel", [128, 4], F32, sx2)
                P.op("sp", lambda: sp.dma_start(out=c_sel[:], in_=sel_d[:, :]), w=[Bsel], dma="const2")
                P.op("pool", lambda: pool.dma_start(out=gat[:], in_=cc_out.ap().rearrange("(r p) c -> p r c", p=128)),
                     r=[Bccout], w=[Bgat], dma="ccback")
                for h in range(8):
                    hs = slice(h * 128, (h + 1) * 128)
                    STT(pfx[:, 0, hs], gat[:, 0, hs], gat[:, 1, 1024 + h:1025 + h], gat[:, 1, hs], ALU.mult, ALU.add,
                        r=[Bgat], w=[Bpfx])
                for h in range(8):
                    hs = slice(h * 128, (h + 1) * 128)
                    STT(pfx[:, 1, hs], pfx[:, 0, hs], gat[:, 2, 1024 + h:1025 + h], gat[:, 2, hs], ALU.mult, ALU.add,
                        r=[Bgat, Bpfx], w=[Bpfx])
                sc = s_carry[:].rearrange("p h d -> p (h d)")
                TS(sc, gat[:, 0, 0:1024], c_sel[:, 1:2], None, ALU.mult, None, r=[Bgat, Bsel], w=Bcarry)
                STT(sc, pfx[:, 0, :], c_sel[:, 2:3], sc, ALU.mult, ALU.add, r=[Bpfx, Bsel] + Bcarry, w=Bcarry)
                STT(sc, pfx[:, 1, :], c_sel[:, 3:4], sc, ALU.mult, ALU.add, r=[Bpfx, Bsel] + Bcarry, w=Bcarry)
                P.barrier()

        def run_pass(p):
            p_cur["p"] = p
            with contextlib.ExitStack() as sp_pass:
                merged = sb("merged", [128, 8, T], BF16, sp_pass)
                Bmerged = [Buf("merged%d" % t) for t in range(2)]
                with contextlib.ExitStack() as s1:
                    xT = sb("xT", [128, 8, HALO + T], BF16, s1)
                    oaT = sb("oaT", [128, 8, T], BF16, s1)
                    obT = sb("obT", [128, 8, T], BF16, s1)
                    ocT = sb("ocT", [128, 8, T], BF16, s1)
                    BxT = Buf("xT")
                    Boa = [Buf("oa%d" % t) for t in range(2)]
                    Bob = [Buf("ob%d" % t) for t in range(2)]
                    Boc = [Buf("oc%d" % t) for t in range(2)]
                    P.op("pool", lambda xT=xT, p=p: pool.dma_start(
                        out=xT[:], in_=xT_d[p].rearrange("(k q) t -> q k t", q=128)), w=[BxT], dma="xT")

                    def phase_a():
                        with contextlib.ExitStack() as sa:
                            hgrn_phase(sa, xT, BxT, HALO, True, oaT, Boa)
                            P.barrier()
                        if p == 0:
                            dbg_dump("oaT", oaT[:], [128, 8, T], BF16, Boa)

                    def phase_b():
                        with contextlib.ExitStack() as sbk:
                            biasT = sb("biasT", [128, 4096], F32, sbk)
                            esrow = [sb("esrow%d" % g, [128, 512], BF16, sbk) for g in range(2)]
                            Bbias = Buf("biasT")
                            Bes = Buf("esrow")
                            P.op("sp", lambda biasT=biasT: sp.dma_start(out=biasT[:], in_=bias_d[:, :]), w=[Bbias],
                                 dma="bias")
                            with contextlib.ExitStack() as ssk:
                                sinkt = sb("sinkt", [128, 2048], F32, ssk)
                                s_hi = sb("s_hi", [128, 2048], BF16, ssk)
                                s_lo = sb("s_lo", [128, 2048], BF16, ssk)
                                Bsink = Buf("sinkt")
                                P.op("sp", lambda sinkt=sinkt: sp.dma_start(out=sinkt[:], in_=sink_d[:, :]), w=[Bsink],
                                     dma="sink")
                                A(sinkt[:], sinkt[:], AF.Exp, r=[Bsink], w=[Bsink])
                                P.op("dve", lambda s_hi=s_hi, sinkt=sinkt: dve.tensor_copy(out=s_hi[:], in_=sinkt[:]),
                                     r=[Bsink], w=[Bes])
                                TTn(sinkt[:], sinkt[:], s_hi[:], ALU.subtract, r=[Bsink, Bes], w=[Bsink])
                                P.op("dve", lambda s_lo=s_lo, sinkt=sinkt: dve.tensor_copy(out=s_lo[:], in_=sinkt[:]),
                                     r=[Bsink], w=[Bes])
                                for g in range(2):
                                    P.op("dve", lambda g=g: dve.memset(esrow[g][:], 0.0), w=[Bes])
                                    for rr in range(2):
                                        c0 = (g * 2 + rr) * 512
                                        for src, prow in ((s_hi, 64 * rr), (s_lo, 64 * rr + 32)):
                                            P.op("dve", lambda g=g, src=src, prow=prow, c0=c0: dve.tensor_copy(
                                                out=esrow[g][prow:prow + 1, :], in_=src[prow:prow + 1, c0:c0 + 512]),
                                                r=[Bes], w=[Bes])
                                P.barrier()
                            KT = [sb("KT%d" % g, [128, HALO + T], BF16, sbk) for g in range(2)]
                            Vz = [[sb("Vz%d%d" % (g, rr), [128, 9, 128], BF16, sbk) for rr in range(2)] for g in range(2)]
                            BKT = [Buf("KT%d" % g) for g in range(2)]
                            BV2 = [Buf("V2%d" % g) for g in range(2)]
                            QT = sb("QT", [128, 4, T], BF16, sbk)
                            BQT = Buf("QT")
                            t_sb = [sb("b_sb%d" % i, [128, 2048], F32, sbk) for i in range(2)]
                            t_pt = [sb("b_pt%d" % i, [128, 2048], BF16, sbk) for i in range(2)]
                            t_l = [sb("b_l%d" % i, [128, 512], F32, sbk) for i in range(2)]
                            Bsb = [Buf("b_sb%d" % i) for i in range(2)]
                            Bpt = [Buf("b_pt%d" % i) for i in range(2)]
                            Bl = [Buf("b_l%d" % i) for i in range(2)]
                            for g in range(2):
                                for rr in range(2):
                                    P.op("dve", lambda g=g, rr=rr: dve.memset(Vz[g][rr][:], 0.0), w=[BV2[g]])
                            W, BW = w_get()
                            nb = 0
                            for g in range(2):
                                for (c0, c1) in ((0, 512), (512, 1024), (1024, 1152)):
                                    bk = nb % 8
                                    nb += 1
                                    for kc in range(8):
                                        mm(pb[bk][:, 0:c1 - c0], W[:, kc, g * 128:(g + 1) * 128], xT[:, kc, c0:c1],
                                           kc == 0, kc == 7, r=[BW, BxT], w=[Bpb[bk]])
                                    evac(KT[g][:, c0:c1], pb[bk][:, 0:c1 - c0], r=[Bpb[bk]], w=[BKT[g]])
                                for b0 in (0, 4, 8):
                                    nblk = min(4, 9 - b0)
                                    bk = nb % 8
                                    nb += 1
                                    for bi in range(nblk):
                                        blk = b0 + bi
                                        for kc in range(8):
                                            mm(pb[bk][:, bi * 128:(bi + 1) * 128], xT[:, kc, blk * 128:(blk + 1) * 128],
                                               W[:, kc, 256 + g * 128:256 + (g + 1) * 128], kc == 0, kc == 7,
                                               r=[BW, BxT], w=[Bpb[bk]])
                                    pv = pb[bk][:, 0:nblk * 128].rearrange("p (b d) -> p b d", d=128)
                                    for rr in range(2):
                                        evac(Vz[g][rr][:, b0:b0 + nblk, rr * 64:rr * 64 + 64], pv[:, :, rr * 64:rr * 64 + 64],
                                             r=[Bpb[bk]], w=[BV2[g]])
                            it = 0
                            for g in range(2):
                                W, BW = w_get()
                                for i in range(4):
                                    for tt in range(2):
                                        bk = nb % 8
                                        nb += 1
                                        for kc in range(8):
                                            mm(pb[bk][:, :], W[:, kc, i * 128:(i + 1) * 128],
                                               xT[:, kc, HALO + tt * TT:HALO + (tt + 1) * TT], kc == 0, kc == 7,
                                               r=[BW, BxT], w=[Bpb[bk]])
                                        evac(QT[:, i, tt * TT:(tt + 1) * TT], pb[bk][:, :], r=[Bpb[bk]], w=[BQT])

                                def stage1a(n, par, g=g):
                                    for pc in range(2):
                                        for rr in range(2):
                                            bk = pc * 2 + rr
                                            rows = slice(rr * 64, rr * 64 + 64)
                                            for i in range(4):
                                                mm(pb[bk][:, i * 128:(i + 1) * 128],
                                                   KT[g][rows, (n + pc) * 128:(n + pc + 1) * 128],
                                                   QT[rows, i, n * 128:(n + 1) * 128], True, True,
                                                   r=[BKT[g], BQT], w=[Bpb[bk]])

                                def stage1b(n, par, g=g):
                                    for pc in range(2):
                                        for rr in range(2):
                                            bk = pc * 2 + rr
                                            o0 = ((g * 2 + pc) * 2 + rr) * 512
                                            STT(t_sb[par][:, bk * 512:(bk + 1) * 512], pb[bk][:, :], 0.125,
                                                biasT[:, o0:o0 + 512], ALU.mult, ALU.add,
                                                r=[Bpb[bk], Bbias], w=[Bsb[par]])
                                    if p == 0 and n == 0:
                                        TS(t_sb[par][:, 0:1024], t_sb[par][:, 0:1024], c_hm[:, 0:1], None, ALU.add, None,
                                           r=[Bsb[par], Bconst], w=[Bsb[par]])
                                    A(t_pt[par][:], t_sb[par][:], AF.Exp, r=[Bsb[par]], w=[Bpt[par]])

                                def stage2(n, par, g=g):
                                    bo, bd = 4 + par * 2, 5 + par * 2
                                    k = 0
                                    for pc in range(2):
                                        for rr in range(2):
                                            bk = pc * 2 + rr
                                            mm(pb[bo][:, :], Vz[g][rr][:, n + pc, :], t_pt[par][:, bk * 512:(bk + 1) * 512],
                                               k == 0, k == 3, r=[BV2[g], Bpt[par]], w=[Bpb[bo]])
                                            k += 1
                                    k = 0
                                    for pc in range(2):
                                        for rr in range(2):
                                            bk = pc * 2 + rr
                                            mm(pb[bd][:, :], c_cb[:, 128 + rr * 128:256 + rr * 128],
                                               t_pt[par][:, bk * 512:(bk + 1) * 512], k == 0, False,
                                               r=[Bsetup, Bpt[par]], w=[Bpb[bd]])
                                            k += 1
                                    mm(pb[bd][:, :], c_cb[:, 0:128], esrow[g][:], False, True, r=[Bsetup, Bes], w=[Bpb[bd]])
                                    A(t_l[par][:], pb[bd][:, :], AF.Ln, r=[Bpb[bd]], w=[Bl[par]])
                                    A(t_l[par][:], t_l[par][:], AF.Exp, r=[Bl[par]], w=[Bl[par]], scale=-1.0)
                                    TTn(obT[:, 4 * g:4 * g + 4, n * 128:(n + 1) * 128],
                                        pb[bo][:, :].rearrange("p (i q) -> p i q", q=128),
                                        t_l[par][:].rearrange("p (i q) -> p i q", q=128), ALU.mult,
                                        r=[Bpb[bo], Bl[par]], w=[Bob[n // 4]])

                                prev = None
                                for n in range(8):
                                    par = it % 2
                                    it += 1
                                    stage1a(n, par)
                                    l1 = P.capture(lambda: stage1b(n, par))
                                    if prev is None:
                                        P.interleave([l1])
                                    else:
                                        l2 = P.capture(lambda: stage2(*prev))
                                        P.interleave([l1, l2])
                                    prev = (n, par)
                                P.interleave([P.capture(lambda: stage2(*prev))])
                            P.barrier()
                        if p == 0:
                            dbg_dump("obT", obT[:], [128, 8, T], BF16, Bob)

                    def phase_c():
                        with contextlib.ExitStack() as sc:
                            t_mq = [sb("c_mq%d" % i, [128, 2, 512], BF16, sc) for i in range(2)]
                            t_pm = [sb("c_pm%d" % i, [128, 2, 512], BF16, sc) for i in range(2)]
                            t_rc = [sb("c_rc%d" % i, [128, 512], F32, sc) for i in range(2)]
                            Bmq = [Buf("c_mq%d" % i) for i in range(2)]
                            Bpm = [Buf("c_pm%d" % i) for i in range(2)]
                            Brc = [Buf("c_rc%d" % i) for i in range(2)]

                            def cstage1(mh, tt, par, W, BW):
                                tok0 = HALO + tt * TT
                                for dc in range(2):
                                    bk = dc
                                    for kc in range(8):
                                        mm(pb[bk][:, :], W[:, kc, dc * 128:(dc + 1) * 128], xT[:, kc, tok0:tok0 + TT],
                                           kc == 0, kc == 7, r=[BW, BxT], w=[Bpb[bk]])
                                    evac(t_mq[par][:, dc, :], pb[bk][:, :], r=[Bpb[bk]], w=[Bmq[par]])
                                for mb in range(2):
                                    bk = 2 + mb
                                    for dc in range(2):
                                        mm(pb[bk][:, :], mkT[:, mh * 2 + dc, mb * 128:(mb + 1) * 128],
                                           t_mq[par][:, dc, :], dc == 0, dc == 1, r=[Bmk, Bmq[par]], w=[Bpb[bk]])
                                    A(t_pm[par][:, mb, :], pb[bk][:, :], AF.Exp, r=[Bpb[bk]], w=[Bpm[par]],
                                      scale=1.0 / 16.0)

                            def cstage2(mh, tt, par, W, BW):
                                for mb in range(2):
                                    mm(pb[4][:, :], c_one[:], t_pm[par][:, mb, :], mb == 0, mb == 1,
                                       r=[Bsetup, Bpm[par]], w=[Bpb[4]])
                                for vc in range(2):
                                    bk = 5 + vc
                                    for mb in range(2):
                                        mm(pb[bk][:, :], mvb[:, mb, mh * 256 + vc * 128:mh * 256 + (vc + 1) * 128],
                                           t_pm[par][:, mb, :], mb == 0, mb == 1, r=[Bmv, Bpm[par]], w=[Bpb[bk]])
                                A(t_rc[par][:], pb[4][:, :], AF.Ln, r=[Bpb[4]], w=[Brc[par]])
                                A(t_rc[par][:], t_rc[par][:], AF.Exp, r=[Brc[par]], w=[Brc[par]], scale=-1.0)
                                for vc in range(2):
                                    TTn(ocT[:, mh * 2 + vc, tt * TT:(tt + 1) * TT], pb[5 + vc][:, :], t_rc[par][:],
                                        ALU.mult, r=[Bpb[5 + vc], Brc[par]], w=[Boc[tt]])

                            it = 0
                            prev = None
                            for mh in range(4):
                                W, BW = w_get()
                                for tt in range(2):
                                    par = it % 2
                                    it += 1
                                    cur = (mh, tt, par, W, BW)
                                    l1 = P.capture(lambda: cstage1(*cur))
                                    if prev is None:
                                        P.interleave([l1])
                                    else:
                                        P.interleave([l1, P.capture(lambda: cstage2(*prev))])
                                    prev = cur
                            P.interleave([P.capture(lambda: cstage2(*prev))])
                            P.barrier()
                        if p == 0:
                            dbg_dump("ocT", ocT[:], [128, 8, T], BF16, Boc)

                    if p == 0:
                        phase_x(xT, BxT)
                        phase_b()
                        phase_0()
                        phase_c()
                        combine_states()
                        phase_a()
                    else:
                        phase_a()
                        phase_b()
                        phase_c()

                    with contextlib.ExitStack() as sd:
                        t_sg = [sb("d_sg%d" % i, [128, 512], F32, sd) for i in range(4)]
                        t_m = [sb("d_m%d" % i, [128, 512], F32, sd) for i in range(2)]
                        t_tmp = [sb("d_tmp%d" % i, [128, 512], F32, sd) for i in range(4)]
                        Bsg = [Buf("d_sg%d" % i) for i in range(4)]
                        Bm = [Buf("d_m%d" % i) for i in range(2)]
                        Btmp = [Buf("d_tmp%d" % i) for i in range(4)]
                        srcs = ((oaT, Boa), (obT, Bob), (ocT, Boc))
                        it = 0
                        jt = 0
                        for cc in range(8):
                            WG, BWG = w_get()
                            WB, BWB = w_get(ahead=1)
                            for tt in range(2):
                                mp = jt % 2
                                jt += 1
                                tok0 = HALO + tt * TT
                                for b in range(3):
                                    par = it % 4
                                    it += 1
                                    bg, by = par * 2, par * 2 + 1
                                    for kc in range(8):
                                        mm(pb[bg][:, :], WG[:, kc, b * 128:(b + 1) * 128], xT[:, kc, tok0:tok0 + TT],
                                           kc == 0, kc == 7, r=[BWG, BxT], w=[Bpb[bg]])
                                    src, Bsrc = srcs[b]
                                    for kc in range(8):
                                        mm(pb[by][:, :], WB[:, kc, b * 128:(b + 1) * 128],
                                           src[:, kc, tt * TT:(tt + 1) * TT], kc == 0, kc == 7,
                                           r=[BWB, Bsrc[tt]], w=[Bpb[by]])
                                    A(t_sg[par][:], pb[bg][:, :], AF.Sigmoid, r=[Bpb[bg]], w=[Bsg[par]])
                                    if b == 0:
                                        TTn(t_m[mp][:], pb[by][:, :], t_sg[par][:], ALU.mult,
                                            r=[Bpb[by], Bsg[par]], w=[Bm[mp]])
                                    else:
                                        TTn(t_tmp[par][:], pb[by][:, :], t_sg[par][:], ALU.mult,
                                            r=[Bpb[by], Bsg[par]], w=[Btmp[par]])
                                        if b == 1:
                                            TTn(t_m[mp][:], t_m[mp][:], t_tmp[par][:], ALU.add,
                                                r=[Bm[mp], Btmp[par]], w=[Bm[mp]])
                                        else:
                                            TTn(merged[:, cc, tt * TT:(tt + 1) * TT], t_m[mp][:], t_tmp[par][:], ALU.add,
                                                r=[Bm[mp], Btmp[par]], w=[Bmerged[tt]])
                        P.barrier()
                if p == 0:
                    dbg_dump("merged", merged[:], [128, 8, T], BF16, Bmerged)

                with contextlib.ExitStack() as s2:
                    h1T = sb("h1T", [128, 8, T], F32, s2)
                    h1b = sb("h1b", [128, 8, T], BF16, s2)
                    Bh1 = [Buf("h1T%d" % t) for t in range(2)]
                    Bh1b = [Buf("h1b%d" % t) for t in range(2)]
                    xres = [sb("xres%d" % i, [128, 512], F32, s2) for i in range(2)]
                    Bxres = [Buf("xres%d" % i) for i in range(2)]
                    l_sq = [sb("l_sq%d" % i, [128, 512], BF16, s2) for i in range(2)]
                    l_hb = [sb("l_hb%d" % i, [128, 512], BF16, s2) for i in range(2)]
                    Blsq = [Buf("l_sq%d" % i) for i in range(2)]
                    Blhb = [Buf("l_hb%d" % i) for i in range(2)]
                    l_mean = sb("l_mean", [128, 512], F32, s2)
                    l_var = sb("l_var", [128, 512], F32, s2)
                    l_A = sb("l_A", [128, 512], F32, s2)
                    l_Bm = sb("l_Bm", [128, 512], F32, s2)
                    l_t = [sb("l_t%d" % i, [128, 512], F32, s2) for i in range(2)]
                    Bl = {n: Buf("l_" + n) for n in ("mean", "var", "A", "Bm")}
                    Blt = [Buf("l_t%d" % i) for i in range(2)]
                    lnc = {"i": 0, "o": 0}

                    def layernorm(src, Bsrc, tt, goff, boff, final):
                        ts = slice(tt * TT, (tt + 1) * TT)
                        bm, bq = 6, 7
                        for cc in range(8):
                            par = lnc["i"] % 2
                            lnc["i"] += 1
                            A(l_sq[par][:], src[:, cc, ts], AF.Square, r=[Bsrc], w=[Blsq[par]])
                            P.op("dve", lambda par=par, cc=cc: dve.tensor_copy(out=l_hb[par][:], in_=src[:, cc, ts]),
                                 r=[Bsrc], w=[Blhb[par]])
                            mm(pb[bm][:, :], c_on1024[:], l_hb[par][:], cc == 0, cc == 7, r=[Bsetup, Blhb[par]],
                               w=[Bpb[bm]])
                            mm(pb[bq][:, :], c_on1024[:], l_sq[par][:], cc == 0, cc == 7, r=[Bsetup, Blsq[par]],
                               w=[Bpb[bq]])
                        P.op("dve", lambda: dve.tensor_copy(out=l_mean[:], in_=pb[bm][:, :]), r=[Bpb[bm]],
                             w=[Bl["mean"]])
                        TTn(l_var[:], l_mean[:], l_mean[:], ALU.mult, r=[Bl["mean"]], w=[Bl["var"]])
                        TTn(l_var[:], pb[bq][:, :], l_var[:], ALU.subtract, r=[Bpb[bq], Bl["var"]], w=[Bl["var"]])
                        A(l_var[:], l_var[:], AF.Ln, r=[Bl["var"]], w=[Bl["var"]], bias=LN_EPS)
                        A(l_A[:], l_var[:], AF.Exp, r=[Bl["var"]], w=[Bl["A"]], scale=-0.5)
                        STT(l_Bm[:], l_mean[:], -1.0, l_A[:], ALU.mult, ALU.mult, r=[Bl["mean"], Bl["A"]], w=[Bl["Bm"]])
                        for cc in range(8):
                            par = lnc["i"] % 2
                            lnc["i"] += 1
                            TTn(l_t[par][:], src[:, cc, ts], l_A[:], ALU.mult, r=[Bsrc, Bl["A"]], w=[Blt[par]])
                            TTn(l_t[par][:], l_t[par][:], l_Bm[:], ALU.add, r=[Blt[par], Bl["Bm"]], w=[Blt[par]])
                            gsc = c_vec[:, goff + cc:goff + cc + 1]
                            bsc = c_vec[:, boff + cc:boff + cc + 1]
                            if not final:
                                A(src[:, cc, ts], l_t[par][:], AF.Identity, r=[Blt[par], Bconst], w=[Bsrc],
                                  scale=gsc, bias=bsc)
                                A(h1b[:, cc, ts], l_t[par][:], AF.Identity, r=[Blt[par], Bconst], w=[Bh1b[tt]],
                                  scale=gsc, bias=bsc)
                            else:
                                so = lnc["o"] % 2
                                lnc["o"] += 1
                                A(ostg[so][:], l_t[par][:], AF.Identity, r=[Blt[par], Bconst], w=[Bostg[so]],
                                  scale=gsc, bias=bsc)
                                dst = outT_d[cc * 128:(cc + 1) * 128, p * T + tt * TT:p * T + (tt + 1) * TT]
                                P.op("sp", lambda so=so, dst=dst: sp.dma_start(out=dst, in_=ostg[so][:]),
                                     r=[Bostg[so]], w=[Buf("o")], dma="out%d" % so)

                    WO = [w_get(), w_get(ahead=1)]
                    xi = {"i": 0}

                    def d2_tile(tt):
                        for cc in range(8):
                            W, BW = WO[cc // 4]
                            ci = cc % 4
                            bk = cc % 6
                            xp = xi["i"] % 2
                            xi["i"] += 1
                            srcx = xT_d[p, cc * 128:(cc + 1) * 128, HALO + tt * TT:HALO + (tt + 1) * TT]
                            P.op("sp", lambda xp=xp, srcx=srcx: sp.dma_start(out=xres[xp][:], in_=srcx),
                                 w=[Bxres[xp]], dma="xres%d" % xp)
                            for kc in range(8):
                                mm(pb[bk][:, :], W[:, kc, ci * 128:(ci + 1) * 128],
                                   merged[:, kc, tt * TT:(tt + 1) * TT], kc == 0, kc == 7,
                                   r=[BW, Bmerged[tt]], w=[Bpb[bk]])
                            STT(h1T[:, cc, tt * TT:(tt + 1) * TT], xres[xp][:], ALPHA, pb[bk][:, :], ALU.mult,
                                ALU.add, r=[Bxres[xp], Bpb[bk]], w=[Bh1[tt]])

                    d2_tile(0)
                    P.interleave([P.capture(lambda: d2_tile(1)),
                                  P.capture(lambda: layernorm(h1T, Bh1[0], 0, 8, 16, False))])
                    layernorm(h1T, Bh1[1], 1, 8, 16, False)
                    if p == 0:
                        dbg_dump("h1T", h1T[:], [128, 8, T], F32, Bh1)

                    with contextlib.ExitStack() as se:
                        aT = sb("aT", [128, 32, T], BF16, se)
                        BaT = [Buf("aT%d" % t) for t in range(2)]
                        t_r = [sb("e_r%d" % i, [128, 512], F32, se) for i in range(2)]
                        Br = [Buf("e_r%d" % i) for i in range(2)]
                        it = 0
                        for u in range(8):
                            W, BW = w_get()
                            for fi in range(4):
                                fc = 4 * u + fi
                                for tt in range(2):
                                    par = it % 2
                                    bk = it % 6
                                    it += 1
                                    for kc in range(8):
                                        mm(pb[bk][:, :], W[:, kc, fi * 128:(fi + 1) * 128],
                                           h1b[:, kc, tt * TT:(tt + 1) * TT], kc == 0, kc == 7,
                                           r=[BW, Bh1b[tt]], w=[Bpb[bk]])
                                    A(t_r[par][:], pb[bk][:, :], AF.Relu, r=[Bpb[bk]], w=[Br[par]])
                                    TTn(aT[:, fc, tt * TT:(tt + 1) * TT], t_r[par][:], t_r[par][:], ALU.mult,
                                        r=[Br[par]], w=[BaT[tt]])
                        def down_tile(tt):
                            for cc in range(8):
                                W, BW = w_get(kc=32)
                                bk = cc % 6
                                for fc in range(32):
                                    mm(pb[bk][:, :], W[:, fc, :], aT[:, fc, tt * TT:(tt + 1) * TT], fc == 0, fc == 31,
                                       r=[BW, BaT[tt]], w=[Bpb[bk]])
                                STT(h1T[:, cc, tt * TT:(tt + 1) * TT], h1T[:, cc, tt * TT:(tt + 1) * TT], ALPHA,
                                    pb[bk][:, :], ALU.mult, ALU.add, r=[Bh1[tt], Bpb[bk]], w=[Bh1[tt]])

                        down_tile(0)
                        l_ln = P.capture(lambda: layernorm(h1T, Bh1[0], 0, 24, 32, True))
                        P.spread = l_ln
                        down_tile(1)
                        P.flush_spread()
                        layernorm(h1T, Bh1[1], 1, 24, 32, True)
                        P.barrier()
        for p in range(NPASS):
            run_pass(p)
        P.emit()
        for k in list(P.sems):
            pass
        cnt = {}
        for o2 in P.ops:
            if o2.grp and (o2.grp.startswith("out")):
                cnt[o2.grp] = o2.val
        for g, v in cnt.items():
            sp.wait_ge(P.sem("d_" + g), v)
    return nc, dbg_d


def _prep_inputs(x, mem, w_in, lb_logits, hg_norm_gain, swa_sinks, rel_bias, w_mem_kv, w_branch_hg, w_branch_swa,
                 w_branch_mem, w_out, ln1_g, ln1_b, w_up, w_down, ln2_g, ln2_b):
    f = lambda a: np.asarray(a, dtype=np.float32)
    x, mem = f(x), f(mem)
    wall, _, _ = _build_units(f(w_in)[0], f(w_mem_kv)[0], f(w_branch_hg)[0], f(w_branch_swa)[0], f(w_branch_mem)[0],
                              f(w_out)[0], f(w_up)[0], f(w_down)[0])
    lbT = np.ascontiguousarray(f(lb_logits).reshape(2, 8, 128).transpose(2, 0, 1).reshape(128, 16))
    v = lambda a: f(a).reshape(8, 128).T
    vecs = np.ascontiguousarray(np.concatenate([v(hg_norm_gain), v(ln1_g), v(ln1_b), v(ln2_g), v(ln2_b)], axis=1))
    biasT = _bias_table(f(rel_bias))
    sinkT = _sink_table(f(swa_sinks)[0])
    masks = _masks()
    ident = np.eye(128, dtype=np.float32)
    in_maps = []
    for c in range(NCORES):
        b, j = c // 4, c % 4
        t0 = j * TOK_CORE
        xt = np.zeros((NPASS, D, HALO + T), np.float32)
        for p in range(NPASS):
            s = t0 + p * T - HALO
            if s < 0:
                xt[p, :, HALO:] = x[b, 0:T, :].T
            else:
                xt[p] = x[b, s:s + HALO + T, :].T
        sel = np.zeros((128, 4), np.float32)
        sel[:, j] = 1.0
        hm = np.full((128, 1), NEG if j == 0 else 0.0, np.float32)
        in_maps.append({"xT": xt, "sel": sel, "memT": np.ascontiguousarray(mem[b].T), "wall": wall, "lbT": lbT, "vecs": vecs,
                        "hmask": hm, "biasT": biasT, "sinkT": sinkT, "masks": masks, "ident": ident})
    return in_maps


def kernel(**inputs):
    in_maps = _prep_inputs(**inputs)
    nc, _ = build_program(False)
    res = run_bass_kernel_spmd(nc, in_maps, core_ids=list(range(NCORES)))
    out = np.empty((2, SEQ, D), np.float32)
    for c in range(NCORES):
        b, j = c // 4, c % 4
        out[b, j * TOK_CORE:(j + 1) * TOK_CORE, :] = res.results[c]["outT"].T
    return out
```
